# Optimizing a Trainium2 kernel written in Bass

```python
import math
import jax, jax.numpy as jnp
from jax import lax
import numpy as np

D_MODEL = 1024
BATCH = 8
SEQ = 4096
DEPTH = 1

GRID_W = 64
CTX_LEN = 256
N_MLA_HEADS = 8
QK_NOPE_DIM = 64
QK_ROPE_DIM = 32
V_HEAD_DIM = 64
Q_LORA_RANK = 384
KV_LORA_RANK = 256
MLA_WIDTH = N_MLA_HEADS * V_HEAD_DIM
S5_WIDTH = D_MODEL - MLA_WIDTH
S5_GROUP = 16
S5_GROUPS = S5_WIDTH // S5_GROUP
S5_STATE = 64
DT_MIN = 1e-3
DT_MAX = 1e-1
KV_END = Q_LORA_RANK + KV_LORA_RANK
ROPE_END = KV_END + QK_ROPE_DIM
D_IN = ROPE_END + S5_WIDTH
D_FF = ((8 * D_MODEL + 3 * 256 - 1) // (3 * 256)) * 256
ROPE_THETA = 10000.0
Q_BLOCK = 128
NORM_EPS = 1e-6
DN_ALPHA = (2.0 * DEPTH) ** 0.25
DN_BETA = (8.0 * DEPTH) ** -0.25

kernel_name = 'hymba_mla_s5_deepnorm_adaln_prefix_ctx'


def layer_norm(x, g=None, b=None):
    xf = x.astype(jnp.float32)
    mu = xf.mean(-1, keepdims=True)
    var = jnp.square(xf - mu).mean(-1, keepdims=True)
    y = (xf - mu) * lax.rsqrt(var + NORM_EPS)
    if g is not None:
        y = y * g.astype(jnp.float32) + b.astype(jnp.float32)
    return y.astype(x.dtype)


def rms_norm(x, g):
    xf = x.astype(jnp.float32)
    y = xf * lax.rsqrt(jnp.mean(xf * xf, -1, keepdims=True) + NORM_EPS)
    return (y * g.astype(jnp.float32)).astype(x.dtype)


def adaln(cond, w_ada, b_ada):
    return jnp.split(jax.nn.silu(cond) @ w_ada + b_ada, 6, axis=-1)


def modulate(x, shift, scale):
    return layer_norm(x) * (1 + scale) + shift


def axial_rope(rows):
    row = jnp.repeat(jnp.arange(rows), GRID_W).astype(jnp.float32)
    col = jnp.tile(jnp.arange(GRID_W), rows).astype(jnp.float32)
    n_freq = QK_ROPE_DIM // 4
    freqs = ROPE_THETA ** (-jnp.arange(n_freq, dtype=jnp.float32) / n_freq)
    ang = jnp.concatenate([row[:, None] * freqs, col[:, None] * freqs], axis=-1)
    return jnp.cos(ang), jnp.sin(ang)


def rope2d(x, cos, sin):
    xr = x.astype(jnp.float32).reshape(x.shape[:-1] + (QK_ROPE_DIM // 2, 2))
    x0, x1 = xr[..., 0], xr[..., 1]
    out = jnp.stack([x0 * cos - x1 * sin, x0 * sin + x1 * cos], axis=-1)
    return out.reshape(x.shape).astype(x.dtype)


def mla_queries(p, q_norm_g, w_uq):
    q_c = rms_norm(p[..., :Q_LORA_RANK], q_norm_g)
    q = jnp.einsum('blr,rhe->blhe', q_c, w_uq)
    return q[..., :QK_NOPE_DIM], q[..., QK_NOPE_DIM:]


def mla_keys(p, kv_norm_g, w_uk, w_uv):
    kv_c = rms_norm(p[..., Q_LORA_RANK:KV_END], kv_norm_g)
    k_rope = p[..., KV_END:ROPE_END]
    k_nope = jnp.einsum('blr,rhe->blhe', kv_c, w_uk)
    v = jnp.einsum('blr,rhe->blhe', kv_c, w_uv)
    return k_nope, k_rope, v


def block_attention(q_nope, q_rope, k_nope, k_rope, v):
    b, lq = q_nope.shape[:2]
    nb = lq // Q_BLOCK
    scale = (QK_NOPE_DIM + QK_ROPE_DIM) ** -0.5

    def blocks(t):
        return jnp.moveaxis(t.reshape((b, nb, Q_BLOCK) + t.shape[2:]), 1, 0)

    def one_block(qs):
        qn, qr = qs
        s = jnp.einsum('bqhe,bkhe->bhqk', qn, k_nope) + jnp.einsum('bqhe,bke->bhqk', qr, k_rope)
        p = jax.nn.softmax(s.astype(jnp.float32) * scale, axis=-1).astype(v.dtype)
        return jnp.einsum('bhqk,bkhe->bqhe', p, v)

    o = lax.map(one_block, (blocks(q_nope), blocks(q_rope)))
    return jnp.moveaxis(o, 0, 1).reshape(b, lq, MLA_WIDTH)


def s5_discretise(lam_re, lam_im, log_dt, b_re, b_im):
    lam_re, lam_im = lam_re.astype(jnp.float32), lam_im.astype(jnp.float32)
    b_re, b_im = b_re.astype(jnp.float32), b_im.astype(jnp.float32)
    dt = jnp.exp(log_dt.astype(jnp.float32))[:, None]
    mag = jnp.exp(lam_re * dt)
    a_re, a_im = mag * jnp.cos(lam_im * dt), mag * jnp.sin(lam_im * dt)
    den = lam_re * lam_re + lam_im * lam_im
    f_re = ((a_re - 1) * lam_re + a_im * lam_im) / den
    f_im = (a_im * lam_re - (a_re - 1) * lam_im) / den
    bb_re = f_re[..., None] * b_re - f_im[..., None] * b_im
    bb_im = f_re[..., None] * b_im + f_im[..., None] * b_re
    return a_re, a_im, bb_re, bb_im


def complex_affine_combine(e1, e2):
    a1r, a1i, b1r, b1i = e1
    a2r, a2i, b2r, b2i = e2
    return (a2r * a1r - a2i * a1i,
            a2r * a1i + a2i * a1r,
            a2r * b1r - a2i * b1i + b2r,
            a2r * b1i + a2i * b1r + b2i)


def s5_states(u, a_re, a_im, bb_re, bb_im, reverse, h0=None):
    if reverse:
        u = jnp.flip(u, 0)
    bu_re = jnp.einsum('lbgc,gpc->lbgp', u, bb_re)
    bu_im = jnp.einsum('lbgc,gpc->lbgp', u, bb_im)
    if h0 is not None:
        bu_re = bu_re.at[0].add(a_re * h0[0] - a_im * h0[1])
        bu_im = bu_im.at[0].add(a_re * h0[1] + a_im * h0[0])
    n = u.shape[0]
    a_seq_re = jnp.broadcast_to(a_re, (n, 1) + a_re.shape)
    a_seq_im = jnp.broadcast_to(a_im, (n, 1) + a_im.shape)
    _, _, h_re, h_im = lax.associative_scan(complex_affine_combine, (a_seq_re, a_seq_im, bu_re, bu_im), axis=0)
    if reverse:
        h_re, h_im = jnp.flip(h_re, 0), jnp.flip(h_im, 0)
    return h_re, h_im


def s5_readout(h, c_re, c_im):
    h_re, h_im = h
    return (jnp.einsum('lbgp,gcp->lbgc', h_re, c_re.astype(jnp.float32))
            - jnp.einsum('lbgp,gcp->lbgc', h_im, c_im.astype(jnp.float32)))


def s5_glu(y, w_glu, b_glu):
    l, b = y.shape[:2]
    y = jnp.swapaxes(y.reshape(l, b, S5_WIDTH), 0, 1)
    z = jax.nn.gelu(y)
    return z * jax.nn.sigmoid(z @ w_glu.astype(jnp.float32) + b_glu.astype(jnp.float32))


def to_groups(u):
    return jnp.swapaxes(u, 0, 1).reshape(u.shape[1], u.shape[0], S5_GROUPS, S5_GROUP).astype(jnp.float32)


def s5_mixer(u_lat, u_ctx, lp, with_ctx_out):
    ul, uc = to_groups(u_lat), to_groups(u_ctx)
    d = lp['s5_d'].astype(jnp.float32).reshape(S5_GROUPS, S5_GROUP)
    y_lat = ul * d
    y_ctx = uc * d if with_ctx_out else None
    for direction, reverse in ((0, False), (1, True)):
        a_re, a_im, bb_re, bb_im = s5_discretise(lp['s5_lambda_re'][direction], lp['s5_lambda_im'][direction],
                                                 lp['s5_log_dt'][direction], lp['s5_b_re'][direction],
                                                 lp['s5_b_im'][direction])
        h_ctx = s5_states(uc, a_re, a_im, bb_re, bb_im, reverse)
        edge = 0 if reverse else -1
        h_lat = s5_states(ul, a_re, a_im, bb_re, bb_im, reverse, h0=(h_ctx[0][edge], h_ctx[1][edge]))
        y_lat = y_lat + s5_readout(h_lat, lp['s5_c_re'][direction], lp['s5_c_im'][direction])
        if with_ctx_out:
            y_ctx = y_ctx + s5_readout(h_ctx, lp['s5_c_re'][direction], lp['s5_c_im'][direction])
    out_lat = s5_glu(y_lat, lp['s5_w_glu'], lp['s5_b_glu']).astype(u_lat.dtype)
    out_ctx = s5_glu(y_ctx, lp['s5_w_glu'], lp['s5_b_glu']).astype(u_ctx.dtype) if with_ctx_out else None
    return out_lat, out_ctx


def mixer_sublayer(u_lat, u_ctx, lp, cos, sin, with_ctx_out):
    p_lat = u_lat @ lp['w_in']
    p_ctx = u_ctx @ lp['w_in']
    qn, qr = mla_queries(p_lat, lp['q_norm_g'], lp['w_uq'])
    qr = rope2d(qr, cos[:, None, :], sin[:, None, :])
    kn, kr, v = mla_keys(p_lat, lp['kv_norm_g'], lp['w_uk'], lp['w_uv'])
    kr = rope2d(kr, cos, sin)
    kn_c, kr_c, v_c = mla_keys(p_ctx, lp['kv_norm_g'], lp['w_uk'], lp['w_uv'])
    att_lat = block_attention(qn, qr, jnp.concatenate([kn_c, kn], 1), jnp.concatenate([kr_c, kr], 1),
                              jnp.concatenate([v_c, v], 1))
    s5_lat, s5_ctx = s5_mixer(p_lat[..., ROPE_END:], p_ctx[..., ROPE_END:], lp, with_ctx_out)
    y_lat = jnp.concatenate([att_lat, s5_lat], axis=-1) @ lp['w_out']
    y_ctx = None
    if with_ctx_out:
        qn_c, qr_c = mla_queries(p_ctx, lp['q_norm_g'], lp['w_uq'])
        att_ctx = block_attention(qn_c, qr_c, kn_c, kr_c, v_c)
        y_ctx = jnp.concatenate([att_ctx, s5_ctx], axis=-1) @ lp['w_out']
    return y_lat, y_ctx


def swiglu(u, w_gate_up, w_down):
    gate, up = jnp.split(u @ w_gate_up, 2, axis=-1)
    return (jax.nn.silu(gate) * up) @ w_down


def trunk_layer(x, ctx, mod_lat, mod_ctx, lp, cos, sin, with_ctx_out):
    sh1, sc1, g1, sh2, sc2, g2 = mod_lat
    csh1, csc1, cg1, csh2, csc2, cg2 = mod_ctx
    y_lat, y_ctx = mixer_sublayer(modulate(x, sh1, sc1), modulate(ctx, csh1, csc1), lp, cos, sin, with_ctx_out)
    x = layer_norm(DN_ALPHA * x + g1 * y_lat, lp['ln1_g'], lp['ln1_b'])
    x = layer_norm(DN_ALPHA * x + g2 * swiglu(modulate(x, sh2, sc2), lp['w_gate_up'], lp['w_down']),
                   lp['ln2_g'], lp['ln2_b'])
    if with_ctx_out:
        ctx = layer_norm(DN_ALPHA * ctx + cg1 * y_ctx, lp['ln1_g'], lp['ln1_b'])
        ctx = layer_norm(DN_ALPHA * ctx + cg2 * swiglu(modulate(ctx, csh2, csc2), lp['w_gate_up'], lp['w_down']),
                         lp['ln2_g'], lp['ln2_b'])
    return x, ctx


def setup_inputs(seed: int = 0) -> dict:
    key = jax.random.key(seed)
    ks = iter(jax.random.split(key, 29))
    f32 = jnp.float32

    def nrm(shape, scale):
        return jax.random.normal(next(ks), shape, f32) * scale

    lam_re = -0.5 + nrm((DEPTH, 2, S5_GROUPS, S5_STATE), 0.01)
    lam_im = jnp.pi * jnp.arange(S5_STATE, dtype=f32) + nrm((DEPTH, 2, S5_GROUPS, S5_STATE), 0.01)
    return {
        'x': nrm((BATCH, SEQ, D_MODEL), 1.0),
        'c': nrm((BATCH, D_MODEL), 1.0),
        'ctx': nrm((BATCH, CTX_LEN, D_MODEL), 1.0),
        'c_ctx': nrm((D_MODEL,), 1.0),
        'w_ada': nrm((DEPTH, D_MODEL, 6 * D_MODEL), D_MODEL ** -0.5),
        'b_ada': nrm((DEPTH, 6 * D_MODEL), 0.02),
        'w_in': nrm((DEPTH, D_MODEL, D_IN), D_MODEL ** -0.5),
        'q_norm_g': 1.0 + nrm((DEPTH, Q_LORA_RANK), 0.02),
        'kv_norm_g': 1.0 + nrm((DEPTH, KV_LORA_RANK), 0.02),
        'w_uq': nrm((DEPTH, Q_LORA_RANK, N_MLA_HEADS, QK_NOPE_DIM + QK_ROPE_DIM), Q_LORA_RANK ** -0.5),
        'w_uk': nrm((DEPTH, KV_LORA_RANK, N_MLA_HEADS, QK_NOPE_DIM), KV_LORA_RANK ** -0.5),
        'w_uv': nrm((DEPTH, KV_LORA_RANK, N_MLA_HEADS, V_HEAD_DIM), KV_LORA_RANK ** -0.5 * DN_BETA),
        's5_lambda_re': lam_re,
        's5_lambda_im': lam_im,
        's5_log_dt': jax.random.uniform(next(ks), (DEPTH, 2, S5_GROUPS), f32,
                                        minval=math.log(DT_MIN), maxval=math.log(DT_MAX)),
        's5_b_re': nrm((DEPTH, 2, S5_GROUPS, S5_STATE, S5_GROUP), (2 * S5_GROUP) ** -0.5),
        's5_b_im': nrm((DEPTH, 2, S5_GROUPS, S5_STATE, S5_GROUP), (2 * S5_GROUP) ** -0.5),
        's5_c_re': nrm((DEPTH, 2, S5_GROUPS, S5_GROUP, S5_STATE), (2 * S5_STATE) ** -0.5),
        's5_c_im': nrm((DEPTH, 2, S5_GROUPS, S5_GROUP, S5_STATE), (2 * S5_STATE) ** -0.5),
        's5_d': nrm((DEPTH, S5_WIDTH), 1.0),
        's5_w_glu': nrm((DEPTH, S5_WIDTH, S5_WIDTH), S5_WIDTH ** -0.5),
        's5_b_glu': nrm((DEPTH, S5_WIDTH), 0.02),
        'w_out': nrm((DEPTH, D_MODEL, D_MODEL), D_MODEL ** -0.5 * DN_BETA),
        'ln1_g': 1.0 + nrm((DEPTH, D_MODEL), 0.02),
        'ln1_b': nrm((DEPTH, D_MODEL), 0.02),
        'w_gate_up': nrm((DEPTH, D_MODEL, 2 * D_FF), D_MODEL ** -0.5),
        'w_down': nrm((DEPTH, D_FF, D_MODEL), D_FF ** -0.5 * DN_BETA),
        'ln2_g': 1.0 + nrm((DEPTH, D_MODEL), 0.02),
        'ln2_b': nrm((DEPTH, D_MODEL), 0.02),
    }


def reference(x, c, ctx, c_ctx, w_ada, b_ada, w_in, q_norm_g, kv_norm_g, w_uq, w_uk, w_uv,
              s5_lambda_re, s5_lambda_im, s5_log_dt, s5_b_re, s5_b_im, s5_c_re, s5_c_im, s5_d,
              s5_w_glu, s5_b_glu, w_out, ln1_g, ln1_b, w_gate_up, w_down, ln2_g, ln2_b):
    ROWS = x.shape[1] // GRID_W
    cos, sin = axial_rope(ROWS)
    for i in range(DEPTH):
        lp = dict(w_in=w_in[i], q_norm_g=q_norm_g[i], kv_norm_g=kv_norm_g[i], w_uq=w_uq[i], w_uk=w_uk[i],
                  w_uv=w_uv[i], s5_lambda_re=s5_lambda_re[i], s5_lambda_im=s5_lambda_im[i],
                  s5_log_dt=s5_log_dt[i], s5_b_re=s5_b_re[i], s5_b_im=s5_b_im[i], s5_c_re=s5_c_re[i],
                  s5_c_im=s5_c_im[i], s5_d=s5_d[i], s5_w_glu=s5_w_glu[i], s5_b_glu=s5_b_glu[i],
                  w_out=w_out[i], ln1_g=ln1_g[i], ln1_b=ln1_b[i], w_gate_up=w_gate_up[i],
                  w_down=w_down[i], ln2_g=ln2_g[i], ln2_b=ln2_b[i])
        mod_lat = adaln(c[:, None, :], w_ada[i], b_ada[i])
        mod_ctx = adaln(c_ctx, w_ada[i], b_ada[i])
        x, ctx = trunk_layer(x, ctx, mod_lat, mod_ctx, lp, cos, sin, with_ctx_out=(i < DEPTH - 1))
    return x
```

```python
import numpy as np
import concourse.bass as bass
import concourse.mybir as mybir
from concourse.bass_utils import run_bass_kernel_spmd

F32 = mybir.dt.float32
BF16 = mybir.dt.bfloat16
AF = mybir.ActivationFunctionType
ALU = mybir.AluOpType

D = 1024
L = 4096
CTX = 256
T = CTX + L
NH = 8
QL = 384
KVL = 256
DIN = 1184
DFF = 2816
EPS = 1e-6
ALPHA = 2.0 ** 0.25
SCALE = 96.0 ** -0.5
MAGIC = 12582912.0
TWO_PI = float(2 * np.pi)
NT = 9
SEG = 1088
NSEG = T // SEG


class Reg:
    __slots__ = ("w", "r")

    def __init__(self):
        self.w = None
        self.r = {}


class DmaGroup:
    def __init__(self, fw, name, final=False):
        self.fw = fw
        self.name = name
        self.sems = [fw.nc.alloc_semaphore(name="dg_" + name)]
        self.count = 0
        self.final = final

    @property
    def ep(self):
        return len(self.sems) - 1

    def roll(self):
        self.sems.append(self.fw.nc.alloc_semaphore(name="dg_%s_%d" % (self.name, len(self.sems))))
        self.count = 0


class FW:
    ENG = ("pe", "dve", "act", "pool", "sp")
    LIMIT = 900
    DLIMIT = 40

    def __init__(self, nc):
        self.nc = nc
        self.sems = {k: [nc.alloc_semaphore(name="es_" + k)] for k in self.ENG}
        self.cnt = {k: 0 for k in self.ENG}
        self.pending = {k: False for k in self.ENG}
        self.seen = {k: {} for k in self.ENG}
        self.prog = {k: [] for k in self.ENG}
        self.groups = []
        self.trace = {k: [] for k in self.ENG}

    def ep(self, e):
        return len(self.sems[e]) - 1

    def group(self, name, final=False):
        g = DmaGroup(self, name, final)
        self.groups.append(g)
        return g

    def _need(self, e, tok):
        if tok is None:
            return
        kind, obj, val = tok
        if kind == "dma":
            grp, gep = obj
            key = (id(grp), gep)
            sem = grp.sems[gep]
            if grp.final:
                if self.seen[e].get(key, 0):
                    return
                self.seen[e][key] = 1
                self.prog[e].append(lambda E, sem=sem, grp=grp: E.wait_ge(sem, 16 * grp.count))
                self.trace[e].append(("waitf", key, grp))
                return
            if self.seen[e].get(key, 0) >= val:
                return
            self.seen[e][key] = val
            self.prog[e].append(lambda E, sem=sem, val=val: E.wait_ge(sem, 16 * val))
            self.trace[e].append(("wait", key, val))
            return
        eng, eep = obj
        key = obj
        if self.seen[e].get(key, 0) >= val:
            return
        if eng == e and eep == self.ep(e) and val > self.cnt[e]:
            return
        self.seen[e][key] = val
        sem = self.sems[eng][eep]
        self.prog[e].append(lambda E, sem=sem, val=val: E.wait_ge(sem, val))
        self.trace[e].append(("wait", key, val))

    def _deps(self, e, reads, writes):
        for reg in reads:
            self._need(e, reg.w)
        for reg in writes:
            self._need(e, reg.w)
            for t in reg.r.values():
                self._need(e, t)

    def I(self, e, fn, reads=(), writes=(), signal=True):
        if self.cnt[e] >= self.LIMIT and not self.pending[e]:
            self.sems[e].append(self.nc.alloc_semaphore(name="es_%s_%d" % (e, len(self.sems[e]))))
            self.cnt[e] = 0
        self._deps(e, reads, writes)
        idx = self.cnt[e] + 1
        tok = ("eng", (e, self.ep(e)), idx)
        for reg in reads:
            reg.r[e] = tok
        for reg in writes:
            reg.w = tok
            reg.r = {}
        if signal:
            self.cnt[e] = idx
            self.pending[e] = False
            sem = self.sems[e][-1]
            self.prog[e].append(lambda E, fn=fn, sem=sem: fn(E).then_inc(sem, 1))
            self.trace[e].append(("inc", (e, self.ep(e))))
        else:
            self.pending[e] = True
            self.prog[e].append(lambda E, fn=fn: fn(E))

    def D(self, q, grp, fn, reads=(), writes=()):
        subs = grp.__dict__.setdefault("subs", {})
        if q not in subs:
            if not subs:
                subs[q] = grp
            else:
                sg = DmaGroup(self, grp.name + "_" + q, grp.final)
                if getattr(grp, "skip_barrier", False):
                    sg.skip_barrier = True
                self.groups.append(sg)
                subs[q] = sg
        grp = subs[q]
        if grp.count >= self.DLIMIT and not grp.final:
            grp.roll()
        def same(t):
            return t[0] == "dma" and t[1][0] is grp
        for reg in reads:
            if reg.w is not None and not same(reg.w):
                self._need(q, reg.w)
        for reg in writes:
            for t in [reg.w] + list(reg.r.values()):
                if t is not None and not same(t):
                    self._need(q, t)
        grp.count += 1
        tok = ("dma", (grp, grp.ep), grp.count)
        for reg in reads:
            reg.r["dma%d" % id(grp)] = tok
        for reg in writes:
            reg.w = tok
            reg.r = {}
        sem = grp.sems[-1]
        self.prog[q].append(lambda E, fn=fn, sem=sem: fn(E).then_inc(sem, 16))
        self.trace[q].append(("inc", (id(grp), grp.ep)))

    def wait_all(self, e, regs):
        for reg in regs:
            self._need(e, reg.w)
            for t in reg.r.values():
                self._need(e, t)

    def barrier(self):
        for e in self.ENG:
            for e2 in self.ENG:
                if e2 != e and self.cnt[e2] > 0:
                    self._need(e, ("eng", (e2, self.ep(e2)), self.cnt[e2]))
            for g in self.groups:
                if g.count > 0 and not getattr(g, "skip_barrier", False):
                    self._need(e, ("dma", (g, g.ep), g.count))

    def check(self):
        val = {}
        pc = {k: 0 for k in self.ENG}
        while True:
            prog = False
            for e in self.ENG:
                tr = self.trace[e]
                while pc[e] < len(tr):
                    op = tr[pc[e]]
                    if op[0] == "inc":
                        val[op[1]] = val.get(op[1], 0) + 1
                    elif op[0] == "wait":
                        if val.get(op[1], 0) < op[2]:
                            break
                    else:
                        if val.get(op[1], 0) < op[2].count:
                            break
                    pc[e] += 1
                    prog = True
            if not prog:
                break
        stuck = {e: (pc[e], len(self.trace[e]), self.trace[e][pc[e]]) for e in self.ENG if pc[e] < len(self.trace[e])}
        assert not stuck, "SYNC DEADLOCK: %s" % stuck

    def emit(self):
        nc = self.nc
        self.check()
        with nc.Block() as block:
            @block.tensor
            def _(E):
                for f in self.prog["pe"]:
                    f(E)

            @block.vector
            def _(E):
                for f in self.prog["dve"]:
                    f(E)

            @block.scalar
            def _(E):
                for f in self.prog["act"]:
                    f(E)

            @block.gpsimd
            def _(E):
                for f in self.prog["pool"]:
                    f(E)

            @block.sync
            def _(E):
                for f in self.prog["sp"]:
                    f(E)


class Arena:
    LO = 20608
    HI = 228352

    def __init__(self, nc):
        self.nc = nc
        self.lo = self.LO
        self.hi = self.HI
        self.n = 0

    def _sz(self, shape, dt):
        sz = int(np.prod(shape[1:])) * (4 if dt == F32 else 2)
        return (sz + 63) // 64 * 64

    def alloc(self, name, shape, dt, top=False):
        sz = self._sz(shape, dt)
        self.n += 1
        if top:
            self.hi -= sz
            off = self.hi
        else:
            off = self.lo
            self.lo += sz
        assert self.lo <= self.hi, "SBUF arena overflow at %s: lo=%d hi=%d" % (name, self.lo, self.hi)
        return self.nc.alloc_sbuf_tensor_at("%s_%d" % (name, self.n), list(shape), dt, offset=off)

    def mark(self):
        return self.lo

    def release(self, m):
        self.lo = m


class StopBuild(Exception):
    pass


import os
CUT = int(os.environ.get("CUT", "0"))


def cut(n):
    if CUT == n:
        raise StopBuild()


def build(dbg=(), stop_after=99):
    nc = bass.Bass("TRN2", target_bir_lowering=False)
    fw = FW(nc)
    ar = Arena(nc)
    I, Dm = fw.I, fw.D

    used_inputs = []

    class LazyIn:
        def __init__(self, name, shape):
            self.name, self.shape, self._ap = name, shape, None

        def ap(self):
            if self._ap is None:
                used_inputs.append(self.name)
                self._ap = nc.dram_tensor(self.name, list(self.shape), F32, kind="ExternalInput").ap()
            return self._ap

        def __getitem__(self, k):
            return self.ap()[k]

        def rearrange(self, *a, **kw):
            return self.ap().rearrange(*a, **kw)

    def din(name, shape, dt=F32, stage=0):
        return LazyIn(name, shape)

    x_d = din("x", [L, D], stage=1)
    ctx_d = din("ctx", [CTX, D], stage=1)
    cc_d = din("cc", [128, 16], stage=0)
    wada_d = din("w_ada", [D, 6 * D], stage=0)
    bada_d = din("b_ada", [1, 6 * D], stage=0)
    win_d = din("w_in_l", [D, 1280], stage=1)
    wuq_d = din("w_uq_l", [QL, NH * 192], stage=3)
    wuk_d = din("w_uk_l", [KVL, NH * 96], stage=3)
    wuv_d = din("w_uv_l", [KVL, NH * 64], stage=3)
    qg_d = din("qg_l", [128, 3], stage=3)
    kvg_d = din("kvg_l", [128, 2], stage=3)
    rope_d = din("rope_cs", [32, 2, L], stage=1)
    esel_d = din("esel", [32, 96], stage=3)
    ident_d = din("ident", [128, 128], stage=0)
    s5col_d = din("s5_col", [128, 3, 32], stage=2)
    s5row_d = din("s5_row", [128, 3, 1024], stage=2)
    s5b_d = din("s5_b_l", [128, 2, 1024], stage=2)
    s5c_d = din("s5_c_l", [128, 2, 32, 128], stage=2)
    s5d_d = din("s5_d_col", [128, 4], stage=2)
    wglu_d = din("s5_w_glu", [512, 512], stage=2)
    bglu_d = din("b_glu_col", [128, 4], stage=2)
    wout_d = din("w_out", [D, D], stage=4)
    lnp_d = din("ln_params", [4, D], stage=4)
    wgu_d = din("w_gate_up", [D, 2 * DFF], stage=4)
    wdn_d = din("w_down", [DFF, D], stage=4)
    out_d = nc.dram_tensor("out", [L, D], F32, kind="ExternalOutput").ap()
    dbg_d = {}

    def dbg_out(name, shape):
        dbg_d[name] = nc.dram_tensor("dbg_" + name, list(shape), F32, kind="ExternalOutput").ap()
        return dbg_d[name]

    g_const = fw.group("const", final=True)
    g_dbg = fw.group("dbg")
    dbg_regs = []

    def dump(name, src_ap, reg, shape):
        if name not in dbg:
            return
        o = dbg_out(name, shape)
        Dm("pool", g_dbg, lambda E: E.dma_start(out=o, in_=src_ap), reads=[reg])
        dbg_regs.append(reg)

    from contextlib import ExitStack
    es = ExitStack()
    with es:
        ptr = es.enter_context(nc.psum_tensor("ptr", [128, 1024], BF16))
        r_ptr = Reg()
        pbank = [es.enter_context(nc.psum_tensor("pb%d" % i, [128, 512], F32)) for i in range(7)]
        r_pb = [Reg() for _ in range(7)]
        pb_next = [0]

        def bank():
            i = pb_next[0] % 7
            pb_next[0] += 1
            return pbank[i], r_pb[i]

        ident = ar.alloc("ident", [128, 128], BF16); r_ident = Reg()
        identf = ar.alloc("identf", [128, 128], F32); r_identf = Reg()
        Dm("pool", g_const, lambda E: E.dma_start(out=ident[:], in_=ident_d[:, :]), writes=[r_ident])
        Dm("sp", g_const, lambda E: E.dma_start(out=identf[:], in_=ident_d[:, :]), writes=[r_identf])
        onesr = ar.alloc("onesr", [1, 128], F32); r_onesr = Reg()
        I("pool", lambda E: E.memset(onesr[:], 1.0), writes=[r_onesr])
        onesb = ar.alloc("onesb", [128, 128], BF16); r_onesb = Reg()
        I("pool", lambda E: E.memset(onesb[:], 1.0), writes=[r_onesb])
        epsc = ar.alloc("epsc", [128, 1], F32); r_epsc = Reg()
        I("pool", lambda E: E.memset(epsc[:], EPS), writes=[r_epsc])
        halfpi = ar.alloc("halfpi", [128, 1], F32); r_halfpi = Reg()
        I("pool", lambda E: E.memset(halfpi[:], float(np.pi / 2)), writes=[r_halfpi])
        modcol = ar.alloc("modcol", [128, 8, 6], F32); r_modcol = Reg()
        g12b = ar.alloc("g12b", [128, 2, D], F32); r_g12b = Reg()

        try:
            m0 = ar.mark()
            cct = ar.alloc("cct", [128, 16], F32); r_cct = Reg()
            Dm("sp", g_const, lambda E: E.dma_start(out=cct[:], in_=cc_d[:, :]), writes=[r_cct])
            scb = ar.alloc("scb", [128, 8, 32], BF16); r_scb = Reg()
            I("pool", lambda E: E.memset(scb[:], 0.0), writes=[r_scb])
            I("act", lambda E: E.activation(out=scb[:, :, 0:2], in_=cct[:].rearrange("p (k j) -> p k j", j=2), func=AF.Silu), reads=[r_cct], writes=[r_scb])
            badat = ar.alloc("badat", [2, 6 * D], F32); r_badat = Reg()
            Dm("sp", g_const, lambda E: E.dma_start(out=badat[:], in_=bada_d[0:1, :].partition_broadcast(2)), writes=[r_badat])
            cut(1)
            modrow = ar.alloc("modrow", [2, 6 * D], F32); r_modrow = Reg()
            wa = [ar.alloc("wa%d" % i, [128, 8, 512], BF16) for i in range(2)]
            r_wa = [Reg(), Reg()]
            g_wa = [fw.group("wa0"), fw.group("wa1")]
            wada_v = wada_d.rearrange("(k p) n -> p k n", p=128)
            for cch in range(12):
                b = cch % 2
                Dm("pool", g_wa[b], lambda E, b=b, cch=cch: E.dma_start(out=wa[b][:], in_=wada_v[:, :, cch * 512:(cch + 1) * 512]), writes=[r_wa[b]])
                VAR = int(os.environ.get("VAR", "0"))
                if VAR == 1:
                    continue
                pb, rpb = bank()
                for k in range(8):
                    I("pe", lambda E, pb=pb, b=b, k=k: E.matmul(pb[0:32, :], lhsT=scb[:, k, :], rhs=wa[b][:, k, :], start=(k == 0), stop=(k == 7)),
                      reads=[r_scb, r_wa[b]], writes=[rpb], signal=(k == 7 or VAR == 3))
                if VAR in (2, 3):
                    continue
                I("dve", lambda E, pb=pb, cch=cch: E.tensor_tensor(out=modrow[:, cch * 512:(cch + 1) * 512], in0=pb[0:2, :], in1=badat[:, cch * 512:(cch + 1) * 512], op=ALU.add),
                  reads=[rpb, r_badat], writes=[r_modrow])
            cut(2)
            for c0 in (D, 4 * D):
                I("dve", lambda E, c0=c0: E.tensor_scalar(out=modrow[:, c0:c0 + D], in0=modrow[:, c0:c0 + D], scalar1=1.0, scalar2=None, op0=ALU.add),
                  reads=[r_modrow], writes=[r_modrow])
            cut(3)
            pb, rpb = bank()
            srcs = [0, 1, 3, 4]
            for k in range(8):
                for jj, s in enumerate(srcs):
                    col = (k * 4 + jj) * 2
                    I("pe", lambda E, pb=pb, k=k, s=s, col=col: E.transpose(out=pb[:, col:col + 2], in_=modrow[0:2, s * D + k * 128:s * D + (k + 1) * 128], identity=identf[0:2, 0:2]),
                      reads=[r_modrow, r_identf], writes=[rpb], signal=(k == 7 and jj == 3))
            cut(4)
            pbv = pb[:, 0:64].rearrange("p (k j two) -> p k j two", k=8, j=4, two=2)
            I("dve", lambda E: E.tensor_copy(out=modcol[:, :, 0:4], in_=pbv[:, :, :, 0]), reads=[rpb], writes=[r_modcol])
            I("dve", lambda E: E.tensor_copy(out=modcol[:, :, 4:6], in_=pbv[:, :, 0:2, 1]), reads=[rpb], writes=[r_modcol])
            cut(5)
            for gi, c0 in enumerate((2 * D, 5 * D)):
                for hh in range(2):
                    pb, rpb = bank()
                    I("pe", lambda E, pb=pb, c0=c0, hh=hh: E.matmul(pb[:, :], lhsT=onesr[0:1, :], rhs=modrow[0:1, c0 + hh * 512:c0 + (hh + 1) * 512], start=True, stop=True),
                      reads=[r_onesr, r_modrow], writes=[rpb])
                    I("act", lambda E, pb=pb, gi=gi, hh=hh: E.activation(out=g12b[:, gi, hh * 512:(hh + 1) * 512], in_=pb[:, :], func=AF.Identity),
                      reads=[rpb], writes=[r_g12b])
            dump("modcol", modcol[:].rearrange("p k j -> p (k j)"), r_modcol, [128, 48])
            dump("g12b", g12b[:].rearrange("p a d -> p (a d)"), r_g12b, [128, 2 * D])
            fw.barrier()
            ar.release(m0)
            g_const = fw.group("const1", final=True)

            if stop_after >= 1:
                mA = ar.mark()
                kvcT = ar.alloc("kvcT", [128, 2, T], BF16); r_kvcT = [Reg() for _ in range(NT)]
                krT = ar.alloc("krT", [32, T], BF16); r_krT = [Reg() for _ in range(NT)]
                qcT = ar.alloc("qcT", [128, 3, L], BF16); r_qcT = [Reg() for _ in range(NT)]
                mB = ar.mark()
                us5T = ar.alloc("us5T", [128, 4, T], BF16); r_us5T = [Reg() for _ in range(NT)]
                m1a = ar.mark()
                winb = ar.alloc("winb", [128, 8, 1280], BF16); r_winb = Reg()
                win_v = win_d.rearrange("(k p) n -> p k n", p=128)
                for k in range(8):
                    Dm("pool", g_const, lambda E, k=k: E.dma_start(out=winb[:, k, :], in_=win_v[:, k, :]), writes=[r_winb])
                cut(10)
                wgu_bf = nc.dram_tensor("wgu_bf", [44, 128, 1024], BF16).ap()
                wdn_bf = nc.dram_tensor("wdn_bf", [2, 128, 22 * 512], BF16).ap()
                r_wgubf = Reg(); r_wdnbf = Reg()
                g_pre = fw.group("precast", final=True)
                g_pre.skip_barrier = True
                wgu_v0 = wgu_d.rearrange("(k p) n -> p k n", p=128)
                wdn_v0 = wdn_d.rearrange("(f p) n -> p f n", p=128)
                for c in range(44):
                    gu_, f_ = c // 22, c % 22
                    c0 = gu_ * DFF + f_ * 128
                    Dm("pool", g_pre, lambda E, c=c, c0=c0: E.dma_start(out=wgu_bf[c].rearrange("p (k n) -> p k n", k=8), in_=wgu_v0[:, :, c0:c0 + 128]), writes=[r_wgubf])
                for hh in range(2):
                    Dm("pool", g_pre, lambda E, hh=hh: E.dma_start(out=wdn_bf[hh].rearrange("p (f n) -> p f n", f=22), in_=wdn_v0[:, :, hh * 512:(hh + 1) * 512]), writes=[r_wdnbf])
                xt = [ar.alloc("xt%d" % i, [128, D], F32) for i in range(2)]; r_xt = [Reg(), Reg()]
                g_xt = [fw.group("xt0"), fw.group("xt1")]
                xh = ar.alloc("xh", [128, D], BF16); r_xh = Reg()
                uT = ar.alloc("uT", [128, 8, 512], BF16); r_uT = Reg()
                st6 = ar.alloc("st6", [128, 2, 6], F32); r_st6 = Reg()
                mv = ar.alloc("mv", [128, 2], F32); r_mv = Reg()
                rstd = ar.alloc("rstd", [128, 1], F32); r_rstd = Reg()
                nbias = ar.alloc("nbias", [128, 1], F32); r_nbias = Reg()
                ropet = ar.alloc("ropet", [32, 2, 512], F32); r_ropet = Reg()
                g_rope = fw.group("rope")
                sq = [ar.alloc("sq%d" % i, [128, 512], BF16) for i in range(2)]; r_sq = [Reg(), Reg()]
                Rn = ar.alloc("Rn", [128, 512], F32); r_Rn = Reg()
                tmpk = ar.alloc("tmpk", [32, 512], F32); r_tmpk = Reg()

                def ln_rows(src, r_src, dst_bf, r_dst):
                    for hh in range(2):
                        I("dve", lambda E, hh=hh: E.bn_stats(out=st6[:, hh, :], in_=src[:, hh * 512:(hh + 1) * 512]), reads=[r_src], writes=[r_st6])
                    I("dve", lambda E: E.bn_aggr(out=mv[:], in_=st6[:].rearrange("p a b -> p (a b)")), reads=[r_st6], writes=[r_mv])
                    I("act", lambda E: E.activation(out=rstd[:], in_=mv[:, 1:2], func=AF.Sqrt, bias=epsc[:, 0:1], scale=1.0), reads=[r_mv, r_epsc], writes=[r_rstd])
                    I("dve", lambda E: E.reciprocal(out=rstd[:], in_=rstd[:]), reads=[r_rstd], writes=[r_rstd])
                    I("dve", lambda E: E.scalar_tensor_tensor(out=nbias[:], in0=mv[:, 0:1], scalar=-1.0, in1=rstd[:], op0=ALU.mult, op1=ALU.mult),
                      reads=[r_mv, r_rstd], writes=[r_nbias])
                    I("act", lambda E: E.activation(out=dst_bf[:], in_=src[:], func=AF.Identity, bias=nbias[:, 0:1], scale=rstd[:, 0:1]),
                      reads=[r_src, r_rstd, r_nbias], writes=[r_dst])

                def rms_scale(pbs, rpbs, nrows_total, n):
                    pS, rpS = bank()
                    for i, (pb, rpb) in enumerate(zip(pbs, rpbs)):
                        sb = i % 2
                        I("act", lambda E, pb=pb, sb=sb: E.activation(out=sq[sb][:, :n], in_=pb[:, :n], func=AF.Square), reads=[rpb], writes=[r_sq[sb]])
                        I("pe", lambda E, sb=sb, i=i, pS=pS: E.matmul(pS[:, :n], lhsT=onesb[:, :], rhs=sq[sb][:, :n], start=(i == 0), stop=(i == len(pbs) - 1)),
                          reads=[r_onesb, r_sq[sb]], writes=[rpS])
                    I("act", lambda E, pS=pS: E.activation(out=Rn[:, :n], in_=pS[:, :n], func=AF.Sqrt, bias=epsc[:, 0:1], scale=1.0 / nrows_total),
                      reads=[rpS, r_epsc], writes=[r_Rn])
                    I("dve", lambda E: E.reciprocal(out=Rn[:, :n], in_=Rn[:, :n]), reads=[r_Rn], writes=[r_Rn])

                sub = 0
                for ti in range(NT):
                    n = 256 if ti == 0 else 512
                    t0 = 0 if ti == 0 else CTX + (ti - 1) * 512
                    nsub = n // 128
                    for si in range(nsub):
                        b = sub % 2
                        sub += 1
                        if ti == 0:
                            src = ctx_d[si * 128:(si + 1) * 128, :]
                        else:
                            l0 = (ti - 1) * 512 + si * 128
                            src = x_d[l0:l0 + 128, :]
                        Dm("sp", g_xt[b], lambda E, b=b, src=src: E.dma_start(out=xt[b][:], in_=src), writes=[r_xt[b]])
                        ln_rows(xt[b], r_xt[b], xh, r_xh)
                        for k in range(8):
                            I("pe", lambda E, k=k: E.transpose(out=ptr[:, k * 128:(k + 1) * 128], in_=xh[:, k * 128:(k + 1) * 128], identity=ident[:, :]),
                              reads=[r_xh, r_ident], writes=[r_ptr], signal=(k == 7))
                        j0 = 4 if ti == 0 else 0
                        for k in range(8):
                            eng = "act" if k % 2 == 0 else "dve"
                            if eng == "act":
                                I("act", lambda E, k=k, si=si, j0=j0: E.activation(out=uT[:, k, si * 128:(si + 1) * 128], in_=ptr[:, k * 128:(k + 1) * 128], func=AF.Identity,
                                                                                   bias=modcol[:, k, j0:j0 + 1], scale=modcol[:, k, j0 + 1:j0 + 2]),
                                  reads=[r_ptr, r_modcol], writes=[r_uT])
                            else:
                                I("dve", lambda E, k=k, si=si, j0=j0: E.tensor_scalar(out=uT[:, k, si * 128:(si + 1) * 128], in0=ptr[:, k * 128:(k + 1) * 128],
                                                                                      scalar1=modcol[:, k, j0 + 1:j0 + 2], scalar2=modcol[:, k, j0:j0 + 1], op0=ALU.mult, op1=ALU.add),
                                  reads=[r_ptr, r_modcol], writes=[r_uT])
                    cut(11)
                    if ti == 1:
                        cut(25)

                    def proj(c0, m, n=n):
                        pb, rpb = bank()
                        for k in range(8):
                            I("pe", lambda E, pb=pb, k=k, c0=c0, m=m, n=n: E.matmul(pb[0:m, :n], lhsT=winb[:, k, c0:c0 + m], rhs=uT[:, k, :n], start=(k == 0), stop=(k == 7)),
                              reads=[r_winb, r_uT], writes=[rpb], signal=(k == 7))
                        return pb, rpb

                    if ti > 0:
                        l0 = (ti - 1) * 512
                        pq = [proj(i * 128, 128) for i in range(3)]
                        rms_scale([p[0] for p in pq], [p[1] for p in pq], float(QL), n)
                        for i in range(3):
                            I("dve", lambda E, i=i, l0=l0, pq=pq: E.tensor_tensor(out=qcT[:, i, l0:l0 + 512], in0=pq[i][0][:, :], in1=Rn[:, :], op=ALU.mult),
                              reads=[pq[i][1], r_Rn], writes=[r_qcT[ti]])
                    if ti == 1:
                        cut(26)
                    pk = [proj(QL + i * 128, 128) for i in range(2)]
                    cut(20)
                    if ti == 1:
                        cut(27)
                    rms_scale([p[0] for p in pk], [p[1] for p in pk], float(KVL), n)
                    cut(21)
                    if ti == 1:
                        cut(28)
                    for i in range(2):
                        I("dve", lambda E, i=i, n=n, t0=t0, pk=pk: E.tensor_tensor(out=kvcT[:, i, t0:t0 + n], in0=pk[i][0][:, :n], in1=Rn[:, :n], op=ALU.mult),
                          reads=[pk[i][1], r_Rn], writes=[r_kvcT[ti]])
                    cut(22)
                    if ti == 1:
                        cut(12)
                    pr, rpr = proj(QL + KVL, 32)
                    if ti == 0:
                        I("act", lambda E, pr=pr, n=n, t0=t0: E.activation(out=krT[:, t0:t0 + n], in_=pr[0:32, :n], func=AF.Identity), reads=[rpr], writes=[r_krT[ti]])
                    else:
                        ps_, rps = proj(QL + KVL + 32, 32)
                        l0 = (ti - 1) * 512
                        Dm("sp", g_rope, lambda E, l0=l0: E.dma_start(out=ropet[:], in_=rope_d[:, :, l0:l0 + 512]), writes=[r_ropet])
                        I("dve", lambda E, pr=pr: E.tensor_tensor(out=tmpk[:], in0=pr[0:32, :], in1=ropet[:, 0, :], op=ALU.mult), reads=[rpr, r_ropet], writes=[r_tmpk])
                        I("dve", lambda E, ps_=ps_: E.tensor_tensor(out=ropet[:, 1, :], in0=ps_[0:32, :], in1=ropet[:, 1, :], op=ALU.mult), reads=[rps, r_ropet], writes=[r_ropet])
                        I("dve", lambda E, t0=t0: E.tensor_tensor(out=krT[:, t0:t0 + 512], in0=tmpk[:], in1=ropet[:, 1, :], op=ALU.add), reads=[r_tmpk, r_ropet], writes=[r_krT[ti]])
                    cut(23)
                    if ti == 1:
                        cut(13)
                    for i in range(4):
                        p5, rp5 = proj(QL + KVL + 64 + i * 128, 128)
                        if i % 2 == 0:
                            I("act", lambda E, p5=p5, i=i, n=n, t0=t0: E.activation(out=us5T[:, i, t0:t0 + n], in_=p5[:, :n], func=AF.Identity), reads=[rp5], writes=[r_us5T[ti]])
                        else:
                            I("dve", lambda E, p5=p5, i=i, n=n, t0=t0: E.tensor_copy(out=us5T[:, i, t0:t0 + n], in_=p5[:, :n]), reads=[rp5], writes=[r_us5T[ti]])
                dump("qcT", qcT[:, :, 0:1024], r_qcT[1], [128, 3, 1024])
                dump("kvcT", kvcT[:, :, 0:1024], r_kvcT[1], [128, 2, 1024])
                dump("krT", krT[:, 0:1024], r_krT[1], [32, 1024])
                dump("us5T", us5T[:, :, 0:1024], r_us5T[1], [128, 4, 1024])
                fw.barrier()
                ar.release(m1a)

            s5T = ar.alloc("s5T", [128, 4, L], BF16, top=True); r_s5T = [Reg() for _ in range(8)]
            if os.environ.get("ZERO_S5"):
                I("pool", lambda E: E.memset(s5T[:], 0.0), writes=r_s5T)
            if stop_after >= 2 and not os.environ.get("ZERO_S5"):
                m2 = ar.mark()
                g_c2 = fw.group("const2", final=True)
                W = 1024
                colp = ar.alloc("colp", [128, 3, 32], F32); r_colp = Reg()
                dcol = ar.alloc("dcol", [128, 4], F32); r_dcol = Reg()
                bgl = ar.alloc("bgl", [128, 4], F32); r_bgl = Reg()
                CTb = ar.alloc("CTb", [128, 2, 32, 128], BF16); r_CTb = Reg()
                wglub = ar.alloc("wglub", [128, 4, 512], BF16); r_wglub = Reg()
                Dm("sp", g_c2, lambda E: E.dma_start(out=colp[:], in_=s5col_d[:, :, :]), writes=[r_colp])
                Dm("sp", g_c2, lambda E: E.dma_start(out=dcol[:], in_=s5d_d[:, :]), writes=[r_dcol])
                Dm("sp", g_c2, lambda E: E.dma_start(out=bgl[:], in_=bglu_d[:, :]), writes=[r_bgl])
                for c in range(2):
                    Dm("pool", g_c2, lambda E, c=c: E.dma_start(out=CTb[:, c, :, :], in_=s5c_d[:, c, :, :]), writes=[r_CTb])
                wglu_v = wglu_d.rearrange("(k p) n -> p k n", p=128)
                for k in range(4):
                    Dm("pool", g_c2, lambda E, k=k: E.dma_start(out=wglub[:, k, :], in_=wglu_v[:, k, :]), writes=[r_wglub])
                BbT = ar.alloc("BbT", [128, 2, W], BF16); r_BbT = Reg()
                rcol = ar.alloc("rcol", [128, 32], F32); r_rcol = Reg()
                thcol = ar.alloc("thcol", [128, 32], F32); r_thcol = Reg()
                m2b = ar.mark()
                prm = ar.alloc("prm", [128, 3, W], F32); r_prm = Reg()
                bl_ = ar.alloc("bl_", [128, 2, W], F32); r_bl = Reg()
                Dm("sp", g_c2, lambda E: E.dma_start(out=prm[:], in_=s5row_d[:, :, :]), writes=[r_prm])
                Dm("sp", g_c2, lambda E: E.dma_start(out=bl_[:], in_=s5b_d[:, :, :]), writes=[r_bl])
                tt = [ar.alloc("s5t%d" % i, [128, W], F32) for i in range(8)]
                r_tt = [Reg() for _ in range(8)]

                def ew(eng, fn, rd, wr):
                    I(eng, fn, reads=[r_tt[i] for i in rd] + [r_prm, r_bl], writes=[r_tt[i] for i in wr])

                LRE, LIM, LDT = prm[:, 0, :], prm[:, 1, :], prm[:, 2, :]
                DT, MAG, TH, KK, SN, CS, DEN, AM1 = [t[:] for t in tt]
                ew("act", lambda E: E.activation(out=DT, in_=LDT, func=AF.Exp), [], [0])
                ew("dve", lambda E: E.tensor_tensor(out=MAG, in0=LRE, in1=DT, op=ALU.mult), [0], [1])
                ew("act", lambda E: E.activation(out=MAG, in_=MAG, func=AF.Exp), [1], [1])
                ew("dve", lambda E: E.tensor_tensor(out=TH, in0=LIM, in1=DT, op=ALU.mult), [0], [2])
                ew("dve", lambda E: E.tensor_scalar(out=KK, in0=TH, scalar1=1.0 / TWO_PI, scalar2=MAGIC, op0=ALU.mult, op1=ALU.add), [2], [3])
                ew("dve", lambda E: E.tensor_scalar(out=KK, in0=KK, scalar1=MAGIC, scalar2=-TWO_PI, op0=ALU.subtract, op1=ALU.mult), [3], [3])
                ew("dve", lambda E: E.tensor_tensor(out=TH, in0=TH, in1=KK, op=ALU.add), [2, 3], [2])
                ew("act", lambda E: E.activation(out=SN, in_=TH, func=AF.Sin), [2], [4])
                ew("act", lambda E: E.activation(out=KK, in_=TH, func=AF.Abs), [2], [3])
                I("act", lambda E: E.activation(out=CS, in_=KK, func=AF.Sin, bias=halfpi[:, 0:1], scale=-1.0), reads=[r_tt[3], r_halfpi], writes=[r_tt[5]])
                ew("dve", lambda E: E.tensor_tensor(out=CS, in0=CS, in1=MAG, op=ALU.mult), [5, 1], [5])
                ew("dve", lambda E: E.tensor_tensor(out=SN, in0=SN, in1=MAG, op=ALU.mult), [4, 1], [4])
                ew("dve", lambda E: E.tensor_tensor(out=DEN, in0=LRE, in1=LRE, op=ALU.mult), [], [6])
                ew("dve", lambda E: E.tensor_tensor(out=KK, in0=LIM, in1=LIM, op=ALU.mult), [], [3])
                ew("dve", lambda E: E.tensor_tensor(out=DEN, in0=DEN, in1=KK, op=ALU.add), [6, 3], [6])
                ew("dve", lambda E: E.reciprocal(out=DEN, in_=DEN), [6], [6])
                ew("dve", lambda E: E.tensor_scalar(out=AM1, in0=CS, scalar1=-1.0, scalar2=None, op0=ALU.add), [5], [7])
                ew("dve", lambda E: E.tensor_tensor(out=DT, in0=AM1, in1=LRE, op=ALU.mult), [7], [0])
                ew("dve", lambda E: E.tensor_tensor(out=KK, in0=SN, in1=LIM, op=ALU.mult), [4], [3])
                ew("dve", lambda E: E.tensor_tensor(out=DT, in0=DT, in1=KK, op=ALU.add), [0, 3], [0])
                ew("dve", lambda E: E.tensor_tensor(out=DT, in0=DT, in1=DEN, op=ALU.mult), [0, 6], [0])
                ew("dve", lambda E: E.tensor_tensor(out=MAG, in0=SN, in1=LRE, op=ALU.mult), [4], [1])
                ew("dve", lambda E: E.tensor_tensor(out=KK, in0=AM1, in1=LIM, op=ALU.mult), [7], [3])
                ew("dve", lambda E: E.tensor_tensor(out=MAG, in0=MAG, in1=KK, op=ALU.subtract), [1, 3], [1])
                ew("dve", lambda E: E.tensor_tensor(out=MAG, in0=MAG, in1=DEN, op=ALU.mult), [1, 6], [1])
                BRE, BIM = bl_[:, 0, :], bl_[:, 1, :]
                ew("dve", lambda E: E.tensor_tensor(out=TH, in0=DT, in1=BRE, op=ALU.mult), [0], [2])
                ew("dve", lambda E: E.tensor_tensor(out=KK, in0=MAG, in1=BIM, op=ALU.mult), [1], [3])
                I("dve", lambda E: E.tensor_tensor(out=BbT[:, 0, :], in0=TH, in1=KK, op=ALU.subtract), reads=[r_tt[2], r_tt[3]], writes=[r_BbT])
                ew("dve", lambda E: E.tensor_tensor(out=TH, in0=DT, in1=BIM, op=ALU.mult), [0], [2])
                ew("dve", lambda E: E.tensor_tensor(out=KK, in0=MAG, in1=BRE, op=ALU.mult), [1], [3])
                I("dve", lambda E: E.tensor_tensor(out=BbT[:, 1, :], in0=TH, in1=KK, op=ALU.add), reads=[r_tt[2], r_tt[3]], writes=[r_BbT])
                I("act", lambda E: E.activation(out=colp[:, 2, :], in_=colp[:, 2, :], func=AF.Exp), reads=[r_colp], writes=[r_colp])
                I("dve", lambda E: E.tensor_tensor(out=rcol[:], in0=colp[:, 0, :], in1=colp[:, 2, :], op=ALU.mult), reads=[r_colp], writes=[r_rcol])
                I("act", lambda E: E.activation(out=rcol[:], in_=rcol[:], func=AF.Exp), reads=[r_rcol], writes=[r_rcol])
                I("dve", lambda E: E.tensor_tensor(out=thcol[:], in0=colp[:, 1, :], in1=colp[:, 2, :], op=ALU.mult), reads=[r_colp], writes=[r_thcol])
                fw.barrier()
                ar.release(m2b)
                NW = 512
                NWT = 520
                idx = ar.alloc("idx", [128, NWT], F32); r_idx = Reg()
                I("pool", lambda E: E.iota(idx[:], pattern=[[1, NWT]], base=0, channel_multiplier=0, allow_small_or_imprecise_dtypes=True), writes=[r_idx])
                def wb(name, w=NW):
                    return ar.alloc(name, [128, w], F32), Reg()
                ph0, r_ph0 = wb("ph0", NWT)
                NCb = [wb("NC0", NWT), wb("NC1", NWT)]; NSb = [wb("NS0", NWT), wb("NS1", NWT)]
                crot = ar.alloc("crot", [128, 4], F32); r_crot = Reg()
                upcount = [0]
                ma, r_ma = wb("ma"); mb_, r_mb = wb("mb"); mc, r_mc = wb("mc", NWT); md, r_md = wb("md")
                kk_, r_kk = mc, r_mc
                aph, r_aph = mc, r_mc
                VR, r_VR = wb("VR"); VI, r_VI = wb("VI")
                GRb = [wb("GR0"), wb("GR1")]; GIb = [wb("GI0"), wb("GI1")]
                HRt = ar.alloc("HRt", [128, NW], BF16); r_HR = Reg()
                HIt = ar.alloc("HIt", [128, NW], BF16); r_HI = Reg()
                yblk = ar.alloc("yblk", [128, L], F32); r_yblk = Reg()
                tiles = [(0, 256)] + [(CTX + i * 512, 512) for i in range(8)]
                gcount = [0]
                HRb = [(HRt, r_HR), (ar.alloc("HRt1", [128, NW], BF16), Reg())]
                HIb = [(HIt, r_HI), (ar.alloc("HIt1", [128, NW], BF16), Reg())]
                its = []
                for blk in range(4):
                    for jj in range(4):
                        for d in range(2):
                            for ti in range(NT):
                                its.append((blk, jj, d, ti))
                state = {"prev": None, "NC": None, "NS": None}

                def init_y(blk):
                    I("dve", lambda E, blk=blk: E.tensor_scalar(out=yblk[:], in0=us5T[:, blk, CTX:T], scalar1=dcol[:, blk:blk + 1], scalar2=None, op0=ALU.mult),
                      reads=r_us5T + [r_dcol], writes=[r_yblk])

                def emit_bu(it):
                    blk, jj, d, ti = it
                    j0, n = tiles[ti]
                    nti = ti if d == 0 else (0 if ti == 0 else 9 - ti)
                    s0, _ = tiles[nti]
                    bcol = (d * 4 + blk) * 128
                    pbr, rpbr = bank()
                    pbi, rpbi = bank()
                    for c, (pp, rpp) in enumerate(((pbr, rpbr), (pbi, rpbi))):
                        I("pe", lambda E, pp=pp, c=c, jj=jj, bcol=bcol, blk=blk, s0=s0, n=n: E.matmul(pp[:, :n], lhsT=BbT[32 * jj:32 * jj + 32, c, bcol:bcol + 128], rhs=us5T[32 * jj:32 * jj + 32, blk, s0:s0 + n],
                                                                                                   start=True, stop=True, tile_position=(32 * jj, 0)),
                          reads=[r_BbT, r_us5T[nti]], writes=[rpp])
                    return (pbr, rpbr, pbi, rpbi)

                def stage_a(i, bu):
                    blk, jj, d, ti = its[i]
                    j0, n = tiles[ti]
                    up = (d * 4 + blk) * 4 + jj
                    pbr, rpbr, pbi, rpbi = bu
                    if ti == 0:
                        state["prev"] = None
                        upcount[0] += 1
                        (NCt, r_NC), (NSt, r_NS) = NCb[upcount[0] % 2], NSb[upcount[0] % 2]
                        state["NC"], state["NS"] = (NCt, r_NC), (NSt, r_NS)
                        I("pool", lambda E, up=up: E.tensor_scalar(out=ph0[:, :], in0=idx[:, :], scalar1=thcol[:, up:up + 1], scalar2=None, op0=ALU.mult),
                          reads=[r_idx, r_thcol], writes=[r_ph0])
                        I("pool", lambda E: E.tensor_scalar(out=kk_[:, :], in0=ph0[:, :], scalar1=1.0 / TWO_PI, scalar2=MAGIC, op0=ALU.mult, op1=ALU.add), reads=[r_ph0], writes=[r_kk])
                        I("pool", lambda E: E.tensor_scalar(out=kk_[:, :], in0=kk_[:, :], scalar1=MAGIC, scalar2=-TWO_PI, op0=ALU.subtract, op1=ALU.mult), reads=[r_kk], writes=[r_kk])
                        I("pool", lambda E: E.tensor_tensor(out=ph0[:, :], in0=ph0[:, :], in1=kk_[:, :], op=ALU.add), reads=[r_ph0, r_kk], writes=[r_ph0])
                        I("act", lambda E, NSt=NSt: E.activation(out=NSt[:, :], in_=ph0[:, :], func=AF.Sin), reads=[r_ph0], writes=[r_NS])
                        I("act", lambda E: E.activation(out=aph[:, :], in_=ph0[:, :], func=AF.Abs), reads=[r_ph0], writes=[r_aph])
                        I("act", lambda E, NCt=NCt: E.activation(out=NCt[:, :], in_=aph[:, :], func=AF.Sin, bias=halfpi[:, 0:1], scale=-1.0), reads=[r_aph, r_halfpi], writes=[r_NC])
                    (NCt, r_NC), (NSt, r_NS) = state["NC"], state["NS"]
                    if d == 0:
                        BRv, BIv = pbr[:, :n], pbi[:, :n]
                    else:
                        BRv, BIv = pbr[:, :n][:, ::-1], pbi[:, :n][:, ::-1]
                    I("dve", lambda E, BRv=BRv, n=n, NCt=NCt: E.tensor_tensor(out=ma[:, :n], in0=BRv, in1=NCt[:, :n], op=ALU.mult), reads=[rpbr, r_NC], writes=[r_ma])
                    I("dve", lambda E, BIv=BIv, n=n, NSt=NSt: E.tensor_tensor(out=mb_[:, :n], in0=BIv, in1=NSt[:, :n], op=ALU.mult), reads=[rpbi, r_NS], writes=[r_mb])
                    I("dve", lambda E, n=n: E.tensor_tensor(out=VR[:, :n], in0=ma[:, :n], in1=mb_[:, :n], op=ALU.add), reads=[r_ma, r_mb], writes=[r_VR])
                    I("dve", lambda E, BIv=BIv, n=n, NCt=NCt: E.tensor_tensor(out=ma[:, :n], in0=BIv, in1=NCt[:, :n], op=ALU.mult), reads=[rpbi, r_NC], writes=[r_ma])
                    I("dve", lambda E, BRv=BRv, n=n, NSt=NSt: E.tensor_tensor(out=mb_[:, :n], in0=BRv, in1=NSt[:, :n], op=ALU.mult), reads=[rpbr, r_NS], writes=[r_mb])
                    I("dve", lambda E, n=n: E.tensor_tensor(out=VI[:, :n], in0=ma[:, :n], in1=mb_[:, :n], op=ALU.subtract), reads=[r_ma, r_mb], writes=[r_VI])
                    gcount[0] += 1
                    gb = gcount[0] % 2
                    (GR, r_GR), (GI, r_GI) = GRb[gb], GIb[gb]
                    prev = state["prev"]
                    if prev is None:
                        iR, iI, rds = 0.0, 0.0, []
                    else:
                        (pGR, r_pGR), (pGI, r_pGI), pn = prev
                        gRl, gIl = pGR[:, pn - 1:pn], pGI[:, pn - 1:pn]
                        cc_, ss_ = NCt[:, pn:pn + 1], NSt[:, pn:pn + 1]
                        I("dve", lambda E, gIl=gIl, ss_=ss_: E.tensor_scalar(out=crot[:, 0:1], in0=gIl, scalar1=ss_, scalar2=None, op0=ALU.mult), reads=[r_pGI, r_NS], writes=[r_crot])
                        I("dve", lambda E, gRl=gRl, cc_=cc_: E.scalar_tensor_tensor(out=crot[:, 1:2], in0=gRl, scalar=cc_, in1=crot[:, 0:1], op0=ALU.mult, op1=ALU.subtract), reads=[r_pGR, r_NC, r_crot], writes=[r_crot])
                        I("dve", lambda E, gIl=gIl, cc_=cc_: E.tensor_scalar(out=crot[:, 2:3], in0=gIl, scalar1=cc_, scalar2=None, op0=ALU.mult), reads=[r_pGI, r_NC], writes=[r_crot])
                        I("dve", lambda E, gRl=gRl, ss_=ss_: E.scalar_tensor_tensor(out=crot[:, 3:4], in0=gRl, scalar=ss_, in1=crot[:, 2:3], op0=ALU.mult, op1=ALU.add), reads=[r_pGR, r_NS, r_crot], writes=[r_crot])
                        iR, iI, rds = crot[:, 1:2], crot[:, 3:4], [r_crot]
                    I("dve", lambda E, GR=GR, n=n, up=up, iR=iR: E.tensor_tensor_scan(out=GR[:, :n], data0=rcol[:, up:up + 1].to_broadcast([128, n]), data1=VR[:, :n], initial=iR, op0=ALU.mult, op1=ALU.add),
                      reads=[r_VR, r_rcol] + rds, writes=[r_GR])
                    I("dve", lambda E, GI=GI, n=n, up=up, iI=iI: E.tensor_tensor_scan(out=GI[:, :n], data0=rcol[:, up:up + 1].to_broadcast([128, n]), data1=VI[:, :n], initial=iI, op0=ALU.mult, op1=ALU.add),
                      reads=[r_VI, r_rcol] + rds, writes=[r_GI])
                    state["prev"] = ((GR, r_GR), (GI, r_GI), n)
                    if ti == 0:
                        return None
                    (HRx, r_HRx), (HIx, r_HIx) = HRb[i % 2], HIb[i % 2]
                    I("dve", lambda E, GR=GR, n=n, NCt=NCt: E.tensor_tensor(out=mc[:, :n], in0=GR[:, :n], in1=NCt[:, :n], op=ALU.mult), reads=[r_GR, r_NC], writes=[r_mc])
                    I("dve", lambda E, GI=GI, n=n, NSt=NSt: E.tensor_tensor(out=md[:, :n], in0=GI[:, :n], in1=NSt[:, :n], op=ALU.mult), reads=[r_GI, r_NS], writes=[r_md])
                    I("dve", lambda E, n=n, HRx=HRx: E.tensor_tensor(out=HRx[:, :n], in0=mc[:, :n], in1=md[:, :n], op=ALU.subtract), reads=[r_mc, r_md], writes=[r_HRx])
                    I("pool", lambda E, GR=GR, n=n, NSt=NSt: E.tensor_tensor(out=ma[:, :n], in0=GR[:, :n], in1=NSt[:, :n], op=ALU.mult), reads=[r_GR, r_NS], writes=[r_ma])
                    I("pool", lambda E, GI=GI, n=n, NCt=NCt: E.tensor_tensor(out=mb_[:, :n], in0=GI[:, :n], in1=NCt[:, :n], op=ALU.mult), reads=[r_GI, r_NC], writes=[r_mb])
                    I("dve", lambda E, n=n, HIx=HIx: E.scalar_tensor_tensor(out=HIx[:, :n], in0=ma[:, :n], scalar=-1.0, in1=mb_[:, :n], op0=ALU.mult, op1=ALU.subtract), reads=[r_ma, r_mb], writes=[r_HIx])
                    return (HRx, r_HRx, HIx, r_HIx)

                def stage_b(i, hh_):
                    blk, jj, d, ti = its[i]
                    j0, n = tiles[ti]
                    up = (d * 4 + blk) * 4 + jj
                    if hh_ is not None:
                        HRx, r_HRx, HIx, r_HIx = hh_
                        py, rpy = bank()
                        I("pe", lambda E, py=py, up=up, n=n, HRx=HRx: E.matmul(py[:, :n], lhsT=CTb[:, 0, up, :], rhs=HRx[:, :n], start=True, stop=False), reads=[r_CTb, r_HRx], writes=[rpy], signal=False)
                        I("pe", lambda E, py=py, up=up, n=n, HIx=HIx: E.matmul(py[:, :n], lhsT=CTb[:, 1, up, :], rhs=HIx[:, :n], start=False, stop=True), reads=[r_CTb, r_HIx], writes=[rpy])
                        if d == 0:
                            yv = yblk[:, j0 - CTX:j0 - CTX + n]
                        else:
                            lo_ = L - (j0 - CTX) - n
                            yv = yblk[:, lo_:lo_ + n][:, ::-1]
                        I("dve", lambda E, py=py, yv=yv, n=n: E.tensor_tensor(out=yv, in0=yv, in1=py[:, :n], op=ALU.add), reads=[rpy, r_yblk], writes=[r_yblk])
                    if jj == 3 and d == 1 and ti == NT - 1:
                        for cc in range(8):
                            c0 = cc * 512
                            yv = yblk[:, c0:c0 + 512]
                            I("dve", lambda E, yv=yv: E.tensor_tensor(out=mc[:, 0:512], in0=yv, in1=yv, op=ALU.mult), reads=[r_yblk], writes=[r_mc])
                            I("dve", lambda E: E.tensor_scalar(out=mc[:, 0:512], in0=mc[:, 0:512], scalar1=0.044715 * 1.5957691216, scalar2=1.5957691216, op0=ALU.mult, op1=ALU.add), reads=[r_mc], writes=[r_mc])
                            I("dve", lambda E, yv=yv: E.tensor_tensor(out=mc[:, 0:512], in0=mc[:, 0:512], in1=yv, op=ALU.mult), reads=[r_mc, r_yblk], writes=[r_mc])
                            I("act", lambda E: E.activation(out=md[:], in_=mc[:, 0:512], func=AF.Sigmoid), reads=[r_mc], writes=[r_md])
                            I("dve", lambda E, yv=yv, blk=blk, c0=c0: E.tensor_tensor(out=us5T[:, blk, CTX + c0:CTX + c0 + 512], in0=md[:], in1=yv, op=ALU.mult),
                              reads=[r_md, r_yblk], writes=[r_us5T[1 + cc]])
                        if blk < 3:
                            init_y(blk + 1)

                init_y(0)
                NI = len(its)
                bus = {0: emit_bu(its[0])}
                hres = {}
                for i in range(NI):
                    if i + 1 < NI:
                        bus[i + 1] = emit_bu(its[i + 1])
                    hres[i] = stage_a(i, bus.pop(i))
                    if i >= 1:
                        stage_b(i - 1, hres.pop(i - 1))
                stage_b(NI - 1, hres.pop(NI - 1))
                for tl in range(8):
                    c0 = CTX + tl * 512
                    for nb in range(4):
                        pg, rpg = bank()
                        for k in range(4):
                            I("pe", lambda E, pg=pg, k=k, nb=nb, c0=c0: E.matmul(pg[:, :], lhsT=wglub[:, k, nb * 128:(nb + 1) * 128], rhs=us5T[:, k, c0:c0 + 512], start=(k == 0), stop=(k == 3)),
                              reads=[r_wglub, r_us5T[1 + tl]], writes=[rpg], signal=(k == 3))
                        I("act", lambda E, pg=pg, nb=nb: E.activation(out=ma[:], in_=pg[:, :], func=AF.Sigmoid, bias=bgl[:, nb:nb + 1], scale=1.0), reads=[rpg, r_bgl], writes=[r_ma])
                        I("dve", lambda E, nb=nb, tl=tl, c0=c0: E.tensor_tensor(out=s5T[:, nb, tl * 512:(tl + 1) * 512], in0=ma[:], in1=us5T[:, nb, c0:c0 + 512], op=ALU.mult),
                          reads=[r_ma, r_us5T[1 + tl]], writes=[r_s5T[tl]])
                dump("s5T", s5T[:, :, 0:512], r_s5T[0], [128, 4, 512])
                fw.barrier()
                ar.release(m2)
            ar.release(mB)
            if stop_after >= 3:
                attT = ar.alloc("attT", [128, 4, L], BF16, top=True); r_attT = [Reg() for _ in range(8)]
                m3 = ar.mark()
                g_c3 = fw.group("const3", final=True)
                qgt = ar.alloc("qgt", [128, 3], F32); r_qgt = Reg()
                kvgt = ar.alloc("kvgt", [128, 2], F32); r_kvgt = Reg()
                Dm("sp", g_c3, lambda E: E.dma_start(out=qgt[:], in_=qg_d[:, :]), writes=[r_qgt])
                Dm("sp", g_c3, lambda E: E.dma_start(out=kvgt[:], in_=kvg_d[:, :]), writes=[r_kvgt])
                eselb = ar.alloc("eselb", [32, 96], BF16); r_eselb = Reg()
                Dm("pool", g_c3, lambda E: E.dma_start(out=eselb[:], in_=esel_d[:, :]), writes=[r_eselb])
                wuqb = ar.alloc("wuqb", [128, 3, NH * 192], BF16); r_wuqb = Reg()
                wukb = ar.alloc("wukb", [128, 2, NH * 96], BF16); r_wukb = Reg()
                wuvb = ar.alloc("wuvb", [128, 2, NH * 64], BF16); r_wuvb = Reg()
                stg = ar.alloc("stg", [128, NH * 192], F32); r_stg = Reg()
                g_stg = fw.group("stg")
                wuq_v = wuq_d.rearrange("(k p) n -> p k n", p=128)
                wuk_v = wuk_d.rearrange("(k p) n -> p k n", p=128)
                wuv_v = wuv_d.rearrange("(k p) n -> p k n", p=128)
                for k in range(3):
                    for hh in range(2):
                        Dm("sp", g_stg, lambda E, k=k, hh=hh: E.dma_start(out=stg[:, hh * 768:(hh + 1) * 768], in_=wuq_v[:, k, hh * 768:(hh + 1) * 768]), writes=[r_stg])
                    I("dve", lambda E, k=k: E.tensor_scalar(out=wuqb[:, k, :], in0=stg[:, :], scalar1=qgt[:, k:k + 1], scalar2=SCALE, op0=ALU.mult, op1=ALU.mult),
                      reads=[r_stg, r_qgt], writes=[r_wuqb])
                for k in range(2):
                    Dm("sp", g_stg, lambda E, k=k: E.dma_start(out=stg[:, 0:NH * 96], in_=wuk_v[:, k, :]), writes=[r_stg])
                    I("dve", lambda E, k=k: E.tensor_scalar(out=wukb[:, k, :], in0=stg[:, 0:NH * 96], scalar1=kvgt[:, k:k + 1], scalar2=None, op0=ALU.mult),
                      reads=[r_stg, r_kvgt], writes=[r_wukb])
                for k in range(2):
                    Dm("sp", g_stg, lambda E, k=k: E.dma_start(out=stg[:, 0:NH * 64], in_=wuv_v[:, k, :]), writes=[r_stg])
                    I("dve", lambda E, k=k: E.tensor_scalar(out=wuvb[:, k, :], in0=stg[:, 0:NH * 64], scalar1=kvgt[:, k:k + 1], scalar2=None, op0=ALU.mult),
                      reads=[r_stg, r_kvgt], writes=[r_wuvb])
                cut(30)
                KhT = ar.alloc("KhT", [96, T], BF16); r_KhT = Reg()
                QhT = ar.alloc("QhT", [96, L], BF16); r_QhT = Reg()
                Vaug = [ar.alloc("Vaug%d" % i, [128, 34, 128], BF16) for i in range(2)]; r_Vaug = [Reg(), Reg()]
                I("pool", lambda E: E.memset(Vaug[0][:, :, 64:128], 1.0), writes=[r_Vaug[0]])
                I("pool", lambda E: E.memset(Vaug[1][:, :, 0:64], 1.0), writes=[r_Vaug[1]])
                PT = [ar.alloc("PT%d" % i, [128, 512], BF16) for i in range(3)]; r_PT = [Reg() for _ in range(3)]
                Oun = ar.alloc("Oun", [128, 512], F32); r_Oun = Reg()
                Dsh = ar.alloc("Dsh", [128, 512], F32); r_Dsh = Reg()
                g_dsh = fw.group("dsh")
                ropeq = ar.alloc("ropeq", [96, 2, 512], F32); r_ropeq = Reg()
                g_ropeq = fw.group("ropeq")
                tq = ar.alloc("tq", [96, 512], F32); r_tq = Reg()
                Sb = [(pbank[i], r_pb[i]) for i in (0, 1, 2)]
                Ob = [(pbank[i], r_pb[i]) for i in (3, 4)]
                Mb = [(pbank[i], r_pb[i]) for i in (5, 6)]
                mcnt = [0]
                scnt = [0]
                ocnt = [0]

                def mbank():
                    mcnt[0] += 1
                    return Mb[mcnt[0] % 2]

                for h in [int(v) for v in os.environ.get('HEADS', '0,1,2,3,4,5,6,7').split(',')]:
                    par = h % 2
                    for ti in range(NT):
                        n = 256 if ti == 0 else 512
                        t0 = 0 if ti == 0 else CTX + (ti - 1) * 512
                        pb, rpb = mbank()
                        for k in range(2):
                            I("pe", lambda E, pb=pb, k=k, h=h, n=n, t0=t0: E.matmul(pb[0:96, :n], lhsT=wukb[:, k, h * 96:(h + 1) * 96], rhs=kvcT[:, k, t0:t0 + n], start=(k == 0), stop=False),
                              reads=[r_wukb, r_kvcT[ti]], writes=[rpb], signal=False)
                        I("pe", lambda E, pb=pb, n=n, t0=t0: E.matmul(pb[0:96, :n], lhsT=eselb[0:32, :], rhs=krT[0:32, t0:t0 + n], start=False, stop=True),
                          reads=[r_eselb, r_krT[ti]], writes=[rpb])
                        I("dve", lambda E, pb=pb, n=n, t0=t0: E.tensor_copy(out=KhT[0:96, t0:t0 + n], in_=pb[0:96, :n]), reads=[rpb], writes=[r_KhT])
                    cut(31)
                    for qt in range(8):
                        l0 = qt * 512
                        pa, rpa = mbank()
                        for k in range(3):
                            I("pe", lambda E, pa=pa, k=k, h=h, l0=l0: E.matmul(pa[0:96, :], lhsT=wuqb[:, k, h * 192:h * 192 + 96], rhs=qcT[:, k, l0:l0 + 512], start=(k == 0), stop=(k == 2)),
                              reads=[r_wuqb, r_qcT[qt + 1]], writes=[rpa], signal=(k == 2))
                        pbb, rpbb = mbank()
                        for k in range(3):
                            I("pe", lambda E, pbb=pbb, k=k, h=h, l0=l0: E.matmul(pbb[0:96, :], lhsT=wuqb[:, k, h * 192 + 96:h * 192 + 192], rhs=qcT[:, k, l0:l0 + 512], start=(k == 0), stop=(k == 2)),
                              reads=[r_wuqb, r_qcT[qt + 1]], writes=[rpbb], signal=(k == 2))
                        Dm("sp", g_ropeq, lambda E, l0=l0: E.dma_start(out=ropeq[64:96, :, :], in_=rope_d[:, :, l0:l0 + 512]), writes=[r_ropeq])
                        I("dve", lambda E, pa=pa, l0=l0: E.tensor_copy(out=QhT[0:64, l0:l0 + 512], in_=pa[0:64, :]), reads=[rpa], writes=[r_QhT])
                        I("dve", lambda E, pa=pa: E.tensor_tensor(out=tq[64:96, :], in0=pa[64:96, :], in1=ropeq[64:96, 0, :], op=ALU.mult), reads=[rpa, r_ropeq], writes=[r_tq])
                        I("dve", lambda E, pbb=pbb: E.tensor_tensor(out=ropeq[64:96, 1, :], in0=pbb[64:96, :], in1=ropeq[64:96, 1, :], op=ALU.mult), reads=[rpbb, r_ropeq], writes=[r_ropeq])
                        I("dve", lambda E, l0=l0: E.tensor_tensor(out=QhT[64:96, l0:l0 + 512], in0=tq[64:96, :], in1=ropeq[64:96, 1, :], op=ALU.add), reads=[r_tq, r_ropeq], writes=[r_QhT])
                    cut(32)
                    va, r_va = Vaug[par], r_Vaug[par]
                    voff = 0 if par == 0 else 64
                    for kb in range(5):
                        nk = 8 if kb < 4 else 2
                        pb, rpb = mbank()
                        for j in range(nk):
                            kt = kb * 8 + j
                            ti = 0 if kt < 2 else 1 + (kt - 2) // 4
                            for k in range(2):
                                I("pe", lambda E, pb=pb, j=j, k=k, kt=kt, h=h: E.matmul(pb[:, j * 64:(j + 1) * 64], lhsT=kvcT[:, k, kt * 128:(kt + 1) * 128], rhs=wuvb[:, k, h * 64:(h + 1) * 64], start=(k == 0), stop=(k == 1)),
                                  reads=[r_wuvb, r_kvcT[ti]], writes=[rpb], signal=(j == nk - 1 and k == 1))
                        I("dve", lambda E, pb=pb, kb=kb, nk=nk, va=va, voff=voff: E.tensor_copy(out=va[:, kb * 8:kb * 8 + nk, voff:voff + 64], in_=pb[:, 0:nk * 64].rearrange("p (j d) -> p j d", d=64)),
                          reads=[rpb], writes=[r_va])
                    cut(33)
                    pend = []

                    def emit_pv(it, h=h, par=par, va=va, r_va=r_va):
                        qt, kt, pt, rpt, po, rpo = it
                        l0 = qt * 512
                        I("pe", lambda E, po=po, pt=pt, kt=kt, va=va: E.matmul(po[:, :], lhsT=va[:, kt, :], rhs=pt[:, :], start=(kt == 0), stop=(kt == 33)),
                          reads=[r_va, rpt], writes=[rpo], signal=(kt == 33))
                        if kt != 33:
                            return
                        I("dve", lambda E, po=po: E.tensor_copy(out=Oun[:, :], in_=po[:, :]), reads=[rpo], writes=[r_Oun])
                        if par == 0:
                            Dm("sp", g_dsh, lambda E: E.dma_start(out=Dsh[0:64, :], in_=Oun[64:128, :]), reads=[r_Oun], writes=[r_Dsh])
                            lo_, hi_ = 0, 64
                        else:
                            Dm("sp", g_dsh, lambda E: E.dma_start(out=Dsh[64:128, :], in_=Oun[0:64, :]), reads=[r_Oun], writes=[r_Dsh])
                            lo_, hi_ = 64, 128
                        I("dve", lambda E, lo_=lo_, hi_=hi_: E.reciprocal(out=Dsh[lo_:hi_, :], in_=Dsh[lo_:hi_, :]), reads=[r_Dsh], writes=[r_Dsh])
                        I("dve", lambda E, lo_=lo_, hi_=hi_, h=h, l0=l0: E.tensor_tensor(out=attT[lo_:hi_, h // 2, l0:l0 + 512], in0=Oun[lo_:hi_, :], in1=Dsh[lo_:hi_, :], op=ALU.mult),
                          reads=[r_Oun, r_Dsh], writes=[r_attT[qt]])

                    for qt in range(8):
                        l0 = qt * 512
                        ocnt[0] += 1
                        po, rpo = Ob[ocnt[0] % 2]
                        for kt in range(34):
                            scnt[0] += 1
                            ps_, rps = Sb[scnt[0] % 3]
                            pt, rpt = PT[scnt[0] % 3], r_PT[scnt[0] % 3]
                            I("pe", lambda E, ps_=ps_, kt=kt, l0=l0: E.matmul(ps_[:, :], lhsT=KhT[0:96, kt * 128:(kt + 1) * 128], rhs=QhT[0:96, l0:l0 + 512], start=True, stop=True),
                              reads=[r_KhT, r_QhT], writes=[rps])
                            I("act", lambda E, ps_=ps_, pt=pt: E.activation(out=pt[:, :], in_=ps_[:, :], func=AF.Exp), reads=[rps], writes=[rpt])
                            pend.append((qt, kt, pt, rpt, po, rpo))
                            if len(pend) > 2:
                                emit_pv(pend.pop(0))
                    while pend:
                        emit_pv(pend.pop(0))
                dump("attT", attT[:, :, 0:512], r_attT[0], [128, 4, 512])
                fw.barrier()
                ar.release(m3)
            if stop_after >= 4:
                ar.release(mA)
                m4 = ar.mark()
                g_c4 = fw.group("const4", final=True)
                woutb = ar.alloc("woutb", [128, 8, D], BF16); r_woutb = Reg()
                wout_v = wout_d.rearrange("(k p) n -> p k n", p=128)
                for k in range(8):
                    Dm("pool", g_c4, lambda E, k=k: E.dma_start(out=woutb[:, k, :], in_=wout_v[:, k, :]), writes=[r_woutb])
                lnt = ar.alloc("lnt", [128, 4, D], F32); r_lnt = Reg()
                for i in range(4):
                    Dm("sp", g_c4, lambda E, i=i: E.dma_start(out=lnt[:, i, :], in_=lnp_d[i:i + 1, :].partition_broadcast(128)), writes=[r_lnt])
                xt2 = ar.alloc("xt2", [128, D], F32); r_xt2 = Reg(); g_xt2 = fw.group("xt2")
                r1 = ar.alloc("r1", [128, D], F32); r_r1 = Reg(); g_out = fw.group("outst")
                x1t = ar.alloc("x1t", [128, 4, D], F32); r_x1t = [Reg() for _ in range(4)]
                xh2 = ar.alloc("xh2", [128, D], BF16); r_xh2 = Reg()
                u2T = ar.alloc("u2T", [128, 8, 512], BF16); r_u2T = Reg()
                wgub = [ar.alloc("wgub%d" % i, [128, 2, 8, 128], BF16) for i in range(2)]; r_wgub = [Reg(), Reg()]
                g_wgu = [fw.group("wgu0"), fw.group("wgu1")]
                tmpa = ar.alloc("tmpa", [128, 512], F32); r_tmpa = Reg()
                tmpb = ar.alloc("tmpb", [128, 512], F32); r_tmpb = Reg()
                aT = ar.alloc("aT", [128, 22, 512], BF16); r_aT = Reg()
                wdnh = ar.alloc("wdnh", [128, 22, 512], BF16); r_wdnh = Reg(); g_wdn = fw.group("wdn")
                st6b = ar.alloc("st6b", [128, 2, 6], F32); r_st6b = Reg()
                mvb = ar.alloc("mvb", [128, 2], F32); r_mvb = Reg()
                rstdb = ar.alloc("rstdb", [128, 1], F32); r_rstdb = Reg()
                nbiasb = ar.alloc("nbiasb", [128, 1], F32); r_nbiasb = Reg()
                wgu_v = wgu_d.rearrange("(k p) n -> p k n", p=128)
                wdn_v = wdn_d.rearrange("(f p) n -> p f n", p=128)

                def ln_stats(src, r_src):
                    for hh in range(2):
                        I("dve", lambda E, hh=hh, src=src: E.bn_stats(out=st6b[:, hh, :], in_=src[:, hh * 512:(hh + 1) * 512]), reads=[r_src], writes=[r_st6b])
                    I("dve", lambda E: E.bn_aggr(out=mvb[:], in_=st6b[:].rearrange("p a b -> p (a b)")), reads=[r_st6b], writes=[r_mvb])
                    I("act", lambda E: E.activation(out=rstdb[:], in_=mvb[:, 1:2], func=AF.Sqrt, bias=epsc[:, 0:1], scale=1.0), reads=[r_mvb, r_epsc], writes=[r_rstdb])
                    I("dve", lambda E: E.reciprocal(out=rstdb[:], in_=rstdb[:]), reads=[r_rstdb], writes=[r_rstdb])
                    I("dve", lambda E: E.scalar_tensor_tensor(out=nbiasb[:], in0=mvb[:, 0:1], scalar=-1.0, in1=rstdb[:], op0=ALU.mult, op1=ALU.mult),
                      reads=[r_mvb, r_rstdb], writes=[r_nbiasb])

                fcount = [0]
                for tl in range(8):
                    l0 = tl * 512
                    for s in range(4):
                        ls = l0 + s * 128
                        ybanks = []
                        for hh in range(2):
                            pb, rpb = bank()
                            for k in range(8):
                                src_t = attT if k < 4 else s5T
                                rr = r_attT[tl] if k < 4 else r_s5T[tl]
                                I("pe", lambda E, pb=pb, k=k, hh=hh, ls=ls, src_t=src_t: E.matmul(pb[:, :], lhsT=src_t[:, k % 4, ls:ls + 128], rhs=woutb[:, k, hh * 512:(hh + 1) * 512], start=(k == 0), stop=(k == 7)),
                                  reads=[rr, r_woutb], writes=[rpb], signal=(k == 7))
                            ybanks.append((pb, rpb))
                        Dm("sp", g_xt2, lambda E, ls=ls: E.dma_start(out=xt2[:], in_=x_d[ls:ls + 128, :]), writes=[r_xt2])
                        for hh in range(2):
                            pb, rpb = ybanks[hh]
                            I("dve", lambda E, pb=pb, hh=hh: E.tensor_tensor(out=r1[:, hh * 512:(hh + 1) * 512], in0=pb[:, :], in1=g12b[:, 0, hh * 512:(hh + 1) * 512], op=ALU.mult),
                              reads=[rpb, r_g12b], writes=[r_r1])
                        I("dve", lambda E: E.scalar_tensor_tensor(out=r1[:], in0=xt2[:], scalar=ALPHA, in1=r1[:], op0=ALU.mult, op1=ALU.add), reads=[r_xt2, r_r1], writes=[r_r1])
                        ln_stats(r1, r_r1)
                        I("act", lambda E, s=s: E.activation(out=x1t[:, s, :], in_=r1[:], func=AF.Identity, bias=nbiasb[:, 0:1], scale=rstdb[:, 0:1]),
                          reads=[r_r1, r_rstdb, r_nbiasb], writes=[r_x1t[s]])
                        I("pool", lambda E, s=s: E.tensor_tensor(out=x1t[:, s, :], in0=x1t[:, s, :], in1=lnt[:, 0, :], op=ALU.mult), reads=[r_x1t[s], r_lnt], writes=[r_x1t[s]])
                        I("pool", lambda E, s=s: E.tensor_tensor(out=x1t[:, s, :], in0=x1t[:, s, :], in1=lnt[:, 1, :], op=ALU.add), reads=[r_x1t[s], r_lnt], writes=[r_x1t[s]])
                        ln_stats(x1t[:, s, :], r_x1t[s])
                        I("act", lambda E, s=s: E.activation(out=xh2[:], in_=x1t[:, s, :], func=AF.Identity, bias=nbiasb[:, 0:1], scale=rstdb[:, 0:1]),
                          reads=[r_x1t[s], r_rstdb, r_nbiasb], writes=[r_xh2])
                        for k in range(8):
                            I("pe", lambda E, k=k: E.transpose(out=ptr[:, k * 128:(k + 1) * 128], in_=xh2[:, k * 128:(k + 1) * 128], identity=ident[:, :]),
                              reads=[r_xh2, r_ident], writes=[r_ptr], signal=(k == 7))
                        for k in range(8):
                            if k % 2 == 0:
                                I("act", lambda E, k=k, s=s: E.activation(out=u2T[:, k, s * 128:(s + 1) * 128], in_=ptr[:, k * 128:(k + 1) * 128], func=AF.Identity,
                                                                       bias=modcol[:, k, 2:3], scale=modcol[:, k, 3:4]), reads=[r_ptr, r_modcol], writes=[r_u2T])
                            else:
                                I("dve", lambda E, k=k, s=s: E.tensor_scalar(out=u2T[:, k, s * 128:(s + 1) * 128], in0=ptr[:, k * 128:(k + 1) * 128],
                                                                          scalar1=modcol[:, k, 3:4], scalar2=modcol[:, k, 2:3], op0=ALU.mult, op1=ALU.add),
                                  reads=[r_ptr, r_modcol], writes=[r_u2T])
                    for f in range(22):
                        fcount[0] += 1
                        wb = fcount[0] % 2
                        for gu in range(2):
                            c = gu * 22 + f
                            Dm("sp", g_wgu[wb], lambda E, wb=wb, gu=gu, c=c: E.dma_start(out=wgub[wb][:, gu, :, :].rearrange("p k n -> p (k n)"), in_=wgu_bf[c]), reads=[r_wgubf], writes=[r_wgub[wb]])
                        pg, rpg = bank()
                        pu, rpu = bank()
                        for gu, (pp, rpp) in enumerate(((pg, rpg), (pu, rpu))):
                            for k in range(8):
                                I("pe", lambda E, pp=pp, k=k, gu=gu, wb=wb: E.matmul(pp[:, :], lhsT=wgub[wb][:, gu, k, :], rhs=u2T[:, k, :], start=(k == 0), stop=(k == 7)),
                                  reads=[r_wgub[wb], r_u2T], writes=[rpp], signal=(k == 7))
                        I("act", lambda E, pg=pg: E.activation(out=tmpa[:], in_=pg[:, :], func=AF.Silu), reads=[rpg], writes=[r_tmpa])
                        I("dve", lambda E, pu=pu, f=f: E.tensor_tensor(out=aT[:, f, :], in0=tmpa[:], in1=pu[:, :], op=ALU.mult), reads=[r_tmpa, rpu], writes=[r_aT])
                    for hh in range(2):
                        Dm("sp", g_wdn, lambda E, hh=hh: E.dma_start(out=wdnh[:].rearrange("p f n -> p (f n)"), in_=wdn_bf[hh]), reads=[r_wdnbf], writes=[r_wdnh])
                        for s in range(4):
                            pb, rpb = bank()
                            for f in range(22):
                                I("pe", lambda E, pb=pb, f=f, s=s: E.matmul(pb[:, :], lhsT=aT[:, f, s * 128:(s + 1) * 128], rhs=wdnh[:, f, :], start=(f == 0), stop=(f == 21)),
                                  reads=[r_aT, r_wdnh], writes=[rpb], signal=(f == 21))
                            I("dve", lambda E, pb=pb, hh=hh: E.tensor_tensor(out=tmpb[:], in0=pb[:, :], in1=g12b[:, 1, hh * 512:(hh + 1) * 512], op=ALU.mult), reads=[rpb, r_g12b], writes=[r_tmpb])
                            I("dve", lambda E, s=s, hh=hh: E.scalar_tensor_tensor(out=x1t[:, s, hh * 512:(hh + 1) * 512], in0=x1t[:, s, hh * 512:(hh + 1) * 512], scalar=ALPHA, in1=tmpb[:],
                                                                                op0=ALU.mult, op1=ALU.add), reads=[r_x1t[s], r_tmpb], writes=[r_x1t[s]])
                    for s in range(4):
                        ls = l0 + s * 128
                        ln_stats(x1t[:, s, :], r_x1t[s])
                        I("act", lambda E, s=s: E.activation(out=r1[:], in_=x1t[:, s, :], func=AF.Identity, bias=nbiasb[:, 0:1], scale=rstdb[:, 0:1]),
                          reads=[r_x1t[s], r_rstdb, r_nbiasb], writes=[r_r1])
                        I("pool", lambda E: E.tensor_tensor(out=r1[:], in0=r1[:], in1=lnt[:, 2, :], op=ALU.mult), reads=[r_r1, r_lnt], writes=[r_r1])
                        I("pool", lambda E: E.tensor_tensor(out=r1[:], in0=r1[:], in1=lnt[:, 3, :], op=ALU.add), reads=[r_r1, r_lnt], writes=[r_r1])
                        Dm("sp", g_out, lambda E, ls=ls: E.dma_start(out=out_d[ls:ls + 128, :], in_=r1[:]), reads=[r_r1])
                fw.barrier()
        except StopBuild:
            pass
        fw.barrier()
        fw.emit()
    return nc, used_inputs


def _rope_tables():
    rows = L // 64
    row = np.repeat(np.arange(rows), 64).astype(np.float32)
    col = np.tile(np.arange(64), rows).astype(np.float32)
    n_freq = 8
    freqs = (10000.0 ** (-np.arange(n_freq, dtype=np.float32) / n_freq)).astype(np.float32)
    ang = np.concatenate([row[:, None] * freqs, col[:, None] * freqs], axis=-1).astype(np.float32)
    cos = np.cos(ang).astype(np.float32)
    sin = np.sin(ang).astype(np.float32)
    t = np.zeros((32, 2, L), np.float32)
    for j in range(32):
        t[j, 0] = cos[:, j // 2]
        t[j, 1] = sin[:, j // 2] * (-1.0 if j % 2 == 0 else 1.0)
    return t


def host_shared(inp):
    sh = {}
    f = np.float32
    sh["w_ada"] = np.ascontiguousarray(inp["w_ada"][0], f)
    sh["b_ada"] = np.ascontiguousarray(inp["b_ada"][0][None, :], f)
    w_in = inp["w_in"][0]
    kr = w_in[:, 640:672]
    kr_sw = kr.reshape(D, 16, 2)[:, :, ::-1].reshape(D, 32)
    sh["w_in_l"] = np.ascontiguousarray(np.concatenate([w_in[:, :672], kr_sw, w_in[:, 672:], np.zeros((D, 64), f)], axis=1), f)
    wuq = inp["w_uq"][0]
    rope_sw = wuq[:, :, 64:96].reshape(QL, NH, 16, 2)[:, :, :, ::-1].reshape(QL, NH, 32)
    sh["w_uq_l"] = np.ascontiguousarray(np.concatenate([wuq, wuq[:, :, :64], rope_sw], axis=2).reshape(QL, NH * 192), f)
    wuk = inp["w_uk"][0]
    sh["w_uk_l"] = np.ascontiguousarray(np.concatenate([wuk, np.zeros((KVL, NH, 32), f)], axis=2).reshape(KVL, NH * 96), f)
    sh["w_uv_l"] = np.ascontiguousarray(inp["w_uv"][0].reshape(KVL, NH * 64), f)
    sh["qg_l"] = np.ascontiguousarray(inp["q_norm_g"][0].reshape(3, 128).T, f)
    sh["kvg_l"] = np.ascontiguousarray(inp["kv_norm_g"][0].reshape(2, 128).T, f)
    sh["rope_cs"] = _rope_tables()
    es_ = np.zeros((32, 96), f)
    es_[np.arange(32), 64 + np.arange(32)] = 1.0
    sh["esel"] = es_
    sh["ident"] = np.eye(128, dtype=f)
    lre, lim, ldt = inp["s5_lambda_re"][0], inp["s5_lambda_im"][0], inp["s5_log_dt"][0]
    bre, bim = inp["s5_b_re"][0], inp["s5_b_im"][0]
    cre, cim = inp["s5_c_re"][0], inp["s5_c_im"][0]
    col = np.zeros((128, 3, 32), f)
    rowl = np.zeros((128, 3, 1024), f)
    bl = np.zeros((128, 2, 1024), f)
    cl = np.zeros((128, 2, 32, 128), f)
    for d in range(2):
        for blk in range(4):
            for jj in range(4):
                up = (d * 4 + blk) * 4 + jj
                for gl in range(2):
                    g = blk * 8 + 2 * jj + gl
                    col[gl * 64:(gl + 1) * 64, 0, up] = lre[d, g]
                    col[gl * 64:(gl + 1) * 64, 1, up] = lim[d, g]
                    col[gl * 64:(gl + 1) * 64, 2, up] = ldt[d, g]
                    c0 = (d * 4 + blk) * 128 + gl * 64
                    rowl[32 * jj:32 * jj + 32, 0, c0:c0 + 64] = lre[d, g][None, :]
                    rowl[32 * jj:32 * jj + 32, 1, c0:c0 + 64] = lim[d, g][None, :]
                    rowl[32 * jj:32 * jj + 32, 2, c0:c0 + 64] = ldt[d, g]
                    p0 = 32 * jj + 16 * gl
                    bl[p0:p0 + 16, 0, c0:c0 + 64] = bre[d, g].T
                    bl[p0:p0 + 16, 1, c0:c0 + 64] = bim[d, g].T
                    m0 = 32 * jj + 16 * gl
                    cl[gl * 64:(gl + 1) * 64, 0, up, m0:m0 + 16] = cre[d, g].T
                    cl[gl * 64:(gl + 1) * 64, 1, up, m0:m0 + 16] = cim[d, g].T
    sh["s5_col"], sh["s5_row"], sh["s5_b_l"], sh["s5_c_l"] = col, rowl, bl, cl
    sh["s5_d_col"] = np.ascontiguousarray(inp["s5_d"][0].reshape(4, 128).T, f)
    sh["s5_w_glu"] = np.ascontiguousarray(inp["s5_w_glu"][0], f)
    sh["b_glu_col"] = np.ascontiguousarray(inp["s5_b_glu"][0].reshape(4, 128).T, f)
    sh["w_out"] = np.ascontiguousarray(inp["w_out"][0], f)
    sh["ln_params"] = np.ascontiguousarray(np.stack([inp["ln1_g"][0], inp["ln1_b"][0], inp["ln2_g"][0], inp["ln2_b"][0]]), f)
    sh["w_gate_up"] = np.ascontiguousarray(inp["w_gate_up"][0], f)
    sh["w_down"] = np.ascontiguousarray(inp["w_down"][0], f)
    return sh


def host_core(inp, b, sh):
    m = dict(sh)
    m["x"] = np.ascontiguousarray(inp["x"][b], np.float32)
    m["ctx"] = np.ascontiguousarray(inp["ctx"][b], np.float32)
    cc = np.stack([inp["c"][b], inp["c_ctx"]], axis=-1).astype(np.float32)
    m["cc"] = np.ascontiguousarray(cc.reshape(8, 128, 2).transpose(1, 0, 2).reshape(128, 16))
    return m


_NC = {}


def kernel(**inputs):
    inp = {k: np.asarray(v) for k, v in inputs.items()}
    if "nc" not in _NC:
        _NC["nc"], _NC["names"] = build()
    nc, names = _NC["nc"], set(_NC["names"])
    sh = host_shared(inp)
    in_maps = [{k: v for k, v in host_core(inp, b, sh).items() if k in names} for b in range(8)]
    res = run_bass_kernel_spmd(nc, in_maps, core_ids=list(range(8)))
    return np.stack([np.asarray(r["out"], np.float32) for r in res.results], axis=0)
```

```python
import numpy as np
import concourse.bass as bass
import concourse.mybir as mybir
from concourse.bass_utils import run_bass_kernel_spmd

F32 = mybir.dt.float32
BF16 = mybir.dt.bfloat16
AF = mybir.ActivationFunctionType
ALU = mybir.AluOpType

D = 1024
L = 4096
CTX = 256
T = CTX + L
NH = 8
QL = 384
KVL = 256
DIN = 1184
DFF = 2816
EPS = 1e-6
ALPHA = 2.0 ** 0.25
SCALE = 96.0 ** -0.5
MAGIC = 12582912.0
TWO_PI = float(2 * np.pi)
PI_LO = 3.1415925
NT = 9
SEG = 1088
NSEG = T // SEG


class Reg:
    __slots__ = ("w", "r")

    def __init__(self):
        self.w = None
        self.r = {}


class DmaGroup:
    def __init__(self, fw, name, final=False):
        self.fw = fw
        self.name = name
        self.sems = [fw.nc.alloc_semaphore(name="dg_" + name)]
        self.count = 0
        self.final = final

    @property
    def ep(self):
        return len(self.sems) - 1

    def roll(self):
        self.sems.append(self.fw.nc.alloc_semaphore(name="dg_%s_%d" % (self.name, len(self.sems))))
        self.count = 0


class FW:
    ENG = ("pe", "dve", "act", "pool", "sp")
    LIMIT = 900
    DLIMIT = 40

    def __init__(self, nc):
        self.nc = nc
        self.sems = {k: [nc.alloc_semaphore(name="es_" + k)] for k in self.ENG}
        self.cnt = {k: 0 for k in self.ENG}
        self.pending = {k: False for k in self.ENG}
        self.seen = {k: {} for k in self.ENG}
        self.prog = {k: [] for k in self.ENG}
        self.groups = []
        self.trace = {k: [] for k in self.ENG}

    def ep(self, e):
        return len(self.sems[e]) - 1

    def group(self, name, final=False):
        g = DmaGroup(self, name, final)
        self.groups.append(g)
        return g

    def _need(self, e, tok):
        if tok is None:
            return
        kind, obj, val = tok
        if kind == "dma":
            grp, gep = obj
            key = (id(grp), gep)
            sem = grp.sems[gep]
            if grp.final:
                if self.seen[e].get(key, 0):
                    return
                self.seen[e][key] = 1
                self.prog[e].append(lambda E, sem=sem, grp=grp: E.wait_ge(sem, 16 * grp.count))
                self.trace[e].append(("waitf", key, grp))
                return
            if self.seen[e].get(key, 0) >= val:
                return
            self.seen[e][key] = val
            self.prog[e].append(lambda E, sem=sem, val=val: E.wait_ge(sem, 16 * val))
            self.trace[e].append(("wait", key, val))
            return
        eng, eep = obj
        key = obj
        if self.seen[e].get(key, 0) >= val:
            return
        if eng == e and eep == self.ep(e) and val > self.cnt[e]:
            return
        self.seen[e][key] = val
        sem = self.sems[eng][eep]
        self.prog[e].append(lambda E, sem=sem, val=val: E.wait_ge(sem, val))
        self.trace[e].append(("wait", key, val))

    def _deps(self, e, reads, writes):
        for reg in reads:
            self._need(e, reg.w)
        for reg in writes:
            self._need(e, reg.w)
            for t in reg.r.values():
                self._need(e, t)

    def I(self, e, fn, reads=(), writes=(), signal=True):
        if self.cnt[e] >= self.LIMIT and not self.pending[e]:
            self.sems[e].append(self.nc.alloc_semaphore(name="es_%s_%d" % (e, len(self.sems[e]))))
            self.cnt[e] = 0
        self._deps(e, reads, writes)
        idx = self.cnt[e] + 1
        tok = ("eng", (e, self.ep(e)), idx)
        for reg in reads:
            reg.r[e] = tok
        for reg in writes:
            reg.w = tok
            reg.r = {}
        if signal:
            self.cnt[e] = idx
            self.pending[e] = False
            sem = self.sems[e][-1]
            self.prog[e].append(lambda E, fn=fn, sem=sem: fn(E).then_inc(sem, 1))
            self.trace[e].append(("inc", (e, self.ep(e))))
        else:
            self.pending[e] = True
            self.prog[e].append(lambda E, fn=fn: fn(E))

    def D(self, q, grp, fn, reads=(), writes=()):
        subs = grp.__dict__.setdefault("subs", {})
        if q not in subs:
            if not subs:
                subs[q] = grp
            else:
                sg = DmaGroup(self, grp.name + "_" + q, grp.final)
                if getattr(grp, "skip_barrier", False):
                    sg.skip_barrier = True
                self.groups.append(sg)
                subs[q] = sg
        grp = subs[q]
        if grp.count >= self.DLIMIT and not grp.final:
            grp.roll()
        def same(t):
            return t[0] == "dma" and t[1][0] is grp
        for reg in reads:
            if reg.w is not None and not same(reg.w):
                self._need(q, reg.w)
        for reg in writes:
            for t in [reg.w] + list(reg.r.values()):
                if t is not None and not same(t):
                    self._need(q, t)
        grp.count += 1
        tok = ("dma", (grp, grp.ep), grp.count)
        for reg in reads:
            reg.r["dma%d" % id(grp)] = tok
        for reg in writes:
            reg.w = tok
            reg.r = {}
        sem = grp.sems[-1]
        self.prog[q].append(lambda E, fn=fn, sem=sem: fn(E).then_inc(sem, 16))
        self.trace[q].append(("inc", (id(grp), grp.ep)))

    def wait_all(self, e, regs):
        for reg in regs:
            self._need(e, reg.w)
            for t in reg.r.values():
                self._need(e, t)

    def barrier(self):
        for e in self.ENG:
            for e2 in self.ENG:
                if e2 != e and self.cnt[e2] > 0:
                    self._need(e, ("eng", (e2, self.ep(e2)), self.cnt[e2]))
            for g in self.groups:
                if g.count > 0 and not getattr(g, "skip_barrier", False):
                    self._need(e, ("dma", (g, g.ep), g.count))

    def check(self):
        val = {}
        pc = {k: 0 for k in self.ENG}
        while True:
            prog = False
            for e in self.ENG:
                tr = self.trace[e]
                while pc[e] < len(tr):
                    op = tr[pc[e]]
                    if op[0] == "inc":
                        val[op[1]] = val.get(op[1], 0) + 1
                    elif op[0] == "wait":
                        if val.get(op[1], 0) < op[2]:
                            break
                    else:
                        if val.get(op[1], 0) < op[2].count:
                            break
                    pc[e] += 1
                    prog = True
            if not prog:
                break
        stuck = {e: (pc[e], len(self.trace[e]), self.trace[e][pc[e]]) for e in self.ENG if pc[e] < len(self.trace[e])}
        assert not stuck, "SYNC DEADLOCK: %s" % stuck

    def emit(self):
        nc = self.nc
        self.check()
        with nc.Block() as block:
            @block.tensor
            def _(E):
                for f in self.prog["pe"]:
                    f(E)

            @block.vector
            def _(E):
                for f in self.prog["dve"]:
                    f(E)

            @block.scalar
            def _(E):
                for f in self.prog["act"]:
                    f(E)

            @block.gpsimd
            def _(E):
                for f in self.prog["pool"]:
                    f(E)

            @block.sync
            def _(E):
                for f in self.prog["sp"]:
                    f(E)


class Arena:
    LO = 20608
    HI = 228352

    def __init__(self, nc):
        self.nc = nc
        self.lo = self.LO
        self.hi = self.HI
        self.n = 0

    def _sz(self, shape, dt):
        sz = int(np.prod(shape[1:])) * (4 if dt == F32 else 2)
        return (sz + 63) // 64 * 64

    def alloc(self, name, shape, dt, top=False):
        sz = self._sz(shape, dt)
        self.n += 1
        if top:
            self.hi -= sz
            off = self.hi
        else:
            off = self.lo
            self.lo += sz
        assert self.lo <= self.hi, "SBUF arena overflow at %s: lo=%d hi=%d" % (name, self.lo, self.hi)
        return self.nc.alloc_sbuf_tensor_at("%s_%d" % (name, self.n), list(shape), dt, offset=off)

    def mark(self):
        return self.lo

    def release(self, m):
        self.lo = m


class StopBuild(Exception):
    pass


import os
CUT = int(os.environ.get("CUT", "0"))


def cut(n):
    if CUT == n:
        raise StopBuild()


def build(dbg=(), stop_after=99):
    nc = bass.Bass("TRN2", target_bir_lowering=False)
    fw = FW(nc)
    ar = Arena(nc)
    I, Dm = fw.I, fw.D

    used_inputs = []

    class LazyIn:
        def __init__(self, name, shape):
            self.name, self.shape, self._ap = name, shape, None

        def ap(self):
            if self._ap is None:
                used_inputs.append(self.name)
                self._ap = nc.dram_tensor(self.name, list(self.shape), F32, kind="ExternalInput").ap()
            return self._ap

        def __getitem__(self, k):
            return self.ap()[k]

        def rearrange(self, *a, **kw):
            return self.ap().rearrange(*a, **kw)

    def din(name, shape, dt=F32, stage=0):
        return LazyIn(name, shape)

    x_d = din("x", [L, D], stage=1)
    ctx_d = din("ctx", [CTX, D], stage=1)
    cc_d = din("cc", [128, 16], stage=0)
    wada_d = din("w_ada", [D, 6 * D], stage=0)
    bada_d = din("b_ada", [1, 6 * D], stage=0)
    win_d = din("w_in_l", [D, 1280], stage=1)
    wuq_d = din("w_uq_l", [QL, NH * 192], stage=3)
    wuk_d = din("w_uk_l", [KVL, NH * 96], stage=3)
    wuv_d = din("w_uv_l", [KVL, NH * 64], stage=3)
    qg_d = din("qg_l", [128, 3], stage=3)
    kvg_d = din("kvg_l", [128, 2], stage=3)
    rope_d = din("rope_cs", [32, 2, L], stage=1)
    esel_d = din("esel", [32, 96], stage=3)
    ident_d = din("ident", [128, 128], stage=0)
    s5col_d = din("s5_col", [128, 3, 32], stage=2)
    s5row_d = din("s5_row", [128, 3, 1024], stage=2)
    s5b_d = din("s5_b_l", [128, 2, 1024], stage=2)
    s5c_d = din("s5_c_l", [128, 2, 32, 128], stage=2)
    s5d_d = din("s5_d_col", [128, 4], stage=2)
    wglu_d = din("s5_w_glu", [512, 512], stage=2)
    bglu_d = din("b_glu_col", [128, 4], stage=2)
    wout_d = din("w_out", [D, D], stage=4)
    lnp_d = din("ln_params", [4, D], stage=4)
    wgu_d = din("w_gate_up", [D, 2 * DFF], stage=4)
    wdn_d = din("w_down", [DFF, D], stage=4)
    out_d = nc.dram_tensor("out", [L, D], F32, kind="ExternalOutput").ap()
    dbg_d = {}

    def dbg_out(name, shape):
        dbg_d[name] = nc.dram_tensor("dbg_" + name, list(shape), F32, kind="ExternalOutput").ap()
        return dbg_d[name]

    g_const = fw.group("const", final=True)
    g_dbg = fw.group("dbg")
    dbg_regs = []

    def dump(name, src_ap, reg, shape):
        if name not in dbg:
            return
        o = dbg_out(name, shape)
        Dm("pool", g_dbg, lambda E: E.dma_start(out=o, in_=src_ap), reads=[reg])
        dbg_regs.append(reg)

    from contextlib import ExitStack
    es = ExitStack()
    with es:
        ptr = es.enter_context(nc.psum_tensor("ptr", [128, 1024], BF16))
        r_ptr = Reg()
        pbank = [es.enter_context(nc.psum_tensor("pb%d" % i, [128, 512], F32)) for i in range(7)]
        r_pb = [Reg() for _ in range(7)]
        pb_next = [0]

        def bank():
            i = pb_next[0] % 7
            pb_next[0] += 1
            return pbank[i], r_pb[i]

        ident = ar.alloc("ident", [128, 128], BF16); r_ident = Reg()
        identf = ar.alloc("identf", [128, 128], F32); r_identf = Reg()
        Dm("pool", g_const, lambda E: E.dma_start(out=ident[:], in_=ident_d[:, :]), writes=[r_ident])
        Dm("sp", g_const, lambda E: E.dma_start(out=identf[:], in_=ident_d[:, :]), writes=[r_identf])
        onesr = ar.alloc("onesr", [1, 128], F32); r_onesr = Reg()
        I("pool", lambda E: E.memset(onesr[:], 1.0), writes=[r_onesr])
        onesb = ar.alloc("onesb", [128, 128], BF16); r_onesb = Reg()
        I("pool", lambda E: E.memset(onesb[:], 1.0), writes=[r_onesb])
        epsc = ar.alloc("epsc", [128, 1], F32); r_epsc = Reg()
        I("pool", lambda E: E.memset(epsc[:], EPS), writes=[r_epsc])
        halfpi = ar.alloc("halfpi", [128, 1], F32); r_halfpi = Reg()
        I("pool", lambda E: E.memset(halfpi[:], float(np.pi / 2)), writes=[r_halfpi])
        modcol = ar.alloc("modcol", [128, 8, 6], F32); r_modcol = Reg()
        g12b = ar.alloc("g12b", [128, 2, D], F32); r_g12b = Reg()

        try:
            m0 = ar.mark()
            cct = ar.alloc("cct", [128, 16], F32); r_cct = Reg()
            Dm("sp", g_const, lambda E: E.dma_start(out=cct[:], in_=cc_d[:, :]), writes=[r_cct])
            scb = ar.alloc("scb", [128, 8, 32], BF16); r_scb = Reg()
            I("pool", lambda E: E.memset(scb[:], 0.0), writes=[r_scb])
            I("act", lambda E: E.activation(out=scb[:, :, 0:2], in_=cct[:].rearrange("p (k j) -> p k j", j=2), func=AF.Silu), reads=[r_cct], writes=[r_scb])
            badat = ar.alloc("badat", [2, 6 * D], F32); r_badat = Reg()
            Dm("sp", g_const, lambda E: E.dma_start(out=badat[:], in_=bada_d[0:1, :].partition_broadcast(2)), writes=[r_badat])
            cut(1)
            modrow = ar.alloc("modrow", [2, 6 * D], F32); r_modrow = Reg()
            wa = [ar.alloc("wa%d" % i, [128, 8, 512], BF16) for i in range(2)]
            r_wa = [Reg(), Reg()]
            g_wa = [fw.group("wa0"), fw.group("wa1")]
            wada_v = wada_d.rearrange("(k p) n -> p k n", p=128)
            for cch in range(12):
                b = cch % 2
                Dm("pool", g_wa[b], lambda E, b=b, cch=cch: E.dma_start(out=wa[b][:], in_=wada_v[:, :, cch * 512:(cch + 1) * 512]), writes=[r_wa[b]])
                VAR = int(os.environ.get("VAR", "0"))
                if VAR == 1:
                    continue
                pb, rpb = bank()
                for k in range(8):
                    I("pe", lambda E, pb=pb, b=b, k=k: E.matmul(pb[0:32, :], lhsT=scb[:, k, :], rhs=wa[b][:, k, :], start=(k == 0), stop=(k == 7)),
                      reads=[r_scb, r_wa[b]], writes=[rpb], signal=(k == 7 or VAR == 3))
                if VAR in (2, 3):
                    continue
                I("dve", lambda E, pb=pb, cch=cch: E.tensor_tensor(out=modrow[:, cch * 512:(cch + 1) * 512], in0=pb[0:2, :], in1=badat[:, cch * 512:(cch + 1) * 512], op=ALU.add),
                  reads=[rpb, r_badat], writes=[r_modrow])
            cut(2)
            for c0 in (D, 4 * D):
                I("dve", lambda E, c0=c0: E.tensor_scalar(out=modrow[:, c0:c0 + D], in0=modrow[:, c0:c0 + D], scalar1=1.0, scalar2=None, op0=ALU.add),
                  reads=[r_modrow], writes=[r_modrow])
            cut(3)
            pb, rpb = bank()
            srcs = [0, 1, 3, 4]
            for k in range(8):
                for jj, s in enumerate(srcs):
                    col = (k * 4 + jj) * 2
                    I("pe", lambda E, pb=pb, k=k, s=s, col=col: E.transpose(out=pb[:, col:col + 2], in_=modrow[0:2, s * D + k * 128:s * D + (k + 1) * 128], identity=identf[0:2, 0:2]),
                      reads=[r_modrow, r_identf], writes=[rpb], signal=(k == 7 and jj == 3))
            cut(4)
            pbv = pb[:, 0:64].rearrange("p (k j two) -> p k j two", k=8, j=4, two=2)
            I("dve", lambda E: E.tensor_copy(out=modcol[:, :, 0:4], in_=pbv[:, :, :, 0]), reads=[rpb], writes=[r_modcol])
            I("dve", lambda E: E.tensor_copy(out=modcol[:, :, 4:6], in_=pbv[:, :, 0:2, 1]), reads=[rpb], writes=[r_modcol])
            cut(5)
            for gi, c0 in enumerate((2 * D, 5 * D)):
                for hh in range(2):
                    pb, rpb = bank()
                    I("pe", lambda E, pb=pb, c0=c0, hh=hh: E.matmul(pb[:, :], lhsT=onesr[0:1, :], rhs=modrow[0:1, c0 + hh * 512:c0 + (hh + 1) * 512], start=True, stop=True),
                      reads=[r_onesr, r_modrow], writes=[rpb])
                    I("act", lambda E, pb=pb, gi=gi, hh=hh: E.activation(out=g12b[:, gi, hh * 512:(hh + 1) * 512], in_=pb[:, :], func=AF.Identity),
                      reads=[rpb], writes=[r_g12b])
            dump("modcol", modcol[:].rearrange("p k j -> p (k j)"), r_modcol, [128, 48])
            dump("g12b", g12b[:].rearrange("p a d -> p (a d)"), r_g12b, [128, 2 * D])
            fw.barrier()
            ar.release(m0)
            g_const = fw.group("const1", final=True)

            if stop_after >= 1:
                mA = ar.mark()
                kvcT = ar.alloc("kvcT", [128, 2, T], BF16); r_kvcT = [Reg() for _ in range(NT)]
                krT = ar.alloc("krT", [32, T], BF16); r_krT = [Reg() for _ in range(NT)]
                qcT = ar.alloc("qcT", [128, 3, L], BF16); r_qcT = [Reg() for _ in range(NT)]
                mB = ar.mark()
                us5T = ar.alloc("us5T", [128, 4, T], BF16); r_us5T = [Reg() for _ in range(NT)]
                m1a = ar.mark()
                winb = ar.alloc("winb", [128, 8, 1280], BF16); r_winb = Reg()
                win_v = win_d.rearrange("(k p) n -> p k n", p=128)
                for k in range(8):
                    Dm("pool", g_const, lambda E, k=k: E.dma_start(out=winb[:, k, :], in_=win_v[:, k, :]), writes=[r_winb])
                cut(10)
                wgu_bf = nc.dram_tensor("wgu_bf", [44, 128, 1024], BF16).ap()
                wdn_bf = nc.dram_tensor("wdn_bf", [2, 128, 22 * 512], BF16).ap()
                r_wgubf = Reg(); r_wdnbf = Reg()
                g_pre = fw.group("precast", final=True)
                g_pre.skip_barrier = True
                wgu_v0 = wgu_d.rearrange("(k p) n -> p k n", p=128)
                wdn_v0 = wdn_d.rearrange("(f p) n -> p f n", p=128)
                for c in range(44):
                    gu_, f_ = c // 22, c % 22
                    c0 = gu_ * DFF + f_ * 128
                    Dm("pool", g_pre, lambda E, c=c, c0=c0: E.dma_start(out=wgu_bf[c].rearrange("p (k n) -> p k n", k=8), in_=wgu_v0[:, :, c0:c0 + 128]), writes=[r_wgubf])
                for hh in range(2):
                    Dm("pool", g_pre, lambda E, hh=hh: E.dma_start(out=wdn_bf[hh].rearrange("p (f n) -> p f n", f=22), in_=wdn_v0[:, :, hh * 512:(hh + 1) * 512]), writes=[r_wdnbf])
                xt = [ar.alloc("xt%d" % i, [128, D], F32) for i in range(2)]; r_xt = [Reg(), Reg()]
                g_xt = [fw.group("xt0"), fw.group("xt1")]
                xh = ar.alloc("xh", [128, D], BF16); r_xh = Reg()
                uT = ar.alloc("uT", [128, 8, 512], BF16); r_uT = Reg()
                st6 = ar.alloc("st6", [128, 2, 6], F32); r_st6 = Reg()
                mv = ar.alloc("mv", [128, 2], F32); r_mv = Reg()
                rstd = ar.alloc("rstd", [128, 1], F32); r_rstd = Reg()
                nbias = ar.alloc("nbias", [128, 1], F32); r_nbias = Reg()
                ropet = ar.alloc("ropet", [32, 2, 512], F32); r_ropet = Reg()
                g_rope = fw.group("rope")
                sq = [ar.alloc("sq%d" % i, [128, 512], BF16) for i in range(2)]; r_sq = [Reg(), Reg()]
                Rn = ar.alloc("Rn", [128, 512], F32); r_Rn = Reg()
                tmpk = ar.alloc("tmpk", [32, 512], F32); r_tmpk = Reg()

                def ln_rows(src, r_src, dst_bf, r_dst):
                    for hh in range(2):
                        I("dve", lambda E, hh=hh: E.bn_stats(out=st6[:, hh, :], in_=src[:, hh * 512:(hh + 1) * 512]), reads=[r_src], writes=[r_st6])
                    I("dve", lambda E: E.bn_aggr(out=mv[:], in_=st6[:].rearrange("p a b -> p (a b)")), reads=[r_st6], writes=[r_mv])
                    I("act", lambda E: E.activation(out=rstd[:], in_=mv[:, 1:2], func=AF.Sqrt, bias=epsc[:, 0:1], scale=1.0), reads=[r_mv, r_epsc], writes=[r_rstd])
                    I("dve", lambda E: E.reciprocal(out=rstd[:], in_=rstd[:]), reads=[r_rstd], writes=[r_rstd])
                    I("dve", lambda E: E.scalar_tensor_tensor(out=nbias[:], in0=mv[:, 0:1], scalar=-1.0, in1=rstd[:], op0=ALU.mult, op1=ALU.mult),
                      reads=[r_mv, r_rstd], writes=[r_nbias])
                    I("act", lambda E: E.activation(out=dst_bf[:], in_=src[:], func=AF.Identity, bias=nbias[:, 0:1], scale=rstd[:, 0:1]),
                      reads=[r_src, r_rstd, r_nbias], writes=[r_dst])

                def rms_scale(pbs, rpbs, nrows_total, n):
                    pS, rpS = bank()
                    for i, (pb, rpb) in enumerate(zip(pbs, rpbs)):
                        sb = i % 2
                        I("act", lambda E, pb=pb, sb=sb: E.activation(out=sq[sb][:, :n], in_=pb[:, :n], func=AF.Square), reads=[rpb], writes=[r_sq[sb]])
                        I("pe", lambda E, sb=sb, i=i, pS=pS: E.matmul(pS[:, :n], lhsT=onesb[:, :], rhs=sq[sb][:, :n], start=(i == 0), stop=(i == len(pbs) - 1)),
                          reads=[r_onesb, r_sq[sb]], writes=[rpS])
                    I("act", lambda E, pS=pS: E.activation(out=Rn[:, :n], in_=pS[:, :n], func=AF.Sqrt, bias=epsc[:, 0:1], scale=1.0 / nrows_total),
                      reads=[rpS, r_epsc], writes=[r_Rn])
                    I("dve", lambda E: E.reciprocal(out=Rn[:, :n], in_=Rn[:, :n]), reads=[r_Rn], writes=[r_Rn])

                sub = 0
                for ti in range(NT):
                    n = 256 if ti == 0 else 512
                    t0 = 0 if ti == 0 else CTX + (ti - 1) * 512
                    nsub = n // 128
                    for si in range(nsub):
                        b = sub % 2
                        sub += 1
                        if ti == 0:
                            src = ctx_d[si * 128:(si + 1) * 128, :]
                        else:
                            l0 = (ti - 1) * 512 + si * 128
                            src = x_d[l0:l0 + 128, :]
                        Dm("sp", g_xt[b], lambda E, b=b, src=src: E.dma_start(out=xt[b][:], in_=src), writes=[r_xt[b]])
                        ln_rows(xt[b], r_xt[b], xh, r_xh)
                        for k in range(8):
                            I("pe", lambda E, k=k: E.transpose(out=ptr[:, k * 128:(k + 1) * 128], in_=xh[:, k * 128:(k + 1) * 128], identity=ident[:, :]),
                              reads=[r_xh, r_ident], writes=[r_ptr], signal=(k == 7))
                        j0 = 4 if ti == 0 else 0
                        for k in range(8):
                            eng = "act" if k % 2 == 0 else "dve"
                            if eng == "act":
                                I("act", lambda E, k=k, si=si, j0=j0: E.activation(out=uT[:, k, si * 128:(si + 1) * 128], in_=ptr[:, k * 128:(k + 1) * 128], func=AF.Identity,
                                                                                   bias=modcol[:, k, j0:j0 + 1], scale=modcol[:, k, j0 + 1:j0 + 2]),
                                  reads=[r_ptr, r_modcol], writes=[r_uT])
                            else:
                                I("dve", lambda E, k=k, si=si, j0=j0: E.tensor_scalar(out=uT[:, k, si * 128:(si + 1) * 128], in0=ptr[:, k * 128:(k + 1) * 128],
                                                                                      scalar1=modcol[:, k, j0 + 1:j0 + 2], scalar2=modcol[:, k, j0:j0 + 1], op0=ALU.mult, op1=ALU.add),
                                  reads=[r_ptr, r_modcol], writes=[r_uT])
                    cut(11)
                    if ti == 1:
                        cut(25)

                    def proj(c0, m, n=n):
                        pb, rpb = bank()
                        for k in range(8):
                            I("pe", lambda E, pb=pb, k=k, c0=c0, m=m, n=n: E.matmul(pb[0:m, :n], lhsT=winb[:, k, c0:c0 + m], rhs=uT[:, k, :n], start=(k == 0), stop=(k == 7)),
                              reads=[r_winb, r_uT], writes=[rpb], signal=(k == 7))
                        return pb, rpb

                    if ti > 0:
                        l0 = (ti - 1) * 512
                        pq = [proj(i * 128, 128) for i in range(3)]
                        rms_scale([p[0] for p in pq], [p[1] for p in pq], float(QL), n)
                        for i in range(3):
                            I("dve", lambda E, i=i, l0=l0, pq=pq: E.tensor_tensor(out=qcT[:, i, l0:l0 + 512], in0=pq[i][0][:, :], in1=Rn[:, :], op=ALU.mult),
                              reads=[pq[i][1], r_Rn], writes=[r_qcT[ti]])
                    if ti == 1:
                        cut(26)
                    pk = [proj(QL + i * 128, 128) for i in range(2)]
                    cut(20)
                    if ti == 1:
                        cut(27)
                    rms_scale([p[0] for p in pk], [p[1] for p in pk], float(KVL), n)
                    cut(21)
                    if ti == 1:
                        cut(28)
                    for i in range(2):
                        I("dve", lambda E, i=i, n=n, t0=t0, pk=pk: E.tensor_tensor(out=kvcT[:, i, t0:t0 + n], in0=pk[i][0][:, :n], in1=Rn[:, :n], op=ALU.mult),
                          reads=[pk[i][1], r_Rn], writes=[r_kvcT[ti]])
                    cut(22)
                    if ti == 1:
                        cut(12)
                    pr, rpr = proj(QL + KVL, 32)
                    if ti == 0:
                        I("act", lambda E, pr=pr, n=n, t0=t0: E.activation(out=krT[:, t0:t0 + n], in_=pr[0:32, :n], func=AF.Identity), reads=[rpr], writes=[r_krT[ti]])
                    else:
                        ps_, rps = proj(QL + KVL + 32, 32)
                        l0 = (ti - 1) * 512
                        Dm("sp", g_rope, lambda E, l0=l0: E.dma_start(out=ropet[:], in_=rope_d[:, :, l0:l0 + 512]), writes=[r_ropet])
                        I("dve", lambda E, pr=pr: E.tensor_tensor(out=tmpk[:], in0=pr[0:32, :], in1=ropet[:, 0, :], op=ALU.mult), reads=[rpr, r_ropet], writes=[r_tmpk])
                        I("dve", lambda E, ps_=ps_: E.tensor_tensor(out=ropet[:, 1, :], in0=ps_[0:32, :], in1=ropet[:, 1, :], op=ALU.mult), reads=[rps, r_ropet], writes=[r_ropet])
                        I("dve", lambda E, t0=t0: E.tensor_tensor(out=krT[:, t0:t0 + 512], in0=tmpk[:], in1=ropet[:, 1, :], op=ALU.add), reads=[r_tmpk, r_ropet], writes=[r_krT[ti]])
                    cut(23)
                    if ti == 1:
                        cut(13)
                    for i in range(4):
                        p5, rp5 = proj(QL + KVL + 64 + i * 128, 128)
                        if i % 2 == 0:
                            I("act", lambda E, p5=p5, i=i, n=n, t0=t0: E.activation(out=us5T[:, i, t0:t0 + n], in_=p5[:, :n], func=AF.Identity), reads=[rp5], writes=[r_us5T[ti]])
                        else:
                            I("dve", lambda E, p5=p5, i=i, n=n, t0=t0: E.tensor_copy(out=us5T[:, i, t0:t0 + n], in_=p5[:, :n]), reads=[rp5], writes=[r_us5T[ti]])
                dump("qcT", qcT[:, :, 0:1024], r_qcT[1], [128, 3, 1024])
                dump("kvcT", kvcT[:, :, 0:1024], r_kvcT[1], [128, 2, 1024])
                dump("krT", krT[:, 0:1024], r_krT[1], [32, 1024])
                dump("us5T", us5T[:, :, 0:1024], r_us5T[1], [128, 4, 1024])
                fw.barrier()
                ar.release(m1a)

            s5T = ar.alloc("s5T", [128, 4, L], BF16, top=True); r_s5T = [Reg() for _ in range(8)]
            if os.environ.get("ZERO_S5"):
                I("pool", lambda E: E.memset(s5T[:], 0.0), writes=r_s5T)
            if stop_after >= 2 and not os.environ.get("ZERO_S5"):
                m2 = ar.mark()
                g_c2 = fw.group("const2", final=True)
                W = 1024
                colp = ar.alloc("colp", [128, 3, 32], F32); r_colp = Reg()
                dcol = ar.alloc("dcol", [128, 4], F32); r_dcol = Reg()
                bgl = ar.alloc("bgl", [128, 4], F32); r_bgl = Reg()
                CTb = ar.alloc("CTb", [128, 2, 32, 128], BF16); r_CTb = Reg()
                wglub = ar.alloc("wglub", [128, 4, 512], BF16); r_wglub = Reg()
                Dm("sp", g_c2, lambda E: E.dma_start(out=colp[:], in_=s5col_d[:, :, :]), writes=[r_colp])
                Dm("sp", g_c2, lambda E: E.dma_start(out=dcol[:], in_=s5d_d[:, :]), writes=[r_dcol])
                Dm("sp", g_c2, lambda E: E.dma_start(out=bgl[:], in_=bglu_d[:, :]), writes=[r_bgl])
                for c in range(2):
                    Dm("pool", g_c2, lambda E, c=c: E.dma_start(out=CTb[:, c, :, :], in_=s5c_d[:, c, :, :]), writes=[r_CTb])
                wglu_v = wglu_d.rearrange("(k p) n -> p k n", p=128)
                for k in range(4):
                    Dm("pool", g_c2, lambda E, k=k: E.dma_start(out=wglub[:, k, :], in_=wglu_v[:, k, :]), writes=[r_wglub])
                BbT = ar.alloc("BbT", [128, 2, W], BF16); r_BbT = Reg()
                rcol = ar.alloc("rcol", [128, 32], F32); r_rcol = Reg()
                thcol = ar.alloc("thcol", [128, 32], F32); r_thcol = Reg()
                m2b = ar.mark()
                prm = ar.alloc("prm", [128, 3, W], F32); r_prm = Reg()
                bl_ = ar.alloc("bl_", [128, 2, W], F32); r_bl = Reg()
                Dm("sp", g_c2, lambda E: E.dma_start(out=prm[:], in_=s5row_d[:, :, :]), writes=[r_prm])
                Dm("sp", g_c2, lambda E: E.dma_start(out=bl_[:], in_=s5b_d[:, :, :]), writes=[r_bl])
                tt = [ar.alloc("s5t%d" % i, [128, W], F32) for i in range(8)]
                r_tt = [Reg() for _ in range(8)]

                def ew(eng, fn, rd, wr):
                    I(eng, fn, reads=[r_tt[i] for i in rd] + [r_prm, r_bl], writes=[r_tt[i] for i in wr])

                LRE, LIM, LDT = prm[:, 0, :], prm[:, 1, :], prm[:, 2, :]
                DT, MAG, TH, KK, SN, CS, DEN, AM1 = [t[:] for t in tt]
                ew("act", lambda E: E.activation(out=DT, in_=LDT, func=AF.Exp), [], [0])
                ew("dve", lambda E: E.tensor_tensor(out=MAG, in0=LRE, in1=DT, op=ALU.mult), [0], [1])
                ew("act", lambda E: E.activation(out=MAG, in_=MAG, func=AF.Exp), [1], [1])
                ew("dve", lambda E: E.tensor_tensor(out=TH, in0=LIM, in1=DT, op=ALU.mult), [0], [2])
                ew("dve", lambda E: E.tensor_scalar(out=KK, in0=TH, scalar1=1.0 / TWO_PI, scalar2=MAGIC, op0=ALU.mult, op1=ALU.add), [2], [3])
                ew("dve", lambda E: E.tensor_scalar(out=KK, in0=KK, scalar1=MAGIC, scalar2=-TWO_PI, op0=ALU.subtract, op1=ALU.mult), [3], [3])
                ew("dve", lambda E: E.tensor_tensor(out=TH, in0=TH, in1=KK, op=ALU.add), [2, 3], [2])
                ew("dve", lambda E: E.tensor_scalar(out=TH, in0=TH, scalar1=PI_LO, scalar2=-PI_LO, op0=ALU.min, op1=ALU.max), [2], [2])
                ew("act", lambda E: E.activation(out=SN, in_=TH, func=AF.Sin), [2], [4])
                ew("act", lambda E: E.activation(out=KK, in_=TH, func=AF.Abs), [2], [3])
                I("act", lambda E: E.activation(out=CS, in_=KK, func=AF.Sin, bias=halfpi[:, 0:1], scale=-1.0), reads=[r_tt[3], r_halfpi], writes=[r_tt[5]])
                ew("dve", lambda E: E.tensor_tensor(out=CS, in0=CS, in1=MAG, op=ALU.mult), [5, 1], [5])
                ew("dve", lambda E: E.tensor_tensor(out=SN, in0=SN, in1=MAG, op=ALU.mult), [4, 1], [4])
                ew("dve", lambda E: E.tensor_tensor(out=DEN, in0=LRE, in1=LRE, op=ALU.mult), [], [6])
                ew("dve", lambda E: E.tensor_tensor(out=KK, in0=LIM, in1=LIM, op=ALU.mult), [], [3])
                ew("dve", lambda E: E.tensor_tensor(out=DEN, in0=DEN, in1=KK, op=ALU.add), [6, 3], [6])
                ew("dve", lambda E: E.reciprocal(out=DEN, in_=DEN), [6], [6])
                ew("dve", lambda E: E.tensor_scalar(out=AM1, in0=CS, scalar1=-1.0, scalar2=None, op0=ALU.add), [5], [7])
                ew("dve", lambda E: E.tensor_tensor(out=DT, in0=AM1, in1=LRE, op=ALU.mult), [7], [0])
                ew("dve", lambda E: E.tensor_tensor(out=KK, in0=SN, in1=LIM, op=ALU.mult), [4], [3])
                ew("dve", lambda E: E.tensor_tensor(out=DT, in0=DT, in1=KK, op=ALU.add), [0, 3], [0])
                ew("dve", lambda E: E.tensor_tensor(out=DT, in0=DT, in1=DEN, op=ALU.mult), [0, 6], [0])
                ew("dve", lambda E: E.tensor_tensor(out=MAG, in0=SN, in1=LRE, op=ALU.mult), [4], [1])
                ew("dve", lambda E: E.tensor_tensor(out=KK, in0=AM1, in1=LIM, op=ALU.mult), [7], [3])
                ew("dve", lambda E: E.tensor_tensor(out=MAG, in0=MAG, in1=KK, op=ALU.subtract), [1, 3], [1])
                ew("dve", lambda E: E.tensor_tensor(out=MAG, in0=MAG, in1=DEN, op=ALU.mult), [1, 6], [1])
                BRE, BIM = bl_[:, 0, :], bl_[:, 1, :]
                ew("dve", lambda E: E.tensor_tensor(out=TH, in0=DT, in1=BRE, op=ALU.mult), [0], [2])
                ew("dve", lambda E: E.tensor_tensor(out=KK, in0=MAG, in1=BIM, op=ALU.mult), [1], [3])
                I("dve", lambda E: E.tensor_tensor(out=BbT[:, 0, :], in0=TH, in1=KK, op=ALU.subtract), reads=[r_tt[2], r_tt[3]], writes=[r_BbT])
                ew("dve", lambda E: E.tensor_tensor(out=TH, in0=DT, in1=BIM, op=ALU.mult), [0], [2])
                ew("dve", lambda E: E.tensor_tensor(out=KK, in0=MAG, in1=BRE, op=ALU.mult), [1], [3])
                I("dve", lambda E: E.tensor_tensor(out=BbT[:, 1, :], in0=TH, in1=KK, op=ALU.add), reads=[r_tt[2], r_tt[3]], writes=[r_BbT])
                I("act", lambda E: E.activation(out=colp[:, 2, :], in_=colp[:, 2, :], func=AF.Exp), reads=[r_colp], writes=[r_colp])
                I("dve", lambda E: E.tensor_tensor(out=rcol[:], in0=colp[:, 0, :], in1=colp[:, 2, :], op=ALU.mult), reads=[r_colp], writes=[r_rcol])
                I("act", lambda E: E.activation(out=rcol[:], in_=rcol[:], func=AF.Exp), reads=[r_rcol], writes=[r_rcol])
                I("dve", lambda E: E.tensor_tensor(out=thcol[:], in0=colp[:, 1, :], in1=colp[:, 2, :], op=ALU.mult), reads=[r_colp], writes=[r_thcol])
                fw.barrier()
                ar.release(m2b)
                NW = 512
                NWT = 520
                idx = ar.alloc("idx", [128, NWT], F32); r_idx = Reg()
                I("pool", lambda E: E.iota(idx[:], pattern=[[1, NWT]], base=0, channel_multiplier=0, allow_small_or_imprecise_dtypes=True), writes=[r_idx])
                def wb(name, w=NW):
                    return ar.alloc(name, [128, w], F32), Reg()
                ph0, r_ph0 = wb("ph0", NWT)
                NCb = [wb("NC0", NWT), wb("NC1", NWT)]; NSb = [wb("NS0", NWT), wb("NS1", NWT)]
                crot = ar.alloc("crot", [128, 4], F32); r_crot = Reg()
                upcount = [0]
                ma, r_ma = wb("ma"); mb_, r_mb = wb("mb"); mc, r_mc = wb("mc", NWT); md, r_md = wb("md")
                kk_, r_kk = mc, r_mc
                aph, r_aph = mc, r_mc
                VR, r_VR = wb("VR"); VI, r_VI = wb("VI")
                GRb = [wb("GR0"), wb("GR1")]; GIb = [wb("GI0"), wb("GI1")]
                HRt = ar.alloc("HRt", [128, NW], BF16); r_HR = Reg()
                HIt = ar.alloc("HIt", [128, NW], BF16); r_HI = Reg()
                yblk = ar.alloc("yblk", [128, L], F32); r_yblk = Reg()
                tiles = [(0, 256)] + [(CTX + i * 512, 512) for i in range(8)]
                gcount = [0]
                HRb = [(HRt, r_HR), (ar.alloc("HRt1", [128, NW], BF16), Reg())]
                HIb = [(HIt, r_HI), (ar.alloc("HIt1", [128, NW], BF16), Reg())]
                its = []
                for blk in range(4):
                    for jj in range(4):
                        for d in range(2):
                            for ti in range(NT):
                                its.append((blk, jj, d, ti))
                state = {"prev": None, "NC": None, "NS": None}

                def init_y(blk):
                    I("dve", lambda E, blk=blk: E.tensor_scalar(out=yblk[:], in0=us5T[:, blk, CTX:T], scalar1=dcol[:, blk:blk + 1], scalar2=None, op0=ALU.mult),
                      reads=r_us5T + [r_dcol], writes=[r_yblk])

                def emit_bu(it):
                    blk, jj, d, ti = it
                    j0, n = tiles[ti]
                    nti = ti if d == 0 else (0 if ti == 0 else 9 - ti)
                    s0, _ = tiles[nti]
                    bcol = (d * 4 + blk) * 128
                    pbr, rpbr = bank()
                    pbi, rpbi = bank()
                    for c, (pp, rpp) in enumerate(((pbr, rpbr), (pbi, rpbi))):
                        I("pe", lambda E, pp=pp, c=c, jj=jj, bcol=bcol, blk=blk, s0=s0, n=n: E.matmul(pp[:, :n], lhsT=BbT[32 * jj:32 * jj + 32, c, bcol:bcol + 128], rhs=us5T[32 * jj:32 * jj + 32, blk, s0:s0 + n],
                                                                                                   start=True, stop=True, tile_position=(32 * jj, 0)),
                          reads=[r_BbT, r_us5T[nti]], writes=[rpp])
                    return (pbr, rpbr, pbi, rpbi)

                def stage_a(i, bu):
                    blk, jj, d, ti = its[i]
                    j0, n = tiles[ti]
                    up = (d * 4 + blk) * 4 + jj
                    pbr, rpbr, pbi, rpbi = bu
                    if ti == 0:
                        state["prev"] = None
                        upcount[0] += 1
                        (NCt, r_NC), (NSt, r_NS) = NCb[upcount[0] % 2], NSb[upcount[0] % 2]
                        state["NC"], state["NS"] = (NCt, r_NC), (NSt, r_NS)
                        I("pool", lambda E, up=up: E.tensor_scalar(out=ph0[:, :], in0=idx[:, :], scalar1=thcol[:, up:up + 1], scalar2=None, op0=ALU.mult),
                          reads=[r_idx, r_thcol], writes=[r_ph0])
                        I("pool", lambda E: E.tensor_scalar(out=kk_[:, :], in0=ph0[:, :], scalar1=1.0 / TWO_PI, scalar2=MAGIC, op0=ALU.mult, op1=ALU.add), reads=[r_ph0], writes=[r_kk])
                        I("pool", lambda E: E.tensor_scalar(out=kk_[:, :], in0=kk_[:, :], scalar1=MAGIC, scalar2=-TWO_PI, op0=ALU.subtract, op1=ALU.mult), reads=[r_kk], writes=[r_kk])
                        I("pool", lambda E: E.tensor_tensor(out=ph0[:, :], in0=ph0[:, :], in1=kk_[:, :], op=ALU.add), reads=[r_ph0, r_kk], writes=[r_ph0])
                        I("pool", lambda E: E.tensor_scalar(out=ph0[:, :], in0=ph0[:, :], scalar1=PI_LO, scalar2=-PI_LO, op0=ALU.min, op1=ALU.max), reads=[r_ph0], writes=[r_ph0])
                        I("act", lambda E, NSt=NSt: E.activation(out=NSt[:, :], in_=ph0[:, :], func=AF.Sin), reads=[r_ph0], writes=[r_NS])
                        I("act", lambda E: E.activation(out=aph[:, :], in_=ph0[:, :], func=AF.Abs), reads=[r_ph0], writes=[r_aph])
                        I("act", lambda E, NCt=NCt: E.activation(out=NCt[:, :], in_=aph[:, :], func=AF.Sin, bias=halfpi[:, 0:1], scale=-1.0), reads=[r_aph, r_halfpi], writes=[r_NC])
                    (NCt, r_NC), (NSt, r_NS) = state["NC"], state["NS"]
                    if d == 0:
                        BRv, BIv = pbr[:, :n], pbi[:, :n]
                    else:
                        BRv, BIv = pbr[:, :n][:, ::-1], pbi[:, :n][:, ::-1]
                    I("dve", lambda E, BRv=BRv, n=n, NCt=NCt: E.tensor_tensor(out=ma[:, :n], in0=BRv, in1=NCt[:, :n], op=ALU.mult), reads=[rpbr, r_NC], writes=[r_ma])
                    I("dve", lambda E, BIv=BIv, n=n, NSt=NSt: E.tensor_tensor(out=mb_[:, :n], in0=BIv, in1=NSt[:, :n], op=ALU.mult), reads=[rpbi, r_NS], writes=[r_mb])
                    I("dve", lambda E, n=n: E.tensor_tensor(out=VR[:, :n], in0=ma[:, :n], in1=mb_[:, :n], op=ALU.add), reads=[r_ma, r_mb], writes=[r_VR])
                    I("dve", lambda E, BIv=BIv, n=n, NCt=NCt: E.tensor_tensor(out=ma[:, :n], in0=BIv, in1=NCt[:, :n], op=ALU.mult), reads=[rpbi, r_NC], writes=[r_ma])
                    I("dve", lambda E, BRv=BRv, n=n, NSt=NSt: E.tensor_tensor(out=mb_[:, :n], in0=BRv, in1=NSt[:, :n], op=ALU.mult), reads=[rpbr, r_NS], writes=[r_mb])
                    I("dve", lambda E, n=n: E.tensor_tensor(out=VI[:, :n], in0=ma[:, :n], in1=mb_[:, :n], op=ALU.subtract), reads=[r_ma, r_mb], writes=[r_VI])
                    gcount[0] += 1
                    gb = gcount[0] % 2
                    (GR, r_GR), (GI, r_GI) = GRb[gb], GIb[gb]
                    prev = state["prev"]
                    if prev is None:
                        iR, iI, rds = 0.0, 0.0, []
                    else:
                        (pGR, r_pGR), (pGI, r_pGI), pn = prev
                        gRl, gIl = pGR[:, pn - 1:pn], pGI[:, pn - 1:pn]
                        cc_, ss_ = NCt[:, pn:pn + 1], NSt[:, pn:pn + 1]
                        I("dve", lambda E, gIl=gIl, ss_=ss_: E.tensor_scalar(out=crot[:, 0:1], in0=gIl, scalar1=ss_, scalar2=None, op0=ALU.mult), reads=[r_pGI, r_NS], writes=[r_crot])
                        I("dve", lambda E, gRl=gRl, cc_=cc_: E.scalar_tensor_tensor(out=crot[:, 1:2], in0=gRl, scalar=cc_, in1=crot[:, 0:1], op0=ALU.mult, op1=ALU.subtract), reads=[r_pGR, r_NC, r_crot], writes=[r_crot])
                        I("dve", lambda E, gIl=gIl, cc_=cc_: E.tensor_scalar(out=crot[:, 2:3], in0=gIl, scalar1=cc_, scalar2=None, op0=ALU.mult), reads=[r_pGI, r_NC], writes=[r_crot])
                        I("dve", lambda E, gRl=gRl, ss_=ss_: E.scalar_tensor_tensor(out=crot[:, 3:4], in0=gRl, scalar=ss_, in1=crot[:, 2:3], op0=ALU.mult, op1=ALU.add), reads=[r_pGR, r_NS, r_crot], writes=[r_crot])
                        iR, iI, rds = crot[:, 1:2], crot[:, 3:4], [r_crot]
                    I("dve", lambda E, GR=GR, n=n, up=up, iR=iR: E.tensor_tensor_scan(out=GR[:, :n], data0=rcol[:, up:up + 1].to_broadcast([128, n]), data1=VR[:, :n], initial=iR, op0=ALU.mult, op1=ALU.add),
                      reads=[r_VR, r_rcol] + rds, writes=[r_GR])
                    I("dve", lambda E, GI=GI, n=n, up=up, iI=iI: E.tensor_tensor_scan(out=GI[:, :n], data0=rcol[:, up:up + 1].to_broadcast([128, n]), data1=VI[:, :n], initial=iI, op0=ALU.mult, op1=ALU.add),
                      reads=[r_VI, r_rcol] + rds, writes=[r_GI])
                    state["prev"] = ((GR, r_GR), (GI, r_GI), n)
                    if ti == 0:
                        return None
                    (HRx, r_HRx), (HIx, r_HIx) = HRb[i % 2], HIb[i % 2]
                    I("dve", lambda E, GR=GR, n=n, NCt=NCt: E.tensor_tensor(out=mc[:, :n], in0=GR[:, :n], in1=NCt[:, :n], op=ALU.mult), reads=[r_GR, r_NC], writes=[r_mc])
                    I("dve", lambda E, GI=GI, n=n, NSt=NSt: E.tensor_tensor(out=md[:, :n], in0=GI[:, :n], in1=NSt[:, :n], op=ALU.mult), reads=[r_GI, r_NS], writes=[r_md])
                    I("dve", lambda E, n=n, HRx=HRx: E.tensor_tensor(out=HRx[:, :n], in0=mc[:, :n], in1=md[:, :n], op=ALU.subtract), reads=[r_mc, r_md], writes=[r_HRx])
                    I("dve", lambda E, GR=GR, n=n, NSt=NSt: E.tensor_tensor(out=ma[:, :n], in0=GR[:, :n], in1=NSt[:, :n], op=ALU.mult), reads=[r_GR, r_NS], writes=[r_ma])
                    I("dve", lambda E, GI=GI, n=n, NCt=NCt: E.tensor_tensor(out=mb_[:, :n], in0=GI[:, :n], in1=NCt[:, :n], op=ALU.mult), reads=[r_GI, r_NC], writes=[r_mb])
                    I("dve", lambda E, n=n, HIx=HIx: E.scalar_tensor_tensor(out=HIx[:, :n], in0=ma[:, :n], scalar=-1.0, in1=mb_[:, :n], op0=ALU.mult, op1=ALU.subtract), reads=[r_ma, r_mb], writes=[r_HIx])
                    return (HRx, r_HRx, HIx, r_HIx)

                def stage_b(i, hh_):
                    blk, jj, d, ti = its[i]
                    j0, n = tiles[ti]
                    up = (d * 4 + blk) * 4 + jj
                    if hh_ is not None:
                        HRx, r_HRx, HIx, r_HIx = hh_
                        py, rpy = bank()
                        I("pe", lambda E, py=py, up=up, n=n, HRx=HRx: E.matmul(py[:, :n], lhsT=CTb[:, 0, up, :], rhs=HRx[:, :n], start=True, stop=False), reads=[r_CTb, r_HRx], writes=[rpy], signal=False)
                        I("pe", lambda E, py=py, up=up, n=n, HIx=HIx: E.matmul(py[:, :n], lhsT=CTb[:, 1, up, :], rhs=HIx[:, :n], start=False, stop=True), reads=[r_CTb, r_HIx], writes=[rpy])
                        if d == 0:
                            yv = yblk[:, j0 - CTX:j0 - CTX + n]
                        else:
                            lo_ = L - (j0 - CTX) - n
                            yv = yblk[:, lo_:lo_ + n][:, ::-1]
                        I("dve", lambda E, py=py, yv=yv, n=n: E.tensor_tensor(out=yv, in0=yv, in1=py[:, :n], op=ALU.add), reads=[rpy, r_yblk], writes=[r_yblk])
                    if jj == 3 and d == 1 and ti == NT - 1:
                        for cc in range(8):
                            c0 = cc * 512
                            yv = yblk[:, c0:c0 + 512]
                            I("dve", lambda E, yv=yv: E.tensor_tensor(out=mc[:, 0:512], in0=yv, in1=yv, op=ALU.mult), reads=[r_yblk], writes=[r_mc])
                            I("dve", lambda E: E.tensor_scalar(out=mc[:, 0:512], in0=mc[:, 0:512], scalar1=0.044715 * 1.5957691216, scalar2=1.5957691216, op0=ALU.mult, op1=ALU.add), reads=[r_mc], writes=[r_mc])
                            I("dve", lambda E, yv=yv: E.tensor_tensor(out=mc[:, 0:512], in0=mc[:, 0:512], in1=yv, op=ALU.mult), reads=[r_mc, r_yblk], writes=[r_mc])
                            I("act", lambda E: E.activation(out=md[:], in_=mc[:, 0:512], func=AF.Sigmoid), reads=[r_mc], writes=[r_md])
                            I("dve", lambda E, yv=yv, blk=blk, c0=c0: E.tensor_tensor(out=us5T[:, blk, CTX + c0:CTX + c0 + 512], in0=md[:], in1=yv, op=ALU.mult),
                              reads=[r_md, r_yblk], writes=[r_us5T[1 + cc]])
                        if blk < 3:
                            init_y(blk + 1)

                init_y(0)
                NI = len(its)
                bus = {0: emit_bu(its[0])}
                hres = {}
                for i in range(NI):
                    if i + 1 < NI:
                        bus[i + 1] = emit_bu(its[i + 1])
                    hres[i] = stage_a(i, bus.pop(i))
                    if i >= 1:
                        stage_b(i - 1, hres.pop(i - 1))
                stage_b(NI - 1, hres.pop(NI - 1))
                for tl in range(8):
                    c0 = CTX + tl * 512
                    for nb in range(4):
                        pg, rpg = bank()
                        for k in range(4):
                            I("pe", lambda E, pg=pg, k=k, nb=nb, c0=c0: E.matmul(pg[:, :], lhsT=wglub[:, k, nb * 128:(nb + 1) * 128], rhs=us5T[:, k, c0:c0 + 512], start=(k == 0), stop=(k == 3)),
                              reads=[r_wglub, r_us5T[1 + tl]], writes=[rpg], signal=(k == 3))
                        I("act", lambda E, pg=pg, nb=nb: E.activation(out=ma[:], in_=pg[:, :], func=AF.Sigmoid, bias=bgl[:, nb:nb + 1], scale=1.0), reads=[rpg, r_bgl], writes=[r_ma])
                        I("dve", lambda E, nb=nb, tl=tl, c0=c0: E.tensor_tensor(out=s5T[:, nb, tl * 512:(tl + 1) * 512], in0=ma[:], in1=us5T[:, nb, c0:c0 + 512], op=ALU.mult),
                          reads=[r_ma, r_us5T[1 + tl]], writes=[r_s5T[tl]])
                dump("s5T", s5T[:, :, 0:512], r_s5T[0], [128, 4, 512])
                fw.barrier()
                ar.release(m2)
            ar.release(mB)
            if stop_after >= 3:
                attT = ar.alloc("attT", [128, 4, L], BF16, top=True); r_attT = [Reg() for _ in range(8)]
                m3 = ar.mark()
                g_c3 = fw.group("const3", final=True)
                qgt = ar.alloc("qgt", [128, 3], F32); r_qgt = Reg()
                kvgt = ar.alloc("kvgt", [128, 2], F32); r_kvgt = Reg()
                Dm("sp", g_c3, lambda E: E.dma_start(out=qgt[:], in_=qg_d[:, :]), writes=[r_qgt])
                Dm("sp", g_c3, lambda E: E.dma_start(out=kvgt[:], in_=kvg_d[:, :]), writes=[r_kvgt])
                eselb = ar.alloc("eselb", [32, 96], BF16); r_eselb = Reg()
                Dm("pool", g_c3, lambda E: E.dma_start(out=eselb[:], in_=esel_d[:, :]), writes=[r_eselb])
                wuqb = ar.alloc("wuqb", [128, 3, NH * 192], BF16); r_wuqb = Reg()
                wukb = ar.alloc("wukb", [128, 2, NH * 96], BF16); r_wukb = Reg()
                wuvb = ar.alloc("wuvb", [128, 2, NH * 64], BF16); r_wuvb = Reg()
                stg = ar.alloc("stg", [128, NH * 192], F32); r_stg = Reg()
                g_stg = fw.group("stg")
                wuq_v = wuq_d.rearrange("(k p) n -> p k n", p=128)
                wuk_v = wuk_d.rearrange("(k p) n -> p k n", p=128)
                wuv_v = wuv_d.rearrange("(k p) n -> p k n", p=128)
                for k in range(3):
                    for hh in range(2):
                        Dm("sp", g_stg, lambda E, k=k, hh=hh: E.dma_start(out=stg[:, hh * 768:(hh + 1) * 768], in_=wuq_v[:, k, hh * 768:(hh + 1) * 768]), writes=[r_stg])
                    I("dve", lambda E, k=k: E.tensor_scalar(out=wuqb[:, k, :], in0=stg[:, :], scalar1=qgt[:, k:k + 1], scalar2=SCALE, op0=ALU.mult, op1=ALU.mult),
                      reads=[r_stg, r_qgt], writes=[r_wuqb])
                for k in range(2):
                    Dm("sp", g_stg, lambda E, k=k: E.dma_start(out=stg[:, 0:NH * 96], in_=wuk_v[:, k, :]), writes=[r_stg])
                    I("dve", lambda E, k=k: E.tensor_scalar(out=wukb[:, k, :], in0=stg[:, 0:NH * 96], scalar1=kvgt[:, k:k + 1], scalar2=None, op0=ALU.mult),
                      reads=[r_stg, r_kvgt], writes=[r_wukb])
                for k in range(2):
                    Dm("sp", g_stg, lambda E, k=k: E.dma_start(out=stg[:, 0:NH * 64], in_=wuv_v[:, k, :]), writes=[r_stg])
                    I("dve", lambda E, k=k: E.tensor_scalar(out=wuvb[:, k, :], in0=stg[:, 0:NH * 64], scalar1=kvgt[:, k:k + 1], scalar2=None, op0=ALU.mult),
                      reads=[r_stg, r_kvgt], writes=[r_wuvb])
                cut(30)
                KhT = ar.alloc("KhT", [96, T], BF16); r_KhT = Reg()
                QhT = ar.alloc("QhT", [96, L], BF16); r_QhT = Reg()
                Vaug = [ar.alloc("Vaug%d" % i, [128, 34, 128], BF16) for i in range(2)]; r_Vaug = [Reg(), Reg()]
                I("pool", lambda E: E.memset(Vaug[0][:, :, 64:128], 1.0), writes=[r_Vaug[0]])
                I("pool", lambda E: E.memset(Vaug[1][:, :, 0:64], 1.0), writes=[r_Vaug[1]])
                PT = [ar.alloc("PT%d" % i, [128, 512], BF16) for i in range(3)]; r_PT = [Reg() for _ in range(3)]
                Oun = ar.alloc("Oun", [128, 512], F32); r_Oun = Reg()
                Dsh = ar.alloc("Dsh", [128, 512], F32); r_Dsh = Reg()
                g_dsh = fw.group("dsh")
                ropeq = ar.alloc("ropeq", [96, 2, 512], F32); r_ropeq = Reg()
                g_ropeq = fw.group("ropeq")
                tq = ar.alloc("tq", [96, 512], F32); r_tq = Reg()
                Sb = [(pbank[i], r_pb[i]) for i in (0, 1, 2)]
                Ob = [(pbank[i], r_pb[i]) for i in (3, 4)]
                Mb = [(pbank[i], r_pb[i]) for i in (5, 6)]
                mcnt = [0]
                scnt = [0]
                ocnt = [0]

                def mbank():
                    mcnt[0] += 1
                    return Mb[mcnt[0] % 2]

                for h in [int(v) for v in os.environ.get('HEADS', '0,1,2,3,4,5,6,7').split(',')]:
                    par = h % 2
                    for ti in range(NT):
                        n = 256 if ti == 0 else 512
                        t0 = 0 if ti == 0 else CTX + (ti - 1) * 512
                        pb, rpb = mbank()
                        for k in range(2):
                            I("pe", lambda E, pb=pb, k=k, h=h, n=n, t0=t0: E.matmul(pb[0:96, :n], lhsT=wukb[:, k, h * 96:(h + 1) * 96], rhs=kvcT[:, k, t0:t0 + n], start=(k == 0), stop=False),
                              reads=[r_wukb, r_kvcT[ti]], writes=[rpb], signal=False)
                        I("pe", lambda E, pb=pb, n=n, t0=t0: E.matmul(pb[0:96, :n], lhsT=eselb[0:32, :], rhs=krT[0:32, t0:t0 + n], start=False, stop=True),
                          reads=[r_eselb, r_krT[ti]], writes=[rpb])
                        I("dve", lambda E, pb=pb, n=n, t0=t0: E.tensor_copy(out=KhT[0:96, t0:t0 + n], in_=pb[0:96, :n]), reads=[rpb], writes=[r_KhT])
                    cut(31)
                    for qt in range(8):
                        l0 = qt * 512
                        pa, rpa = mbank()
                        for k in range(3):
                            I("pe", lambda E, pa=pa, k=k, h=h, l0=l0: E.matmul(pa[0:96, :], lhsT=wuqb[:, k, h * 192:h * 192 + 96], rhs=qcT[:, k, l0:l0 + 512], start=(k == 0), stop=(k == 2)),
                              reads=[r_wuqb, r_qcT[qt + 1]], writes=[rpa], signal=(k == 2))
                        pbb, rpbb = mbank()
                        for k in range(3):
                            I("pe", lambda E, pbb=pbb, k=k, h=h, l0=l0: E.matmul(pbb[0:96, :], lhsT=wuqb[:, k, h * 192 + 96:h * 192 + 192], rhs=qcT[:, k, l0:l0 + 512], start=(k == 0), stop=(k == 2)),
                              reads=[r_wuqb, r_qcT[qt + 1]], writes=[rpbb], signal=(k == 2))
                        Dm("sp", g_ropeq, lambda E, l0=l0: E.dma_start(out=ropeq[64:96, :, :], in_=rope_d[:, :, l0:l0 + 512]), writes=[r_ropeq])
                        I("dve", lambda E, pa=pa, l0=l0: E.tensor_copy(out=QhT[0:64, l0:l0 + 512], in_=pa[0:64, :]), reads=[rpa], writes=[r_QhT])
                        I("dve", lambda E, pa=pa: E.tensor_tensor(out=tq[64:96, :], in0=pa[64:96, :], in1=ropeq[64:96, 0, :], op=ALU.mult), reads=[rpa, r_ropeq], writes=[r_tq])
                        I("dve", lambda E, pbb=pbb: E.tensor_tensor(out=ropeq[64:96, 1, :], in0=pbb[64:96, :], in1=ropeq[64:96, 1, :], op=ALU.mult), reads=[rpbb, r_ropeq], writes=[r_ropeq])
                        I("dve", lambda E, l0=l0: E.tensor_tensor(out=QhT[64:96, l0:l0 + 512], in0=tq[64:96, :], in1=ropeq[64:96, 1, :], op=ALU.add), reads=[r_tq, r_ropeq], writes=[r_QhT])
                    cut(32)
                    va, r_va = Vaug[par], r_Vaug[par]
                    voff = 0 if par == 0 else 64
                    for kb in range(5):
                        nk = 8 if kb < 4 else 2
                        pb, rpb = mbank()
                        for j in range(nk):
                            kt = kb * 8 + j
                            ti = 0 if kt < 2 else 1 + (kt - 2) // 4
                            for k in range(2):
                                I("pe", lambda E, pb=pb, j=j, k=k, kt=kt, h=h: E.matmul(pb[:, j * 64:(j + 1) * 64], lhsT=kvcT[:, k, kt * 128:(kt + 1) * 128], rhs=wuvb[:, k, h * 64:(h + 1) * 64], start=(k == 0), stop=(k == 1)),
                                  reads=[r_wuvb, r_kvcT[ti]], writes=[rpb], signal=(j == nk - 1 and k == 1))
                        I("dve", lambda E, pb=pb, kb=kb, nk=nk, va=va, voff=voff: E.tensor_copy(out=va[:, kb * 8:kb * 8 + nk, voff:voff + 64], in_=pb[:, 0:nk * 64].rearrange("p (j d) -> p j d", d=64)),
                          reads=[rpb], writes=[r_va])
                    cut(33)
                    pend = []

                    def emit_pv(it, h=h, par=par, va=va, r_va=r_va):
                        qt, kt, pt, rpt, po, rpo = it
                        l0 = qt * 512
                        I("pe", lambda E, po=po, pt=pt, kt=kt, va=va: E.matmul(po[:, :], lhsT=va[:, kt, :], rhs=pt[:, :], start=(kt == 0), stop=(kt == 33)),
                          reads=[r_va, rpt], writes=[rpo], signal=(kt == 33))
                        if kt != 33:
                            return
                        I("dve", lambda E, po=po: E.tensor_copy(out=Oun[:, :], in_=po[:, :]), reads=[rpo], writes=[r_Oun])
                        if par == 0:
                            Dm("sp", g_dsh, lambda E: E.dma_start(out=Dsh[0:64, :], in_=Oun[64:128, :]), reads=[r_Oun], writes=[r_Dsh])
                            lo_, hi_ = 0, 64
                        else:
                            Dm("sp", g_dsh, lambda E: E.dma_start(out=Dsh[64:128, :], in_=Oun[0:64, :]), reads=[r_Oun], writes=[r_Dsh])
                            lo_, hi_ = 64, 128
                        I("dve", lambda E, lo_=lo_, hi_=hi_: E.reciprocal(out=Dsh[lo_:hi_, :], in_=Dsh[lo_:hi_, :]), reads=[r_Dsh], writes=[r_Dsh])
                        I("dve", lambda E, lo_=lo_, hi_=hi_, h=h, l0=l0: E.tensor_tensor(out=attT[lo_:hi_, h // 2, l0:l0 + 512], in0=Oun[lo_:hi_, :], in1=Dsh[lo_:hi_, :], op=ALU.mult),
                          reads=[r_Oun, r_Dsh], writes=[r_attT[qt]])

                    for qt in range(8):
                        l0 = qt * 512
                        ocnt[0] += 1
                        po, rpo = Ob[ocnt[0] % 2]
                        for kt in range(34):
                            scnt[0] += 1
                            ps_, rps = Sb[scnt[0] % 3]
                            pt, rpt = PT[scnt[0] % 3], r_PT[scnt[0] % 3]
                            I("pe", lambda E, ps_=ps_, kt=kt, l0=l0: E.matmul(ps_[:, :], lhsT=KhT[0:96, kt * 128:(kt + 1) * 128], rhs=QhT[0:96, l0:l0 + 512], start=True, stop=True),
                              reads=[r_KhT, r_QhT], writes=[rps])
                            I("act", lambda E, ps_=ps_, pt=pt: E.activation(out=pt[:, :], in_=ps_[:, :], func=AF.Exp), reads=[rps], writes=[rpt])
                            pend.append((qt, kt, pt, rpt, po, rpo))
                            if len(pend) > 2:
                                emit_pv(pend.pop(0))
                    while pend:
                        emit_pv(pend.pop(0))
                dump("attT", attT[:, :, 0:512], r_attT[0], [128, 4, 512])
                fw.barrier()
                ar.release(m3)
            if stop_after >= 4:
                ar.release(mA)
                m4 = ar.mark()
                g_c4 = fw.group("const4", final=True)
                woutb = ar.alloc("woutb", [128, 8, D], BF16); r_woutb = Reg()
                wout_v = wout_d.rearrange("(k p) n -> p k n", p=128)
                for k in range(8):
                    Dm("pool", g_c4, lambda E, k=k: E.dma_start(out=woutb[:, k, :], in_=wout_v[:, k, :]), writes=[r_woutb])
                lnt = ar.alloc("lnt", [128, 4, D], F32); r_lnt = Reg()
                for i in range(4):
                    Dm("sp", g_c4, lambda E, i=i: E.dma_start(out=lnt[:, i, :], in_=lnp_d[i:i + 1, :].partition_broadcast(128)), writes=[r_lnt])
                xt2 = ar.alloc("xt2", [128, D], F32); r_xt2 = Reg(); g_xt2 = fw.group("xt2")
                r1 = ar.alloc("r1", [128, D], F32); r_r1 = Reg(); g_out = fw.group("outst")
                x1t = ar.alloc("x1t", [128, 4, D], F32); r_x1t = [Reg() for _ in range(4)]
                xh2 = ar.alloc("xh2", [128, D], BF16); r_xh2 = Reg()
                u2T = ar.alloc("u2T", [128, 8, 512], BF16); r_u2T = Reg()
                wgub = [ar.alloc("wgub%d" % i, [128, 2, 8, 128], BF16) for i in range(2)]; r_wgub = [Reg(), Reg()]
                g_wgu = [fw.group("wgu0"), fw.group("wgu1")]
                tmpa = ar.alloc("tmpa", [128, 512], F32); r_tmpa = Reg()
                tmpb = ar.alloc("tmpb", [128, 512], F32); r_tmpb = Reg()
                aT = ar.alloc("aT", [128, 22, 512], BF16); r_aT = Reg()
                wdnh = ar.alloc("wdnh", [128, 22, 512], BF16); r_wdnh = Reg(); g_wdn = fw.group("wdn")
                st6b = ar.alloc("st6b", [128, 2, 6], F32); r_st6b = Reg()
                mvb = ar.alloc("mvb", [128, 2], F32); r_mvb = Reg()
                rstdb = ar.alloc("rstdb", [128, 1], F32); r_rstdb = Reg()
                nbiasb = ar.alloc("nbiasb", [128, 1], F32); r_nbiasb = Reg()
                wgu_v = wgu_d.rearrange("(k p) n -> p k n", p=128)
                wdn_v = wdn_d.rearrange("(f p) n -> p f n", p=128)

                def ln_stats(src, r_src):
                    for hh in range(2):
                        I("dve", lambda E, hh=hh, src=src: E.bn_stats(out=st6b[:, hh, :], in_=src[:, hh * 512:(hh + 1) * 512]), reads=[r_src], writes=[r_st6b])
                    I("dve", lambda E: E.bn_aggr(out=mvb[:], in_=st6b[:].rearrange("p a b -> p (a b)")), reads=[r_st6b], writes=[r_mvb])
                    I("act", lambda E: E.activation(out=rstdb[:], in_=mvb[:, 1:2], func=AF.Sqrt, bias=epsc[:, 0:1], scale=1.0), reads=[r_mvb, r_epsc], writes=[r_rstdb])
                    I("dve", lambda E: E.reciprocal(out=rstdb[:], in_=rstdb[:]), reads=[r_rstdb], writes=[r_rstdb])
                    I("dve", lambda E: E.scalar_tensor_tensor(out=nbiasb[:], in0=mvb[:, 0:1], scalar=-1.0, in1=rstdb[:], op0=ALU.mult, op1=ALU.mult),
                      reads=[r_mvb, r_rstdb], writes=[r_nbiasb])

                fcount = [0]
                for tl in range(8):
                    l0 = tl * 512
                    for s in range(4):
                        ls = l0 + s * 128
                        ybanks = []
                        for hh in range(2):
                            pb, rpb = bank()
                            for k in range(8):
                                src_t = attT if k < 4 else s5T
                                rr = r_attT[tl] if k < 4 else r_s5T[tl]
                                I("pe", lambda E, pb=pb, k=k, hh=hh, ls=ls, src_t=src_t: E.matmul(pb[:, :], lhsT=src_t[:, k % 4, ls:ls + 128], rhs=woutb[:, k, hh * 512:(hh + 1) * 512], start=(k == 0), stop=(k == 7)),
                                  reads=[rr, r_woutb], writes=[rpb], signal=(k == 7))
                            ybanks.append((pb, rpb))
                        Dm("sp", g_xt2, lambda E, ls=ls: E.dma_start(out=xt2[:], in_=x_d[ls:ls + 128, :]), writes=[r_xt2])
                        for hh in range(2):
                            pb, rpb = ybanks[hh]
                            I("dve", lambda E, pb=pb, hh=hh: E.tensor_tensor(out=r1[:, hh * 512:(hh + 1) * 512], in0=pb[:, :], in1=g12b[:, 0, hh * 512:(hh + 1) * 512], op=ALU.mult),
                              reads=[rpb, r_g12b], writes=[r_r1])
                        I("dve", lambda E: E.scalar_tensor_tensor(out=r1[:], in0=xt2[:], scalar=ALPHA, in1=r1[:], op0=ALU.mult, op1=ALU.add), reads=[r_xt2, r_r1], writes=[r_r1])
                        ln_stats(r1, r_r1)
                        I("act", lambda E, s=s: E.activation(out=x1t[:, s, :], in_=r1[:], func=AF.Identity, bias=nbiasb[:, 0:1], scale=rstdb[:, 0:1]),
                          reads=[r_r1, r_rstdb, r_nbiasb], writes=[r_x1t[s]])
                        I("pool", lambda E, s=s: E.tensor_tensor(out=x1t[:, s, :], in0=x1t[:, s, :], in1=lnt[:, 0, :], op=ALU.mult), reads=[r_x1t[s], r_lnt], writes=[r_x1t[s]])
                        I("pool", lambda E, s=s: E.tensor_tensor(out=x1t[:, s, :], in0=x1t[:, s, :], in1=lnt[:, 1, :], op=ALU.add), reads=[r_x1t[s], r_lnt], writes=[r_x1t[s]])
                        ln_stats(x1t[:, s, :], r_x1t[s])
                        I("act", lambda E, s=s: E.activation(out=xh2[:], in_=x1t[:, s, :], func=AF.Identity, bias=nbiasb[:, 0:1], scale=rstdb[:, 0:1]),
                          reads=[r_x1t[s], r_rstdb, r_nbiasb], writes=[r_xh2])
                        for k in range(8):
                            I("pe", lambda E, k=k: E.transpose(out=ptr[:, k * 128:(k + 1) * 128], in_=xh2[:, k * 128:(k + 1) * 128], identity=ident[:, :]),
                              reads=[r_xh2, r_ident], writes=[r_ptr], signal=(k == 7))
                        for k in range(8):
                            if k % 2 == 0:
                                I("act", lambda E, k=k, s=s: E.activation(out=u2T[:, k, s * 128:(s + 1) * 128], in_=ptr[:, k * 128:(k + 1) * 128], func=AF.Identity,
                                                                       bias=modcol[:, k, 2:3], scale=modcol[:, k, 3:4]), reads=[r_ptr, r_modcol], writes=[r_u2T])
                            else:
                                I("dve", lambda E, k=k, s=s: E.tensor_scalar(out=u2T[:, k, s * 128:(s + 1) * 128], in0=ptr[:, k * 128:(k + 1) * 128],
                                                                          scalar1=modcol[:, k, 3:4], scalar2=modcol[:, k, 2:3], op0=ALU.mult, op1=ALU.add),
                                  reads=[r_ptr, r_modcol], writes=[r_u2T])
                    for f in range(22):
                        fcount[0] += 1
                        wb = fcount[0] % 2
                        for gu in range(2):
                            c = gu * 22 + f
                            Dm("sp", g_wgu[wb], lambda E, wb=wb, gu=gu, c=c: E.dma_start(out=wgub[wb][:, gu, :, :].rearrange("p k n -> p (k n)"), in_=wgu_bf[c]), reads=[r_wgubf], writes=[r_wgub[wb]])
                        pg, rpg = bank()
                        pu, rpu = bank()
                        for gu, (pp, rpp) in enumerate(((pg, rpg), (pu, rpu))):
                            for k in range(8):
                                I("pe", lambda E, pp=pp, k=k, gu=gu, wb=wb: E.matmul(pp[:, :], lhsT=wgub[wb][:, gu, k, :], rhs=u2T[:, k, :], start=(k == 0), stop=(k == 7)),
                                  reads=[r_wgub[wb], r_u2T], writes=[rpp], signal=(k == 7))
                        I("act", lambda E, pg=pg: E.activation(out=tmpa[:], in_=pg[:, :], func=AF.Silu), reads=[rpg], writes=[r_tmpa])
                        I("dve", lambda E, pu=pu, f=f: E.tensor_tensor(out=aT[:, f, :], in0=tmpa[:], in1=pu[:, :], op=ALU.mult), reads=[r_tmpa, rpu], writes=[r_aT])
                    for hh in range(2):
                        Dm("sp", g_wdn, lambda E, hh=hh: E.dma_start(out=wdnh[:].rearrange("p f n -> p (f n)"), in_=wdn_bf[hh]), reads=[r_wdnbf], writes=[r_wdnh])
                        for s in range(4):
                            pb, rpb = bank()
                            for f in range(22):
                                I("pe", lambda E, pb=pb, f=f, s=s: E.matmul(pb[:, :], lhsT=aT[:, f, s * 128:(s + 1) * 128], rhs=wdnh[:, f, :], start=(f == 0), stop=(f == 21)),
                                  reads=[r_aT, r_wdnh], writes=[rpb], signal=(f == 21))
                            I("dve", lambda E, pb=pb, hh=hh: E.tensor_tensor(out=tmpb[:], in0=pb[:, :], in1=g12b[:, 1, hh * 512:(hh + 1) * 512], op=ALU.mult), reads=[rpb, r_g12b], writes=[r_tmpb])
                            I("dve", lambda E, s=s, hh=hh: E.scalar_tensor_tensor(out=x1t[:, s, hh * 512:(hh + 1) * 512], in0=x1t[:, s, hh * 512:(hh + 1) * 512], scalar=ALPHA, in1=tmpb[:],
                                                                                op0=ALU.mult, op1=ALU.add), reads=[r_x1t[s], r_tmpb], writes=[r_x1t[s]])
                    for s in range(4):
                        ls = l0 + s * 128
                        ln_stats(x1t[:, s, :], r_x1t[s])
                        I("act", lambda E, s=s: E.activation(out=r1[:], in_=x1t[:, s, :], func=AF.Identity, bias=nbiasb[:, 0:1], scale=rstdb[:, 0:1]),
                          reads=[r_x1t[s], r_rstdb, r_nbiasb], writes=[r_r1])
                        I("pool", lambda E: E.tensor_tensor(out=r1[:], in0=r1[:], in1=lnt[:, 2, :], op=ALU.mult), reads=[r_r1, r_lnt], writes=[r_r1])
                        I("pool", lambda E: E.tensor_tensor(out=r1[:], in0=r1[:], in1=lnt[:, 3, :], op=ALU.add), reads=[r_r1, r_lnt], writes=[r_r1])
                        Dm("sp", g_out, lambda E, ls=ls: E.dma_start(out=out_d[ls:ls + 128, :], in_=r1[:]), reads=[r_r1])
                fw.barrier()
        except StopBuild:
            pass
        fw.barrier()
        fw.emit()
    return nc, used_inputs


def _rope_tables():
    rows = L // 64
    row = np.repeat(np.arange(rows), 64).astype(np.float32)
    col = np.tile(np.arange(64), rows).astype(np.float32)
    n_freq = 8
    freqs = (10000.0 ** (-np.arange(n_freq, dtype=np.float32) / n_freq)).astype(np.float32)
    ang = np.concatenate([row[:, None] * freqs, col[:, None] * freqs], axis=-1).astype(np.float32)
    cos = np.cos(ang).astype(np.float32)
    sin = np.sin(ang).astype(np.float32)
    t = np.zeros((32, 2, L), np.float32)
    for j in range(32):
        t[j, 0] = cos[:, j // 2]
        t[j, 1] = sin[:, j // 2] * (-1.0 if j % 2 == 0 else 1.0)
    return t


def host_shared(inp):
    sh = {}
    f = np.float32
    sh["w_ada"] = np.ascontiguousarray(inp["w_ada"][0], f)
    sh["b_ada"] = np.ascontiguousarray(inp["b_ada"][0][None, :], f)
    w_in = inp["w_in"][0]
    kr = w_in[:, 640:672]
    kr_sw = kr.reshape(D, 16, 2)[:, :, ::-1].reshape(D, 32)
    sh["w_in_l"] = np.ascontiguousarray(np.concatenate([w_in[:, :672], kr_sw, w_in[:, 672:], np.zeros((D, 64), f)], axis=1), f)
    wuq = inp["w_uq"][0]
    rope_sw = wuq[:, :, 64:96].reshape(QL, NH, 16, 2)[:, :, :, ::-1].reshape(QL, NH, 32)
    sh["w_uq_l"] = np.ascontiguousarray(np.concatenate([wuq, wuq[:, :, :64], rope_sw], axis=2).reshape(QL, NH * 192), f)
    wuk = inp["w_uk"][0]
    sh["w_uk_l"] = np.ascontiguousarray(np.concatenate([wuk, np.zeros((KVL, NH, 32), f)], axis=2).reshape(KVL, NH * 96), f)
    sh["w_uv_l"] = np.ascontiguousarray(inp["w_uv"][0].reshape(KVL, NH * 64), f)
    sh["qg_l"] = np.ascontiguousarray(inp["q_norm_g"][0].reshape(3, 128).T, f)
    sh["kvg_l"] = np.ascontiguousarray(inp["kv_norm_g"][0].reshape(2, 128).T, f)
    sh["rope_cs"] = _rope_tables()
    es_ = np.zeros((32, 96), f)
    es_[np.arange(32), 64 + np.arange(32)] = 1.0
    sh["esel"] = es_
    sh["ident"] = np.eye(128, dtype=f)
    lre, lim, ldt = inp["s5_lambda_re"][0], inp["s5_lambda_im"][0], inp["s5_log_dt"][0]
    bre, bim = inp["s5_b_re"][0], inp["s5_b_im"][0]
    cre, cim = inp["s5_c_re"][0], inp["s5_c_im"][0]
    col = np.zeros((128, 3, 32), f)
    rowl = np.zeros((128, 3, 1024), f)
    bl = np.zeros((128, 2, 1024), f)
    cl = np.zeros((128, 2, 32, 128), f)
    for d in range(2):
        for blk in range(4):
            for jj in range(4):
                up = (d * 4 + blk) * 4 + jj
                for gl in range(2):
                    g = blk * 8 + 2 * jj + gl
                    col[gl * 64:(gl + 1) * 64, 0, up] = lre[d, g]
                    col[gl * 64:(gl + 1) * 64, 1, up] = lim[d, g]
                    col[gl * 64:(gl + 1) * 64, 2, up] = ldt[d, g]
                    c0 = (d * 4 + blk) * 128 + gl * 64
                    rowl[32 * jj:32 * jj + 32, 0, c0:c0 + 64] = lre[d, g][None, :]
                    rowl[32 * jj:32 * jj + 32, 1, c0:c0 + 64] = lim[d, g][None, :]
                    rowl[32 * jj:32 * jj + 32, 2, c0:c0 + 64] = ldt[d, g]
                    p0 = 32 * jj + 16 * gl
                    bl[p0:p0 + 16, 0, c0:c0 + 64] = bre[d, g].T
                    bl[p0:p0 + 16, 1, c0:c0 + 64] = bim[d, g].T
                    m0 = 32 * jj + 16 * gl
                    cl[gl * 64:(gl + 1) * 64, 0, up, m0:m0 + 16] = cre[d, g].T
                    cl[gl * 64:(gl + 1) * 64, 1, up, m0:m0 + 16] = cim[d, g].T
    sh["s5_col"], sh["s5_row"], sh["s5_b_l"], sh["s5_c_l"] = col, rowl, bl, cl
    sh["s5_d_col"] = np.ascontiguousarray(inp["s5_d"][0].reshape(4, 128).T, f)
    sh["s5_w_glu"] = np.ascontiguousarray(inp["s5_w_glu"][0], f)
    sh["b_glu_col"] = np.ascontiguousarray(inp["s5_b_glu"][0].reshape(4, 128).T, f)
    sh["w_out"] = np.ascontiguousarray(inp["w_out"][0], f)
    sh["ln_params"] = np.ascontiguousarray(np.stack([inp["ln1_g"][0], inp["ln1_b"][0], inp["ln2_g"][0], inp["ln2_b"][0]]), f)
    sh["w_gate_up"] = np.ascontiguousarray(inp["w_gate_up"][0], f)
    sh["w_down"] = np.ascontiguousarray(inp["w_down"][0], f)
    return sh


def host_core(inp, b, sh):
    m = dict(sh)
    m["x"] = np.ascontiguousarray(inp["x"][b], np.float32)
    m["ctx"] = np.ascontiguousarray(inp["ctx"][b], np.float32)
    cc = np.stack([inp["c"][b], inp["c_ctx"]], axis=-1).astype(np.float32)
    m["cc"] = np.ascontiguousarray(cc.reshape(8, 128, 2).transpose(1, 0, 2).reshape(128, 16))
    return m


_NC = {}


def kernel(**inputs):
    inp = {k: np.asarray(v) for k, v in inputs.items()}
    if "nc" not in _NC:
        _NC["nc"], _NC["names"] = build()
    nc, names = _NC["nc"], set(_NC["names"])
    sh = host_shared(inp)
    in_maps = [{k: v for k, v in host_core(inp, b, sh).items() if k in names} for b in range(8)]
    res = run_bass_kernel_spmd(nc, in_maps, core_ids=list(range(8)))
    return np.stack([np.asarray(r["out"], np.float32) for r in res.results], axis=0)
```

```python
import numpy as np
import concourse.bass as bass
import concourse.mybir as mybir
from concourse.bass_utils import run_bass_kernel_spmd

F32 = mybir.dt.float32
BF16 = mybir.dt.bfloat16
AF = mybir.ActivationFunctionType
ALU = mybir.AluOpType

D = 1024
L = 4096
CTX = 256
T = CTX + L
NH = 8
QL = 384
KVL = 256
DIN = 1184
DFF = 2816
EPS = 1e-6
ALPHA = 2.0 ** 0.25
SCALE = 96.0 ** -0.5
MAGIC = 12582912.0
TWO_PI = float(2 * np.pi)
PI_LO = 3.1415925
NT = 9
SEG = 1088
NSEG = T // SEG


class Reg:
    __slots__ = ("w", "r")

    def __init__(self):
        self.w = None
        self.r = {}


class DmaGroup:
    def __init__(self, fw, name, final=False):
        self.fw = fw
        self.name = name
        self.sems = [fw.nc.alloc_semaphore(name="dg_" + name)]
        self.count = 0
        self.final = final

    @property
    def ep(self):
        return len(self.sems) - 1

    def roll(self):
        self.sems.append(self.fw.nc.alloc_semaphore(name="dg_%s_%d" % (self.name, len(self.sems))))
        self.count = 0


class FW:
    ENG = ("pe", "dve", "act", "pool", "sp")
    LIMIT = 900
    DLIMIT = 40

    def __init__(self, nc):
        self.nc = nc
        self.sems = {k: [nc.alloc_semaphore(name="es_" + k)] for k in self.ENG}
        self.cnt = {k: 0 for k in self.ENG}
        self.pending = {k: False for k in self.ENG}
        self.seen = {k: {} for k in self.ENG}
        self.prog = {k: [] for k in self.ENG}
        self.groups = []
        self.trace = {k: [] for k in self.ENG}

    def ep(self, e):
        return len(self.sems[e]) - 1

    def group(self, name, final=False):
        g = DmaGroup(self, name, final)
        self.groups.append(g)
        return g

    def _need(self, e, tok):
        if tok is None:
            return
        kind, obj, val = tok
        if kind == "dma":
            grp, gep = obj
            key = (id(grp), gep)
            sem = grp.sems[gep]
            if grp.final:
                if self.seen[e].get(key, 0):
                    return
                self.seen[e][key] = 1
                self.prog[e].append(lambda E, sem=sem, grp=grp: E.wait_ge(sem, 16 * grp.count))
                self.trace[e].append(("waitf", key, grp))
                return
            if self.seen[e].get(key, 0) >= val:
                return
            self.seen[e][key] = val
            self.prog[e].append(lambda E, sem=sem, val=val: E.wait_ge(sem, 16 * val))
            self.trace[e].append(("wait", key, val))
            return
        eng, eep = obj
        key = obj
        if self.seen[e].get(key, 0) >= val:
            return
        if eng == e and eep == self.ep(e) and val > self.cnt[e]:
            return
        self.seen[e][key] = val
        sem = self.sems[eng][eep]
        self.prog[e].append(lambda E, sem=sem, val=val: E.wait_ge(sem, val))
        self.trace[e].append(("wait", key, val))

    def _deps(self, e, reads, writes):
        for reg in reads:
            self._need(e, reg.w)
        for reg in writes:
            self._need(e, reg.w)
            for t in reg.r.values():
                self._need(e, t)

    def I(self, e, fn, reads=(), writes=(), signal=True):
        if self.cnt[e] >= self.LIMIT and not self.pending[e]:
            self.sems[e].append(self.nc.alloc_semaphore(name="es_%s_%d" % (e, len(self.sems[e]))))
            self.cnt[e] = 0
        self._deps(e, reads, writes)
        idx = self.cnt[e] + 1
        tok = ("eng", (e, self.ep(e)), idx)
        for reg in reads:
            reg.r[e] = tok
        for reg in writes:
            reg.w = tok
            reg.r = {}
        if signal:
            self.cnt[e] = idx
            self.pending[e] = False
            sem = self.sems[e][-1]
            self.prog[e].append(lambda E, fn=fn, sem=sem: fn(E).then_inc(sem, 1))
            self.trace[e].append(("inc", (e, self.ep(e))))
        else:
            self.pending[e] = True
            self.prog[e].append(lambda E, fn=fn: fn(E))

    def D(self, q, grp, fn, reads=(), writes=()):
        subs = grp.__dict__.setdefault("subs", {})
        if q not in subs:
            if not subs:
                subs[q] = grp
            else:
                sg = DmaGroup(self, grp.name + "_" + q, grp.final)
                if getattr(grp, "skip_barrier", False):
                    sg.skip_barrier = True
                self.groups.append(sg)
                subs[q] = sg
        grp = subs[q]
        if grp.count >= self.DLIMIT and not grp.final:
            grp.roll()
        def same(t):
            return t[0] == "dma" and t[1][0] is grp
        for reg in reads:
            if reg.w is not None and not same(reg.w):
                self._need(q, reg.w)
        for reg in writes:
            for t in [reg.w] + list(reg.r.values()):
                if t is not None and not same(t):
                    self._need(q, t)
        grp.count += 1
        tok = ("dma", (grp, grp.ep), grp.count)
        for reg in reads:
            reg.r["dma%d" % id(grp)] = tok
        for reg in writes:
            reg.w = tok
            reg.r = {}
        sem = grp.sems[-1]
        self.prog[q].append(lambda E, fn=fn, sem=sem: fn(E).then_inc(sem, 16))
        self.trace[q].append(("inc", (id(grp), grp.ep)))

    def wait_all(self, e, regs):
        for reg in regs:
            self._need(e, reg.w)
            for t in reg.r.values():
                self._need(e, t)

    def barrier(self):
        for e in self.ENG:
            for e2 in self.ENG:
                if e2 != e and self.cnt[e2] > 0:
                    self._need(e, ("eng", (e2, self.ep(e2)), self.cnt[e2]))
            for g in self.groups:
                if g.count > 0 and not getattr(g, "skip_barrier", False):
                    self._need(e, ("dma", (g, g.ep), g.count))

    def check(self):
        val = {}
        pc = {k: 0 for k in self.ENG}
        while True:
            prog = False
            for e in self.ENG:
                tr = self.trace[e]
                while pc[e] < len(tr):
                    op = tr[pc[e]]
                    if op[0] == "inc":
                        val[op[1]] = val.get(op[1], 0) + 1
                    elif op[0] == "wait":
                        if val.get(op[1], 0) < op[2]:
                            break
                    else:
                        if val.get(op[1], 0) < op[2].count:
                            break
                    pc[e] += 1
                    prog = True
            if not prog:
                break
        stuck = {e: (pc[e], len(self.trace[e]), self.trace[e][pc[e]]) for e in self.ENG if pc[e] < len(self.trace[e])}
        assert not stuck, "SYNC DEADLOCK: %s" % stuck

    def emit(self):
        nc = self.nc
        self.check()
        with nc.Block() as block:
            @block.tensor
            def _(E):
                for f in self.prog["pe"]:
                    f(E)

            @block.vector
            def _(E):
                for f in self.prog["dve"]:
                    f(E)

            @block.scalar
            def _(E):
                for f in self.prog["act"]:
                    f(E)

            @block.gpsimd
            def _(E):
                for f in self.prog["pool"]:
                    f(E)

            @block.sync
            def _(E):
                for f in self.prog["sp"]:
                    f(E)


class Arena:
    LO = 20608
    HI = 228352

    def __init__(self, nc):
        self.nc = nc
        self.lo = self.LO
        self.hi = self.HI
        self.n = 0

    def _sz(self, shape, dt):
        sz = int(np.prod(shape[1:])) * (4 if dt == F32 else 2)
        return (sz + 63) // 64 * 64

    def alloc(self, name, shape, dt, top=False):
        sz = self._sz(shape, dt)
        self.n += 1
        if top:
            self.hi -= sz
            off = self.hi
        else:
            off = self.lo
            self.lo += sz
        assert self.lo <= self.hi, "SBUF arena overflow at %s: lo=%d hi=%d" % (name, self.lo, self.hi)
        return self.nc.alloc_sbuf_tensor_at("%s_%d" % (name, self.n), list(shape), dt, offset=off)

    def mark(self):
        return self.lo

    def release(self, m):
        self.lo = m


class StopBuild(Exception):
    pass


import os
CUT = int(os.environ.get("CUT", "0"))


def cut(n):
    if CUT == n:
        raise StopBuild()


def build(dbg=(), stop_after=99):
    nc = bass.Bass("TRN2", target_bir_lowering=False)
    fw = FW(nc)
    ar = Arena(nc)
    I, Dm = fw.I, fw.D

    used_inputs = []

    class LazyIn:
        def __init__(self, name, shape):
            self.name, self.shape, self._ap = name, shape, None

        def ap(self):
            if self._ap is None:
                used_inputs.append(self.name)
                self._ap = nc.dram_tensor(self.name, list(self.shape), F32, kind="ExternalInput").ap()
            return self._ap

        def __getitem__(self, k):
            return self.ap()[k]

        def rearrange(self, *a, **kw):
            return self.ap().rearrange(*a, **kw)

    def din(name, shape, dt=F32, stage=0):
        return LazyIn(name, shape)

    x_d = din("x", [L, D], stage=1)
    ctx_d = din("ctx", [CTX, D], stage=1)
    cc_d = din("cc", [128, 16], stage=0)
    wada_d = din("w_ada", [D, 6 * D], stage=0)
    bada_d = din("b_ada", [1, 6 * D], stage=0)
    win_d = din("w_in_l", [D, 1280], stage=1)
    wuq_d = din("w_uq_l", [QL, NH * 192], stage=3)
    wuk_d = din("w_uk_l", [KVL, NH * 96], stage=3)
    wuv_d = din("w_uv_l", [KVL, NH * 64], stage=3)
    qg_d = din("qg_l", [128, 3], stage=3)
    kvg_d = din("kvg_l", [128, 2], stage=3)
    rope_d = din("rope_cs", [32, 2, L], stage=1)
    esel_d = din("esel", [32, 96], stage=3)
    ident_d = din("ident", [128, 128], stage=0)
    s5col_d = din("s5_col", [128, 3, 32], stage=2)
    s5row_d = din("s5_row", [128, 3, 1024], stage=2)
    s5b_d = din("s5_b_l", [128, 2, 1024], stage=2)
    s5c_d = din("s5_c_l", [128, 2, 32, 128], stage=2)
    s5d_d = din("s5_d_col", [128, 4], stage=2)
    wglu_d = din("s5_w_glu", [512, 512], stage=2)
    bglu_d = din("b_glu_col", [128, 4], stage=2)
    wout_d = din("w_out", [D, D], stage=4)
    lnp_d = din("ln_params", [4, D], stage=4)
    wgu_d = din("w_gate_up", [D, 2 * DFF], stage=4)
    wdn_d = din("w_down", [DFF, D], stage=4)
    out_d = nc.dram_tensor("out", [L, D], F32, kind="ExternalOutput").ap()
    dbg_d = {}

    def dbg_out(name, shape):
        dbg_d[name] = nc.dram_tensor("dbg_" + name, list(shape), F32, kind="ExternalOutput").ap()
        return dbg_d[name]

    g_const = fw.group("const", final=True)
    g_dbg = fw.group("dbg")
    dbg_regs = []

    def dump(name, src_ap, reg, shape):
        if name not in dbg:
            return
        o = dbg_out(name, shape)
        Dm("pool", g_dbg, lambda E: E.dma_start(out=o, in_=src_ap), reads=[reg])
        dbg_regs.append(reg)

    from contextlib import ExitStack
    es = ExitStack()
    with es:
        ptr = es.enter_context(nc.psum_tensor("ptr", [128, 1024], BF16))
        r_ptr = Reg()
        pbank = [es.enter_context(nc.psum_tensor("pb%d" % i, [128, 512], F32)) for i in range(7)]
        r_pb = [Reg() for _ in range(7)]
        pb_next = [0]

        def bank():
            i = pb_next[0] % 7
            pb_next[0] += 1
            return pbank[i], r_pb[i]

        ident = ar.alloc("ident", [128, 128], BF16); r_ident = Reg()
        identf = ar.alloc("identf", [128, 128], F32); r_identf = Reg()
        Dm("pool", g_const, lambda E: E.dma_start(out=ident[:], in_=ident_d[:, :]), writes=[r_ident])
        Dm("sp", g_const, lambda E: E.dma_start(out=identf[:], in_=ident_d[:, :]), writes=[r_identf])
        onesr = ar.alloc("onesr", [1, 128], F32); r_onesr = Reg()
        I("pool", lambda E: E.memset(onesr[:], 1.0), writes=[r_onesr])
        onesb = ar.alloc("onesb", [128, 128], BF16); r_onesb = Reg()
        I("pool", lambda E: E.memset(onesb[:], 1.0), writes=[r_onesb])
        epsc = ar.alloc("epsc", [128, 1], F32); r_epsc = Reg()
        I("pool", lambda E: E.memset(epsc[:], EPS), writes=[r_epsc])
        halfpi = ar.alloc("halfpi", [128, 1], F32); r_halfpi = Reg()
        I("pool", lambda E: E.memset(halfpi[:], float(np.pi / 2)), writes=[r_halfpi])
        modcol = ar.alloc("modcol", [128, 8, 6], F32); r_modcol = Reg()
        g12b = ar.alloc("g12b", [128, 2, D], F32); r_g12b = Reg()

        try:
            m0 = ar.mark()
            cct = ar.alloc("cct", [128, 16], F32); r_cct = Reg()
            Dm("sp", g_const, lambda E: E.dma_start(out=cct[:], in_=cc_d[:, :]), writes=[r_cct])
            scb = ar.alloc("scb", [128, 8, 32], BF16); r_scb = Reg()
            I("pool", lambda E: E.memset(scb[:], 0.0), writes=[r_scb])
            I("act", lambda E: E.activation(out=scb[:, :, 0:2], in_=cct[:].rearrange("p (k j) -> p k j", j=2), func=AF.Silu), reads=[r_cct], writes=[r_scb])
            badat = ar.alloc("badat", [2, 6 * D], F32); r_badat = Reg()
            Dm("sp", g_const, lambda E: E.dma_start(out=badat[:], in_=bada_d[0:1, :].partition_broadcast(2)), writes=[r_badat])
            cut(1)
            modrow = ar.alloc("modrow", [2, 6 * D], F32); r_modrow = Reg()
            wa = [ar.alloc("wa%d" % i, [128, 8, 512], BF16) for i in range(2)]
            r_wa = [Reg(), Reg()]
            g_wa = [fw.group("wa0"), fw.group("wa1")]
            wada_v = wada_d.rearrange("(k p) n -> p k n", p=128)
            for cch in range(12):
                b = cch % 2
                Dm("pool", g_wa[b], lambda E, b=b, cch=cch: E.dma_start(out=wa[b][:], in_=wada_v[:, :, cch * 512:(cch + 1) * 512]), writes=[r_wa[b]])
                VAR = int(os.environ.get("VAR", "0"))
                if VAR == 1:
                    continue
                pb, rpb = bank()
                for k in range(8):
                    I("pe", lambda E, pb=pb, b=b, k=k: E.matmul(pb[0:32, :], lhsT=scb[:, k, :], rhs=wa[b][:, k, :], start=(k == 0), stop=(k == 7)),
                      reads=[r_scb, r_wa[b]], writes=[rpb], signal=(k == 7 or VAR == 3))
                if VAR in (2, 3):
                    continue
                I("dve", lambda E, pb=pb, cch=cch: E.tensor_tensor(out=modrow[:, cch * 512:(cch + 1) * 512], in0=pb[0:2, :], in1=badat[:, cch * 512:(cch + 1) * 512], op=ALU.add),
                  reads=[rpb, r_badat], writes=[r_modrow])
            cut(2)
            for c0 in (D, 4 * D):
                I("dve", lambda E, c0=c0: E.tensor_scalar(out=modrow[:, c0:c0 + D], in0=modrow[:, c0:c0 + D], scalar1=1.0, scalar2=None, op0=ALU.add),
                  reads=[r_modrow], writes=[r_modrow])
            cut(3)
            pb, rpb = bank()
            srcs = [0, 1, 3, 4]
            for k in range(8):
                for jj, s in enumerate(srcs):
                    col = (k * 4 + jj) * 2
                    I("pe", lambda E, pb=pb, k=k, s=s, col=col: E.transpose(out=pb[:, col:col + 2], in_=modrow[0:2, s * D + k * 128:s * D + (k + 1) * 128], identity=identf[0:2, 0:2]),
                      reads=[r_modrow, r_identf], writes=[rpb], signal=(k == 7 and jj == 3))
            cut(4)
            pbv = pb[:, 0:64].rearrange("p (k j two) -> p k j two", k=8, j=4, two=2)
            I("dve", lambda E: E.tensor_copy(out=modcol[:, :, 0:4], in_=pbv[:, :, :, 0]), reads=[rpb], writes=[r_modcol])
            I("dve", lambda E: E.tensor_copy(out=modcol[:, :, 4:6], in_=pbv[:, :, 0:2, 1]), reads=[rpb], writes=[r_modcol])
            cut(5)
            for gi, c0 in enumerate((2 * D, 5 * D)):
                for hh in range(2):
                    pb, rpb = bank()
                    I("pe", lambda E, pb=pb, c0=c0, hh=hh: E.matmul(pb[:, :], lhsT=onesr[0:1, :], rhs=modrow[0:1, c0 + hh * 512:c0 + (hh + 1) * 512], start=True, stop=True),
                      reads=[r_onesr, r_modrow], writes=[rpb])
                    I("act", lambda E, pb=pb, gi=gi, hh=hh: E.activation(out=g12b[:, gi, hh * 512:(hh + 1) * 512], in_=pb[:, :], func=AF.Identity),
                      reads=[rpb], writes=[r_g12b])
            dump("modcol", modcol[:].rearrange("p k j -> p (k j)"), r_modcol, [128, 48])
            dump("g12b", g12b[:].rearrange("p a d -> p (a d)"), r_g12b, [128, 2 * D])
            fw.barrier()
            ar.release(m0)
            g_const = fw.group("const1", final=True)

            if stop_after >= 1:
                mA = ar.mark()
                kvcT = ar.alloc("kvcT", [128, 2, T], BF16); r_kvcT = [Reg() for _ in range(NT)]
                krT = ar.alloc("krT", [32, T], BF16); r_krT = [Reg() for _ in range(NT)]
                qcT = ar.alloc("qcT", [128, 3, L], BF16); r_qcT = [Reg() for _ in range(NT)]
                mB = ar.mark()
                us5T = ar.alloc("us5T", [128, 4, T], BF16); r_us5T = [Reg() for _ in range(NT)]
                m1a = ar.mark()
                winb = ar.alloc("winb", [128, 8, 1280], BF16); r_winb = Reg()
                win_v = win_d.rearrange("(k p) n -> p k n", p=128)
                for k in range(8):
                    Dm("pool", g_const, lambda E, k=k: E.dma_start(out=winb[:, k, :], in_=win_v[:, k, :]), writes=[r_winb])
                cut(10)
                wgu_bf = nc.dram_tensor("wgu_bf", [44, 128, 1024], BF16).ap()
                wdn_bf = nc.dram_tensor("wdn_bf", [2, 128, 22 * 512], BF16).ap()
                r_wgubf = Reg(); r_wdnbf = Reg()
                g_pre = fw.group("precast", final=True)
                g_pre.skip_barrier = True
                wgu_v0 = wgu_d.rearrange("(k p) n -> p k n", p=128)
                wdn_v0 = wdn_d.rearrange("(f p) n -> p f n", p=128)
                for c in range(44):
                    gu_, f_ = c // 22, c % 22
                    c0 = gu_ * DFF + f_ * 128
                    Dm("pool", g_pre, lambda E, c=c, c0=c0: E.dma_start(out=wgu_bf[c].rearrange("p (k n) -> p k n", k=8), in_=wgu_v0[:, :, c0:c0 + 128]), writes=[r_wgubf])
                for hh in range(2):
                    Dm("pool", g_pre, lambda E, hh=hh: E.dma_start(out=wdn_bf[hh].rearrange("p (f n) -> p f n", f=22), in_=wdn_v0[:, :, hh * 512:(hh + 1) * 512]), writes=[r_wdnbf])
                xt = [ar.alloc("xt%d" % i, [128, D], F32) for i in range(2)]; r_xt = [Reg(), Reg()]
                g_xt = [fw.group("xt0"), fw.group("xt1")]
                xh = ar.alloc("xh", [128, D], BF16); r_xh = Reg()
                uT = ar.alloc("uT", [128, 8, 512], BF16); r_uT = Reg()
                st6 = ar.alloc("st6", [128, 2, 6], F32); r_st6 = Reg()
                mv = ar.alloc("mv", [128, 2], F32); r_mv = Reg()
                rstd = ar.alloc("rstd", [128, 1], F32); r_rstd = Reg()
                nbias = ar.alloc("nbias", [128, 1], F32); r_nbias = Reg()
                ropet = ar.alloc("ropet", [32, 2, 512], F32); r_ropet = Reg()
                g_rope = fw.group("rope")
                sq = [ar.alloc("sq%d" % i, [128, 512], BF16) for i in range(2)]; r_sq = [Reg(), Reg()]
                Rn = ar.alloc("Rn", [128, 512], F32); r_Rn = Reg()
                tmpk = ar.alloc("tmpk", [32, 512], F32); r_tmpk = Reg()

                def ln_rows(src, r_src, dst_bf, r_dst):
                    for hh in range(2):
                        I("dve", lambda E, hh=hh: E.bn_stats(out=st6[:, hh, :], in_=src[:, hh * 512:(hh + 1) * 512]), reads=[r_src], writes=[r_st6])
                    I("dve", lambda E: E.bn_aggr(out=mv[:], in_=st6[:].rearrange("p a b -> p (a b)")), reads=[r_st6], writes=[r_mv])
                    I("act", lambda E: E.activation(out=rstd[:], in_=mv[:, 1:2], func=AF.Sqrt, bias=epsc[:, 0:1], scale=1.0), reads=[r_mv, r_epsc], writes=[r_rstd])
                    I("dve", lambda E: E.reciprocal(out=rstd[:], in_=rstd[:]), reads=[r_rstd], writes=[r_rstd])
                    I("dve", lambda E: E.scalar_tensor_tensor(out=nbias[:], in0=mv[:, 0:1], scalar=-1.0, in1=rstd[:], op0=ALU.mult, op1=ALU.mult),
                      reads=[r_mv, r_rstd], writes=[r_nbias])
                    I("act", lambda E: E.activation(out=dst_bf[:], in_=src[:], func=AF.Identity, bias=nbias[:, 0:1], scale=rstd[:, 0:1]),
                      reads=[r_src, r_rstd, r_nbias], writes=[r_dst])

                def rms_scale(pbs, rpbs, nrows_total, n):
                    pS, rpS = bank()
                    for i, (pb, rpb) in enumerate(zip(pbs, rpbs)):
                        sb = i % 2
                        I("act", lambda E, pb=pb, sb=sb: E.activation(out=sq[sb][:, :n], in_=pb[:, :n], func=AF.Square), reads=[rpb], writes=[r_sq[sb]])
                        I("pe", lambda E, sb=sb, i=i, pS=pS: E.matmul(pS[:, :n], lhsT=onesb[:, :], rhs=sq[sb][:, :n], start=(i == 0), stop=(i == len(pbs) - 1)),
                          reads=[r_onesb, r_sq[sb]], writes=[rpS])
                    I("act", lambda E, pS=pS: E.activation(out=Rn[:, :n], in_=pS[:, :n], func=AF.Sqrt, bias=epsc[:, 0:1], scale=1.0 / nrows_total),
                      reads=[rpS, r_epsc], writes=[r_Rn])
                    I("dve", lambda E: E.reciprocal(out=Rn[:, :n], in_=Rn[:, :n]), reads=[r_Rn], writes=[r_Rn])

                sub = 0
                for ti in range(NT):
                    n = 256 if ti == 0 else 512
                    t0 = 0 if ti == 0 else CTX + (ti - 1) * 512
                    nsub = n // 128
                    for si in range(nsub):
                        b = sub % 2
                        sub += 1
                        if ti == 0:
                            src = ctx_d[si * 128:(si + 1) * 128, :]
                        else:
                            l0 = (ti - 1) * 512 + si * 128
                            src = x_d[l0:l0 + 128, :]
                        Dm("sp", g_xt[b], lambda E, b=b, src=src: E.dma_start(out=xt[b][:], in_=src), writes=[r_xt[b]])
                        ln_rows(xt[b], r_xt[b], xh, r_xh)
                        for k in range(8):
                            I("pe", lambda E, k=k: E.transpose(out=ptr[:, k * 128:(k + 1) * 128], in_=xh[:, k * 128:(k + 1) * 128], identity=ident[:, :]),
                              reads=[r_xh, r_ident], writes=[r_ptr], signal=(k == 7))
                        j0 = 4 if ti == 0 else 0
                        for k in range(8):
                            eng = "act" if k % 2 == 0 else "dve"
                            if eng == "act":
                                I("act", lambda E, k=k, si=si, j0=j0: E.activation(out=uT[:, k, si * 128:(si + 1) * 128], in_=ptr[:, k * 128:(k + 1) * 128], func=AF.Identity,
                                                                                   bias=modcol[:, k, j0:j0 + 1], scale=modcol[:, k, j0 + 1:j0 + 2]),
                                  reads=[r_ptr, r_modcol], writes=[r_uT])
                            else:
                                I("dve", lambda E, k=k, si=si, j0=j0: E.tensor_scalar(out=uT[:, k, si * 128:(si + 1) * 128], in0=ptr[:, k * 128:(k + 1) * 128],
                                                                                      scalar1=modcol[:, k, j0 + 1:j0 + 2], scalar2=modcol[:, k, j0:j0 + 1], op0=ALU.mult, op1=ALU.add),
                                  reads=[r_ptr, r_modcol], writes=[r_uT])
                    cut(11)
                    if ti == 1:
                        cut(25)

                    def proj(c0, m, n=n):
                        pb, rpb = bank()
                        for k in range(8):
                            I("pe", lambda E, pb=pb, k=k, c0=c0, m=m, n=n: E.matmul(pb[0:m, :n], lhsT=winb[:, k, c0:c0 + m], rhs=uT[:, k, :n], start=(k == 0), stop=(k == 7)),
                              reads=[r_winb, r_uT], writes=[rpb], signal=(k == 7))
                        return pb, rpb

                    if ti > 0:
                        l0 = (ti - 1) * 512
                        pq = [proj(i * 128, 128) for i in range(3)]
                        rms_scale([p[0] for p in pq], [p[1] for p in pq], float(QL), n)
                        for i in range(3):
                            I("dve", lambda E, i=i, l0=l0, pq=pq: E.tensor_tensor(out=qcT[:, i, l0:l0 + 512], in0=pq[i][0][:, :], in1=Rn[:, :], op=ALU.mult),
                              reads=[pq[i][1], r_Rn], writes=[r_qcT[ti]])
                    if ti == 1:
                        cut(26)
                    pk = [proj(QL + i * 128, 128) for i in range(2)]
                    cut(20)
                    if ti == 1:
                        cut(27)
                    rms_scale([p[0] for p in pk], [p[1] for p in pk], float(KVL), n)
                    cut(21)
                    if ti == 1:
                        cut(28)
                    for i in range(2):
                        I("dve", lambda E, i=i, n=n, t0=t0, pk=pk: E.tensor_tensor(out=kvcT[:, i, t0:t0 + n], in0=pk[i][0][:, :n], in1=Rn[:, :n], op=ALU.mult),
                          reads=[pk[i][1], r_Rn], writes=[r_kvcT[ti]])
                    cut(22)
                    if ti == 1:
                        cut(12)
                    pr, rpr = proj(QL + KVL, 32)
                    if ti == 0:
                        I("act", lambda E, pr=pr, n=n, t0=t0: E.activation(out=krT[:, t0:t0 + n], in_=pr[0:32, :n], func=AF.Identity), reads=[rpr], writes=[r_krT[ti]])
                    else:
                        ps_, rps = proj(QL + KVL + 32, 32)
                        l0 = (ti - 1) * 512
                        Dm("sp", g_rope, lambda E, l0=l0: E.dma_start(out=ropet[:], in_=rope_d[:, :, l0:l0 + 512]), writes=[r_ropet])
                        I("dve", lambda E, pr=pr: E.tensor_tensor(out=tmpk[:], in0=pr[0:32, :], in1=ropet[:, 0, :], op=ALU.mult), reads=[rpr, r_ropet], writes=[r_tmpk])
                        I("dve", lambda E, ps_=ps_: E.tensor_tensor(out=ropet[:, 1, :], in0=ps_[0:32, :], in1=ropet[:, 1, :], op=ALU.mult), reads=[rps, r_ropet], writes=[r_ropet])
                        I("dve", lambda E, t0=t0: E.tensor_tensor(out=krT[:, t0:t0 + 512], in0=tmpk[:], in1=ropet[:, 1, :], op=ALU.add), reads=[r_tmpk, r_ropet], writes=[r_krT[ti]])
                    cut(23)
                    if ti == 1:
                        cut(13)
                    for i in range(4):
                        p5, rp5 = proj(QL + KVL + 64 + i * 128, 128)
                        if i % 2 == 0:
                            I("act", lambda E, p5=p5, i=i, n=n, t0=t0: E.activation(out=us5T[:, i, t0:t0 + n], in_=p5[:, :n], func=AF.Identity), reads=[rp5], writes=[r_us5T[ti]])
                        else:
                            I("dve", lambda E, p5=p5, i=i, n=n, t0=t0: E.tensor_copy(out=us5T[:, i, t0:t0 + n], in_=p5[:, :n]), reads=[rp5], writes=[r_us5T[ti]])
                dump("qcT", qcT[:, :, 0:1024], r_qcT[1], [128, 3, 1024])
                dump("kvcT", kvcT[:, :, 0:1024], r_kvcT[1], [128, 2, 1024])
                dump("krT", krT[:, 0:1024], r_krT[1], [32, 1024])
                dump("us5T", us5T[:, :, 0:1024], r_us5T[1], [128, 4, 1024])
                fw.barrier()
                ar.release(m1a)

            s5T = ar.alloc("s5T", [128, 4, L], BF16, top=True); r_s5T = [Reg() for _ in range(8)]
            if os.environ.get("ZERO_S5"):
                I("pool", lambda E: E.memset(s5T[:], 0.0), writes=r_s5T)
            if stop_after >= 2 and not os.environ.get("ZERO_S5"):
                m2 = ar.mark()
                g_c2 = fw.group("const2", final=True)
                W = 1024
                colp = ar.alloc("colp", [128, 3, 32], F32); r_colp = Reg()
                dcol = ar.alloc("dcol", [128, 4], F32); r_dcol = Reg()
                bgl = ar.alloc("bgl", [128, 4], F32); r_bgl = Reg()
                CTb = ar.alloc("CTb", [128, 2, 32, 128], BF16); r_CTb = Reg()
                wglub = ar.alloc("wglub", [128, 4, 512], BF16); r_wglub = Reg()
                Dm("sp", g_c2, lambda E: E.dma_start(out=colp[:], in_=s5col_d[:, :, :]), writes=[r_colp])
                Dm("sp", g_c2, lambda E: E.dma_start(out=dcol[:], in_=s5d_d[:, :]), writes=[r_dcol])
                Dm("sp", g_c2, lambda E: E.dma_start(out=bgl[:], in_=bglu_d[:, :]), writes=[r_bgl])
                for c in range(2):
                    Dm("pool", g_c2, lambda E, c=c: E.dma_start(out=CTb[:, c, :, :], in_=s5c_d[:, c, :, :]), writes=[r_CTb])
                wglu_v = wglu_d.rearrange("(k p) n -> p k n", p=128)
                for k in range(4):
                    Dm("pool", g_c2, lambda E, k=k: E.dma_start(out=wglub[:, k, :], in_=wglu_v[:, k, :]), writes=[r_wglub])
                BbT = ar.alloc("BbT", [128, 2, W], BF16); r_BbT = Reg()
                rcol = ar.alloc("rcol", [128, 32], F32); r_rcol = Reg()
                thcol = ar.alloc("thcol", [128, 32], F32); r_thcol = Reg()
                m2b = ar.mark()
                prm = ar.alloc("prm", [128, 3, W], F32); r_prm = Reg()
                bl_ = ar.alloc("bl_", [128, 2, W], F32); r_bl = Reg()
                Dm("sp", g_c2, lambda E: E.dma_start(out=prm[:], in_=s5row_d[:, :, :]), writes=[r_prm])
                Dm("sp", g_c2, lambda E: E.dma_start(out=bl_[:], in_=s5b_d[:, :, :]), writes=[r_bl])
                tt = [ar.alloc("s5t%d" % i, [128, W], F32) for i in range(8)]
                r_tt = [Reg() for _ in range(8)]

                def ew(eng, fn, rd, wr):
                    I(eng, fn, reads=[r_tt[i] for i in rd] + [r_prm, r_bl], writes=[r_tt[i] for i in wr])

                LRE, LIM, LDT = prm[:, 0, :], prm[:, 1, :], prm[:, 2, :]
                DT, MAG, TH, KK, SN, CS, DEN, AM1 = [t[:] for t in tt]
                ew("act", lambda E: E.activation(out=DT, in_=LDT, func=AF.Exp), [], [0])
                ew("dve", lambda E: E.tensor_tensor(out=MAG, in0=LRE, in1=DT, op=ALU.mult), [0], [1])
                ew("act", lambda E: E.activation(out=MAG, in_=MAG, func=AF.Exp), [1], [1])
                ew("dve", lambda E: E.tensor_tensor(out=TH, in0=LIM, in1=DT, op=ALU.mult), [0], [2])
                ew("dve", lambda E: E.tensor_scalar(out=KK, in0=TH, scalar1=1.0 / TWO_PI, scalar2=MAGIC, op0=ALU.mult, op1=ALU.add), [2], [3])
                ew("dve", lambda E: E.tensor_scalar(out=KK, in0=KK, scalar1=MAGIC, scalar2=-TWO_PI, op0=ALU.subtract, op1=ALU.mult), [3], [3])
                ew("dve", lambda E: E.tensor_tensor(out=TH, in0=TH, in1=KK, op=ALU.add), [2, 3], [2])
                ew("dve", lambda E: E.tensor_scalar(out=TH, in0=TH, scalar1=PI_LO, scalar2=-PI_LO, op0=ALU.min, op1=ALU.max), [2], [2])
                ew("act", lambda E: E.activation(out=SN, in_=TH, func=AF.Sin), [2], [4])
                ew("act", lambda E: E.activation(out=KK, in_=TH, func=AF.Abs), [2], [3])
                I("act", lambda E: E.activation(out=CS, in_=KK, func=AF.Sin, bias=halfpi[:, 0:1], scale=-1.0), reads=[r_tt[3], r_halfpi], writes=[r_tt[5]])
                ew("dve", lambda E: E.tensor_tensor(out=CS, in0=CS, in1=MAG, op=ALU.mult), [5, 1], [5])
                ew("dve", lambda E: E.tensor_tensor(out=SN, in0=SN, in1=MAG, op=ALU.mult), [4, 1], [4])
                ew("dve", lambda E: E.tensor_tensor(out=DEN, in0=LRE, in1=LRE, op=ALU.mult), [], [6])
                ew("dve", lambda E: E.tensor_tensor(out=KK, in0=LIM, in1=LIM, op=ALU.mult), [], [3])
                ew("dve", lambda E: E.tensor_tensor(out=DEN, in0=DEN, in1=KK, op=ALU.add), [6, 3], [6])
                ew("dve", lambda E: E.reciprocal(out=DEN, in_=DEN), [6], [6])
                ew("dve", lambda E: E.tensor_scalar(out=AM1, in0=CS, scalar1=-1.0, scalar2=None, op0=ALU.add), [5], [7])
                ew("dve", lambda E: E.tensor_tensor(out=DT, in0=AM1, in1=LRE, op=ALU.mult), [7], [0])
                ew("dve", lambda E: E.tensor_tensor(out=KK, in0=SN, in1=LIM, op=ALU.mult), [4], [3])
                ew("dve", lambda E: E.tensor_tensor(out=DT, in0=DT, in1=KK, op=ALU.add), [0, 3], [0])
                ew("dve", lambda E: E.tensor_tensor(out=DT, in0=DT, in1=DEN, op=ALU.mult), [0, 6], [0])
                ew("dve", lambda E: E.tensor_tensor(out=MAG, in0=SN, in1=LRE, op=ALU.mult), [4], [1])
                ew("dve", lambda E: E.tensor_tensor(out=KK, in0=AM1, in1=LIM, op=ALU.mult), [7], [3])
                ew("dve", lambda E: E.tensor_tensor(out=MAG, in0=MAG, in1=KK, op=ALU.subtract), [1, 3], [1])
                ew("dve", lambda E: E.tensor_tensor(out=MAG, in0=MAG, in1=DEN, op=ALU.mult), [1, 6], [1])
                BRE, BIM = bl_[:, 0, :], bl_[:, 1, :]
                ew("dve", lambda E: E.tensor_tensor(out=TH, in0=DT, in1=BRE, op=ALU.mult), [0], [2])
                ew("dve", lambda E: E.tensor_tensor(out=KK, in0=MAG, in1=BIM, op=ALU.mult), [1], [3])
                I("dve", lambda E: E.tensor_tensor(out=BbT[:, 0, :], in0=TH, in1=KK, op=ALU.subtract), reads=[r_tt[2], r_tt[3]], writes=[r_BbT])
                ew("dve", lambda E: E.tensor_tensor(out=TH, in0=DT, in1=BIM, op=ALU.mult), [0], [2])
                ew("dve", lambda E: E.tensor_tensor(out=KK, in0=MAG, in1=BRE, op=ALU.mult), [1], [3])
                I("dve", lambda E: E.tensor_tensor(out=BbT[:, 1, :], in0=TH, in1=KK, op=ALU.add), reads=[r_tt[2], r_tt[3]], writes=[r_BbT])
                I("act", lambda E: E.activation(out=colp[:, 2, :], in_=colp[:, 2, :], func=AF.Exp), reads=[r_colp], writes=[r_colp])
                I("dve", lambda E: E.tensor_tensor(out=rcol[:], in0=colp[:, 0, :], in1=colp[:, 2, :], op=ALU.mult), reads=[r_colp], writes=[r_rcol])
                I("act", lambda E: E.activation(out=rcol[:], in_=rcol[:], func=AF.Exp), reads=[r_rcol], writes=[r_rcol])
                I("dve", lambda E: E.tensor_tensor(out=thcol[:], in0=colp[:, 1, :], in1=colp[:, 2, :], op=ALU.mult), reads=[r_colp], writes=[r_thcol])
                fw.barrier()
                ar.release(m2b)
                NW = 512
                NWT = 520
                idx = ar.alloc("idx", [128, NWT], F32); r_idx = Reg()
                I("pool", lambda E: E.iota(idx[:], pattern=[[1, NWT]], base=0, channel_multiplier=0, allow_small_or_imprecise_dtypes=True), writes=[r_idx])
                def wb(name, w=NW):
                    return ar.alloc(name, [128, w], F32), Reg()
                ph0, r_ph0 = wb("ph0", NWT)
                NCb = [wb("NC0", NWT), wb("NC1", NWT)]; NSb = [wb("NS0", NWT), wb("NS1", NWT)]
                crot = ar.alloc("crot", [128, 4], F32); r_crot = Reg()
                upcount = [0]
                ma, r_ma = wb("ma"); mb_, r_mb = wb("mb"); mc, r_mc = wb("mc", NWT); md, r_md = wb("md")
                kk_, r_kk = mc, r_mc
                aph, r_aph = mc, r_mc
                VR, r_VR = wb("VR"); VI, r_VI = wb("VI")
                GRb = [wb("GR0"), wb("GR1")]; GIb = [wb("GI0"), wb("GI1")]
                HRt = ar.alloc("HRt", [128, NW], BF16); r_HR = Reg()
                HIt = ar.alloc("HIt", [128, NW], BF16); r_HI = Reg()
                yblk = ar.alloc("yblk", [128, L], F32); r_yblk = Reg()
                tiles = [(0, 256)] + [(CTX + i * 512, 512) for i in range(8)]
                gcount = [0]
                HRb = [(HRt, r_HR), (ar.alloc("HRt1", [128, NW], BF16), Reg())]
                HIb = [(HIt, r_HI), (ar.alloc("HIt1", [128, NW], BF16), Reg())]
                its = []
                for blk in range(4):
                    for jj in range(4):
                        for d in range(2):
                            for ti in range(NT):
                                its.append((blk, jj, d, ti))
                state = {"prev": None, "NC": None, "NS": None}

                def init_y(blk):
                    I("dve", lambda E, blk=blk: E.tensor_scalar(out=yblk[:], in0=us5T[:, blk, CTX:T], scalar1=dcol[:, blk:blk + 1], scalar2=None, op0=ALU.mult),
                      reads=r_us5T + [r_dcol], writes=[r_yblk])

                def emit_bu(it):
                    blk, jj, d, ti = it
                    j0, n = tiles[ti]
                    nti = ti if d == 0 else (0 if ti == 0 else 9 - ti)
                    s0, _ = tiles[nti]
                    bcol = (d * 4 + blk) * 128
                    pbr, rpbr = bank()
                    pbi, rpbi = bank()
                    for c, (pp, rpp) in enumerate(((pbr, rpbr), (pbi, rpbi))):
                        I("pe", lambda E, pp=pp, c=c, jj=jj, bcol=bcol, blk=blk, s0=s0, n=n: E.matmul(pp[:, :n], lhsT=BbT[32 * jj:32 * jj + 32, c, bcol:bcol + 128], rhs=us5T[32 * jj:32 * jj + 32, blk, s0:s0 + n],
                                                                                                   start=True, stop=True, tile_position=(32 * jj, 0)),
                          reads=[r_BbT, r_us5T[nti]], writes=[rpp])
                    return (pbr, rpbr, pbi, rpbi)

                def stage_a(i, bu):
                    blk, jj, d, ti = its[i]
                    j0, n = tiles[ti]
                    up = (d * 4 + blk) * 4 + jj
                    pbr, rpbr, pbi, rpbi = bu
                    if ti == 0:
                        state["prev"] = None
                        upcount[0] += 1
                        (NCt, r_NC), (NSt, r_NS) = NCb[upcount[0] % 2], NSb[upcount[0] % 2]
                        state["NC"], state["NS"] = (NCt, r_NC), (NSt, r_NS)
                        I("pool", lambda E, up=up: E.tensor_scalar(out=ph0[:, :], in0=idx[:, :], scalar1=thcol[:, up:up + 1], scalar2=None, op0=ALU.mult),
                          reads=[r_idx, r_thcol], writes=[r_ph0])
                        I("pool", lambda E: E.tensor_scalar(out=kk_[:, :], in0=ph0[:, :], scalar1=1.0 / TWO_PI, scalar2=MAGIC, op0=ALU.mult, op1=ALU.add), reads=[r_ph0], writes=[r_kk])
                        I("pool", lambda E: E.tensor_scalar(out=kk_[:, :], in0=kk_[:, :], scalar1=MAGIC, scalar2=-TWO_PI, op0=ALU.subtract, op1=ALU.mult), reads=[r_kk], writes=[r_kk])
                        I("pool", lambda E: E.tensor_tensor(out=ph0[:, :], in0=ph0[:, :], in1=kk_[:, :], op=ALU.add), reads=[r_ph0, r_kk], writes=[r_ph0])
                        I("pool", lambda E: E.tensor_scalar(out=ph0[:, :], in0=ph0[:, :], scalar1=PI_LO, scalar2=-PI_LO, op0=ALU.min, op1=ALU.max), reads=[r_ph0], writes=[r_ph0])
                        I("act", lambda E, NSt=NSt: E.activation(out=NSt[:, :], in_=ph0[:, :], func=AF.Sin), reads=[r_ph0], writes=[r_NS])
                        I("act", lambda E: E.activation(out=aph[:, :], in_=ph0[:, :], func=AF.Abs), reads=[r_ph0], writes=[r_aph])
                        I("act", lambda E, NCt=NCt: E.activation(out=NCt[:, :], in_=aph[:, :], func=AF.Sin, bias=halfpi[:, 0:1], scale=-1.0), reads=[r_aph, r_halfpi], writes=[r_NC])
                    (NCt, r_NC), (NSt, r_NS) = state["NC"], state["NS"]
                    if d == 0:
                        BRv, BIv = pbr[:, :n], pbi[:, :n]
                    else:
                        BRv, BIv = pbr[:, :n][:, ::-1], pbi[:, :n][:, ::-1]
                    I("dve", lambda E, BRv=BRv, n=n, NCt=NCt: E.tensor_tensor(out=ma[:, :n], in0=BRv, in1=NCt[:, :n], op=ALU.mult), reads=[rpbr, r_NC], writes=[r_ma])
                    I("dve", lambda E, BIv=BIv, n=n, NSt=NSt: E.tensor_tensor(out=mb_[:, :n], in0=BIv, in1=NSt[:, :n], op=ALU.mult), reads=[rpbi, r_NS], writes=[r_mb])
                    I("dve", lambda E, n=n: E.tensor_tensor(out=VR[:, :n], in0=ma[:, :n], in1=mb_[:, :n], op=ALU.add), reads=[r_ma, r_mb], writes=[r_VR])
                    I("dve", lambda E, BIv=BIv, n=n, NCt=NCt: E.tensor_tensor(out=ma[:, :n], in0=BIv, in1=NCt[:, :n], op=ALU.mult), reads=[rpbi, r_NC], writes=[r_ma])
                    I("dve", lambda E, BRv=BRv, n=n, NSt=NSt: E.tensor_tensor(out=mb_[:, :n], in0=BRv, in1=NSt[:, :n], op=ALU.mult), reads=[rpbr, r_NS], writes=[r_mb])
                    I("dve", lambda E, n=n: E.tensor_tensor(out=VI[:, :n], in0=ma[:, :n], in1=mb_[:, :n], op=ALU.subtract), reads=[r_ma, r_mb], writes=[r_VI])
                    gcount[0] += 1
                    gb = gcount[0] % 2
                    (GR, r_GR), (GI, r_GI) = GRb[gb], GIb[gb]
                    prev = state["prev"]
                    if prev is None:
                        iR, iI, rds = 0.0, 0.0, []
                    else:
                        (pGR, r_pGR), (pGI, r_pGI), pn = prev
                        gRl, gIl = pGR[:, pn - 1:pn], pGI[:, pn - 1:pn]
                        cc_, ss_ = NCt[:, pn:pn + 1], NSt[:, pn:pn + 1]
                        I("dve", lambda E, gIl=gIl, ss_=ss_: E.tensor_scalar(out=crot[:, 0:1], in0=gIl, scalar1=ss_, scalar2=None, op0=ALU.mult), reads=[r_pGI, r_NS], writes=[r_crot])
                        I("dve", lambda E, gRl=gRl, cc_=cc_: E.scalar_tensor_tensor(out=crot[:, 1:2], in0=gRl, scalar=cc_, in1=crot[:, 0:1], op0=ALU.mult, op1=ALU.subtract), reads=[r_pGR, r_NC, r_crot], writes=[r_crot])
                        I("dve", lambda E, gIl=gIl, cc_=cc_: E.tensor_scalar(out=crot[:, 2:3], in0=gIl, scalar1=cc_, scalar2=None, op0=ALU.mult), reads=[r_pGI, r_NC], writes=[r_crot])
                        I("dve", lambda E, gRl=gRl, ss_=ss_: E.scalar_tensor_tensor(out=crot[:, 3:4], in0=gRl, scalar=ss_, in1=crot[:, 2:3], op0=ALU.mult, op1=ALU.add), reads=[r_pGR, r_NS, r_crot], writes=[r_crot])
                        iR, iI, rds = crot[:, 1:2], crot[:, 3:4], [r_crot]
                    I("dve", lambda E, GR=GR, n=n, up=up, iR=iR: E.tensor_tensor_scan(out=GR[:, :n], data0=rcol[:, up:up + 1].to_broadcast([128, n]), data1=VR[:, :n], initial=iR, op0=ALU.mult, op1=ALU.add),
                      reads=[r_VR, r_rcol] + rds, writes=[r_GR])
                    I("dve", lambda E, GI=GI, n=n, up=up, iI=iI: E.tensor_tensor_scan(out=GI[:, :n], data0=rcol[:, up:up + 1].to_broadcast([128, n]), data1=VI[:, :n], initial=iI, op0=ALU.mult, op1=ALU.add),
                      reads=[r_VI, r_rcol] + rds, writes=[r_GI])
                    state["prev"] = ((GR, r_GR), (GI, r_GI), n)
                    if ti == 0:
                        return None
                    (HRx, r_HRx), (HIx, r_HIx) = HRb[i % 2], HIb[i % 2]
                    I("dve", lambda E, GR=GR, n=n, NCt=NCt: E.tensor_tensor(out=mc[:, :n], in0=GR[:, :n], in1=NCt[:, :n], op=ALU.mult), reads=[r_GR, r_NC], writes=[r_mc])
                    I("dve", lambda E, GI=GI, n=n, NSt=NSt: E.tensor_tensor(out=md[:, :n], in0=GI[:, :n], in1=NSt[:, :n], op=ALU.mult), reads=[r_GI, r_NS], writes=[r_md])
                    I("dve", lambda E, n=n, HRx=HRx: E.tensor_tensor(out=HRx[:, :n], in0=mc[:, :n], in1=md[:, :n], op=ALU.subtract), reads=[r_mc, r_md], writes=[r_HRx])
                    I("dve", lambda E, GR=GR, n=n, NSt=NSt: E.tensor_tensor(out=ma[:, :n], in0=GR[:, :n], in1=NSt[:, :n], op=ALU.mult), reads=[r_GR, r_NS], writes=[r_ma])
                    I("dve", lambda E, GI=GI, n=n, NCt=NCt: E.tensor_tensor(out=mb_[:, :n], in0=GI[:, :n], in1=NCt[:, :n], op=ALU.mult), reads=[r_GI, r_NC], writes=[r_mb])
                    I("dve", lambda E, n=n, HIx=HIx: E.scalar_tensor_tensor(out=HIx[:, :n], in0=ma[:, :n], scalar=-1.0, in1=mb_[:, :n], op0=ALU.mult, op1=ALU.subtract), reads=[r_ma, r_mb], writes=[r_HIx])
                    return (HRx, r_HRx, HIx, r_HIx)

                def stage_b(i, hh_):
                    blk, jj, d, ti = its[i]
                    j0, n = tiles[ti]
                    up = (d * 4 + blk) * 4 + jj
                    if hh_ is not None:
                        HRx, r_HRx, HIx, r_HIx = hh_
                        py, rpy = bank()
                        I("pe", lambda E, py=py, up=up, n=n, HRx=HRx: E.matmul(py[:, :n], lhsT=CTb[:, 0, up, :], rhs=HRx[:, :n], start=True, stop=False), reads=[r_CTb, r_HRx], writes=[rpy], signal=False)
                        I("pe", lambda E, py=py, up=up, n=n, HIx=HIx: E.matmul(py[:, :n], lhsT=CTb[:, 1, up, :], rhs=HIx[:, :n], start=False, stop=True), reads=[r_CTb, r_HIx], writes=[rpy])
                        if d == 0:
                            yv = yblk[:, j0 - CTX:j0 - CTX + n]
                        else:
                            lo_ = L - (j0 - CTX) - n
                            yv = yblk[:, lo_:lo_ + n][:, ::-1]
                        I("dve", lambda E, py=py, yv=yv, n=n: E.tensor_tensor(out=yv, in0=yv, in1=py[:, :n], op=ALU.add), reads=[rpy, r_yblk], writes=[r_yblk])
                    if jj == 3 and d == 1 and ti == NT - 1:
                        for cc in range(8):
                            c0 = cc * 512
                            yv = yblk[:, c0:c0 + 512]
                            I("dve", lambda E, yv=yv: E.tensor_tensor(out=mc[:, 0:512], in0=yv, in1=yv, op=ALU.mult), reads=[r_yblk], writes=[r_mc])
                            I("dve", lambda E: E.tensor_scalar(out=mc[:, 0:512], in0=mc[:, 0:512], scalar1=0.044715 * 1.5957691216, scalar2=1.5957691216, op0=ALU.mult, op1=ALU.add), reads=[r_mc], writes=[r_mc])
                            I("dve", lambda E, yv=yv: E.tensor_tensor(out=mc[:, 0:512], in0=mc[:, 0:512], in1=yv, op=ALU.mult), reads=[r_mc, r_yblk], writes=[r_mc])
                            I("act", lambda E: E.activation(out=md[:], in_=mc[:, 0:512], func=AF.Sigmoid), reads=[r_mc], writes=[r_md])
                            I("dve", lambda E, yv=yv, blk=blk, c0=c0: E.tensor_tensor(out=us5T[:, blk, CTX + c0:CTX + c0 + 512], in0=md[:], in1=yv, op=ALU.mult),
                              reads=[r_md, r_yblk], writes=[r_us5T[1 + cc]])
                        if blk < 3:
                            init_y(blk + 1)

                init_y(0)
                NI = len(its)
                bus = {0: emit_bu(its[0])}
                hres = {}
                for i in range(NI):
                    if i + 1 < NI:
                        bus[i + 1] = emit_bu(its[i + 1])
                    hres[i] = stage_a(i, bus.pop(i))
                    if i >= 1:
                        stage_b(i - 1, hres.pop(i - 1))
                stage_b(NI - 1, hres.pop(NI - 1))
                for tl in range(8):
                    c0 = CTX + tl * 512
                    for nb in range(4):
                        pg, rpg = bank()
                        for k in range(4):
                            I("pe", lambda E, pg=pg, k=k, nb=nb, c0=c0: E.matmul(pg[:, :], lhsT=wglub[:, k, nb * 128:(nb + 1) * 128], rhs=us5T[:, k, c0:c0 + 512], start=(k == 0), stop=(k == 3)),
                              reads=[r_wglub, r_us5T[1 + tl]], writes=[rpg], signal=(k == 3))
                        I("act", lambda E, pg=pg, nb=nb: E.activation(out=ma[:], in_=pg[:, :], func=AF.Sigmoid, bias=bgl[:, nb:nb + 1], scale=1.0), reads=[rpg, r_bgl], writes=[r_ma])
                        I("dve", lambda E, nb=nb, tl=tl, c0=c0: E.tensor_tensor(out=s5T[:, nb, tl * 512:(tl + 1) * 512], in0=ma[:], in1=us5T[:, nb, c0:c0 + 512], op=ALU.mult),
                          reads=[r_ma, r_us5T[1 + tl]], writes=[r_s5T[tl]])
                dump("s5T", s5T[:, :, 0:512], r_s5T[0], [128, 4, 512])
                fw.barrier()
                ar.release(m2)
            ar.release(mB)
            if stop_after >= 3:
                attT = ar.alloc("attT", [128, 4, L], BF16, top=True); r_attT = [Reg() for _ in range(8)]
                m3 = ar.mark()
                g_c3 = fw.group("const3", final=True)
                qgt = ar.alloc("qgt", [128, 3], F32); r_qgt = Reg()
                kvgt = ar.alloc("kvgt", [128, 2], F32); r_kvgt = Reg()
                Dm("sp", g_c3, lambda E: E.dma_start(out=qgt[:], in_=qg_d[:, :]), writes=[r_qgt])
                Dm("sp", g_c3, lambda E: E.dma_start(out=kvgt[:], in_=kvg_d[:, :]), writes=[r_kvgt])
                eselb = ar.alloc("eselb", [32, 96], BF16); r_eselb = Reg()
                Dm("pool", g_c3, lambda E: E.dma_start(out=eselb[:], in_=esel_d[:, :]), writes=[r_eselb])
                wuqb = ar.alloc("wuqb", [128, 3, NH * 192], BF16); r_wuqb = Reg()
                wukb = ar.alloc("wukb", [128, 2, NH * 96], BF16); r_wukb = Reg()
                wuvb = ar.alloc("wuvb", [128, 2, NH * 64], BF16); r_wuvb = Reg()
                stg = ar.alloc("stg", [128, NH * 192], F32); r_stg = Reg()
                g_stg = fw.group("stg")
                wuq_v = wuq_d.rearrange("(k p) n -> p k n", p=128)
                wuk_v = wuk_d.rearrange("(k p) n -> p k n", p=128)
                wuv_v = wuv_d.rearrange("(k p) n -> p k n", p=128)
                for k in range(3):
                    for hh in range(2):
                        Dm("sp", g_stg, lambda E, k=k, hh=hh: E.dma_start(out=stg[:, hh * 768:(hh + 1) * 768], in_=wuq_v[:, k, hh * 768:(hh + 1) * 768]), writes=[r_stg])
                    I("dve", lambda E, k=k: E.tensor_scalar(out=wuqb[:, k, :], in0=stg[:, :], scalar1=qgt[:, k:k + 1], scalar2=SCALE, op0=ALU.mult, op1=ALU.mult),
                      reads=[r_stg, r_qgt], writes=[r_wuqb])
                for k in range(2):
                    Dm("sp", g_stg, lambda E, k=k: E.dma_start(out=stg[:, 0:NH * 96], in_=wuk_v[:, k, :]), writes=[r_stg])
                    I("dve", lambda E, k=k: E.tensor_scalar(out=wukb[:, k, :], in0=stg[:, 0:NH * 96], scalar1=kvgt[:, k:k + 1], scalar2=None, op0=ALU.mult),
                      reads=[r_stg, r_kvgt], writes=[r_wukb])
                for k in range(2):
                    Dm("sp", g_stg, lambda E, k=k: E.dma_start(out=stg[:, 0:NH * 64], in_=wuv_v[:, k, :]), writes=[r_stg])
                    I("dve", lambda E, k=k: E.tensor_scalar(out=wuvb[:, k, :], in0=stg[:, 0:NH * 64], scalar1=kvgt[:, k:k + 1], scalar2=None, op0=ALU.mult),
                      reads=[r_stg, r_kvgt], writes=[r_wuvb])
                cut(30)
                KhT = ar.alloc("KhT", [96, T], BF16); r_KhT = Reg()
                QhT = ar.alloc("QhT", [96, L], BF16); r_QhT = Reg()
                Vaug = [ar.alloc("Vaug%d" % i, [128, 34, 128], BF16) for i in range(2)]; r_Vaug = [Reg(), Reg()]
                I("pool", lambda E: E.memset(Vaug[0][:, :, 64:128], 1.0), writes=[r_Vaug[0]])
                I("pool", lambda E: E.memset(Vaug[1][:, :, 0:64], 1.0), writes=[r_Vaug[1]])
                PT = [ar.alloc("PT%d" % i, [128, 512], BF16) for i in range(3)]; r_PT = [Reg() for _ in range(3)]
                Oun = ar.alloc("Oun", [128, 512], F32); r_Oun = Reg()
                Dsh = ar.alloc("Dsh", [128, 512], F32); r_Dsh = Reg()
                g_dsh = fw.group("dsh")
                ropeq = ar.alloc("ropeq", [96, 2, 512], F32); r_ropeq = Reg()
                g_ropeq = fw.group("ropeq")
                tq = ar.alloc("tq", [96, 512], F32); r_tq = Reg()
                Sb = [(pbank[i], r_pb[i]) for i in (0, 1, 2)]
                Ob = [(pbank[i], r_pb[i]) for i in (3, 4)]
                Mb = [(pbank[i], r_pb[i]) for i in (5, 6)]
                mcnt = [0]
                scnt = [0]
                ocnt = [0]

                def mbank():
                    mcnt[0] += 1
                    return Mb[mcnt[0] % 2]

                for h in [int(v) for v in os.environ.get('HEADS', '0,1,2,3,4,5,6,7').split(',')]:
                    par = h % 2
                    for ti in range(NT):
                        n = 256 if ti == 0 else 512
                        t0 = 0 if ti == 0 else CTX + (ti - 1) * 512
                        pb, rpb = mbank()
                        for k in range(2):
                            I("pe", lambda E, pb=pb, k=k, h=h, n=n, t0=t0: E.matmul(pb[0:96, :n], lhsT=wukb[:, k, h * 96:(h + 1) * 96], rhs=kvcT[:, k, t0:t0 + n], start=(k == 0), stop=False),
                              reads=[r_wukb, r_kvcT[ti]], writes=[rpb], signal=False)
                        I("pe", lambda E, pb=pb, n=n, t0=t0: E.matmul(pb[0:96, :n], lhsT=eselb[0:32, :], rhs=krT[0:32, t0:t0 + n], start=False, stop=True),
                          reads=[r_eselb, r_krT[ti]], writes=[rpb])
                        I("dve", lambda E, pb=pb, n=n, t0=t0: E.tensor_copy(out=KhT[0:96, t0:t0 + n], in_=pb[0:96, :n]), reads=[rpb], writes=[r_KhT])
                    cut(31)
                    for qt in range(8):
                        l0 = qt * 512
                        pa, rpa = mbank()
                        for k in range(3):
                            I("pe", lambda E, pa=pa, k=k, h=h, l0=l0: E.matmul(pa[0:96, :], lhsT=wuqb[:, k, h * 192:h * 192 + 96], rhs=qcT[:, k, l0:l0 + 512], start=(k == 0), stop=(k == 2)),
                              reads=[r_wuqb, r_qcT[qt + 1]], writes=[rpa], signal=(k == 2))
                        pbb, rpbb = mbank()
                        for k in range(3):
                            I("pe", lambda E, pbb=pbb, k=k, h=h, l0=l0: E.matmul(pbb[0:96, :], lhsT=wuqb[:, k, h * 192 + 96:h * 192 + 192], rhs=qcT[:, k, l0:l0 + 512], start=(k == 0), stop=(k == 2)),
                              reads=[r_wuqb, r_qcT[qt + 1]], writes=[rpbb], signal=(k == 2))
                        Dm("sp", g_ropeq, lambda E, l0=l0: E.dma_start(out=ropeq[64:96, :, :], in_=rope_d[:, :, l0:l0 + 512]), writes=[r_ropeq])
                        I("dve", lambda E, pa=pa, l0=l0: E.tensor_copy(out=QhT[0:64, l0:l0 + 512], in_=pa[0:64, :]), reads=[rpa], writes=[r_QhT])
                        I("dve", lambda E, pa=pa: E.tensor_tensor(out=tq[64:96, :], in0=pa[64:96, :], in1=ropeq[64:96, 0, :], op=ALU.mult), reads=[rpa, r_ropeq], writes=[r_tq])
                        I("dve", lambda E, pbb=pbb: E.tensor_tensor(out=ropeq[64:96, 1, :], in0=pbb[64:96, :], in1=ropeq[64:96, 1, :], op=ALU.mult), reads=[rpbb, r_ropeq], writes=[r_ropeq])
                        I("dve", lambda E, l0=l0: E.tensor_tensor(out=QhT[64:96, l0:l0 + 512], in0=tq[64:96, :], in1=ropeq[64:96, 1, :], op=ALU.add), reads=[r_tq, r_ropeq], writes=[r_QhT])
                    cut(32)
                    va, r_va = Vaug[par], r_Vaug[par]
                    voff = 0 if par == 0 else 64
                    for kb in range(5):
                        nk = 8 if kb < 4 else 2
                        pb, rpb = mbank()
                        for j in range(nk):
                            kt = kb * 8 + j
                            ti = 0 if kt < 2 else 1 + (kt - 2) // 4
                            for k in range(2):
                                I("pe", lambda E, pb=pb, j=j, k=k, kt=kt, h=h: E.matmul(pb[:, j * 64:(j + 1) * 64], lhsT=kvcT[:, k, kt * 128:(kt + 1) * 128], rhs=wuvb[:, k, h * 64:(h + 1) * 64], start=(k == 0), stop=(k == 1)),
                                  reads=[r_wuvb, r_kvcT[ti]], writes=[rpb], signal=(j == nk - 1 and k == 1))
                        I("dve", lambda E, pb=pb, kb=kb, nk=nk, va=va, voff=voff: E.tensor_copy(out=va[:, kb * 8:kb * 8 + nk, voff:voff + 64], in_=pb[:, 0:nk * 64].rearrange("p (j d) -> p j d", d=64)),
                          reads=[rpb], writes=[r_va])
                    cut(33)
                    pend = []

                    def emit_pv(it, h=h, par=par, va=va, r_va=r_va):
                        qt, kt, pt, rpt, po, rpo = it
                        l0 = qt * 512
                        I("pe", lambda E, po=po, pt=pt, kt=kt, va=va: E.matmul(po[:, :], lhsT=va[:, kt, :], rhs=pt[:, :], start=(kt == 0), stop=(kt == 33)),
                          reads=[r_va, rpt], writes=[rpo], signal=(kt == 33))
                        if kt != 33:
                            return
                        I("dve", lambda E, po=po: E.tensor_copy(out=Oun[:, :], in_=po[:, :]), reads=[rpo], writes=[r_Oun])
                        if par == 0:
                            Dm("sp", g_dsh, lambda E: E.dma_start(out=Dsh[0:64, :], in_=Oun[64:128, :]), reads=[r_Oun], writes=[r_Dsh])
                            lo_, hi_ = 0, 64
                        else:
                            Dm("sp", g_dsh, lambda E: E.dma_start(out=Dsh[64:128, :], in_=Oun[0:64, :]), reads=[r_Oun], writes=[r_Dsh])
                            lo_, hi_ = 64, 128
                        I("dve", lambda E, lo_=lo_, hi_=hi_: E.reciprocal(out=Dsh[lo_:hi_, :], in_=Dsh[lo_:hi_, :]), reads=[r_Dsh], writes=[r_Dsh])
                        I("dve", lambda E, lo_=lo_, hi_=hi_, h=h, l0=l0: E.tensor_tensor(out=attT[lo_:hi_, h // 2, l0:l0 + 512], in0=Oun[lo_:hi_, :], in1=Dsh[lo_:hi_, :], op=ALU.mult),
                          reads=[r_Oun, r_Dsh], writes=[r_attT[qt]])

                    for qt in range(8):
                        l0 = qt * 512
                        ocnt[0] += 1
                        po, rpo = Ob[ocnt[0] % 2]
                        for kt in range(34):
                            scnt[0] += 1
                            ps_, rps = Sb[scnt[0] % 3]
                            pt, rpt = PT[scnt[0] % 3], r_PT[scnt[0] % 3]
                            I("pe", lambda E, ps_=ps_, kt=kt, l0=l0: E.matmul(ps_[:, :], lhsT=KhT[0:96, kt * 128:(kt + 1) * 128], rhs=QhT[0:96, l0:l0 + 512], start=True, stop=True),
                              reads=[r_KhT, r_QhT], writes=[rps])
                            I("act", lambda E, ps_=ps_, pt=pt: E.activation(out=pt[:, :], in_=ps_[:, :], func=AF.Exp), reads=[rps], writes=[rpt])
                            pend.append((qt, kt, pt, rpt, po, rpo))
                            if len(pend) > 2:
                                emit_pv(pend.pop(0))
                    while pend:
                        emit_pv(pend.pop(0))
                dump("attT", attT[:, :, 0:512], r_attT[0], [128, 4, 512])
                fw.barrier()
                ar.release(m3)
            if stop_after >= 4:
                ar.release(mA)
                m4 = ar.mark()
                g_c4 = fw.group("const4", final=True)
                woutb = ar.alloc("woutb", [128, 8, D], BF16); r_woutb = Reg()
                wout_v = wout_d.rearrange("(k p) n -> p k n", p=128)
                for k in range(8):
                    Dm("pool", g_c4, lambda E, k=k: E.dma_start(out=woutb[:, k, :], in_=wout_v[:, k, :]), writes=[r_woutb])
                lnt = ar.alloc("lnt", [128, 4, D], F32); r_lnt = Reg()
                for i in range(4):
                    Dm("sp", g_c4, lambda E, i=i: E.dma_start(out=lnt[:, i, :], in_=lnp_d[i:i + 1, :].partition_broadcast(128)), writes=[r_lnt])
                xt2 = ar.alloc("xt2", [128, D], F32); r_xt2 = Reg(); g_xt2 = fw.group("xt2")
                r1 = ar.alloc("r1", [128, D], F32); r_r1 = Reg(); g_out = fw.group("outst")
                x1t = ar.alloc("x1t", [128, 4, D], F32); r_x1t = [Reg() for _ in range(4)]
                xh2 = ar.alloc("xh2", [128, D], BF16); r_xh2 = Reg()
                u2T = ar.alloc("u2T", [128, 8, 512], BF16); r_u2T = Reg()
                wgub = [ar.alloc("wgub%d" % i, [128, 2, 8, 128], BF16) for i in range(2)]; r_wgub = [Reg(), Reg()]
                g_wgu = [fw.group("wgu0"), fw.group("wgu1")]
                tmpa = ar.alloc("tmpa", [128, 512], F32); r_tmpa = Reg()
                tmpb = ar.alloc("tmpb", [128, 512], F32); r_tmpb = Reg()
                aT = ar.alloc("aT", [128, 22, 512], BF16); r_aT = Reg()
                wdnh = ar.alloc("wdnh", [128, 22, 512], BF16); r_wdnh = Reg(); g_wdn = fw.group("wdn")
                st6b = ar.alloc("st6b", [128, 2, 6], F32); r_st6b = Reg()
                mvb = ar.alloc("mvb", [128, 2], F32); r_mvb = Reg()
                rstdb = ar.alloc("rstdb", [128, 1], F32); r_rstdb = Reg()
                nbiasb = ar.alloc("nbiasb", [128, 1], F32); r_nbiasb = Reg()
                wgu_v = wgu_d.rearrange("(k p) n -> p k n", p=128)
                wdn_v = wdn_d.rearrange("(f p) n -> p f n", p=128)

                def ln_stats(src, r_src):
                    for hh in range(2):
                        I("dve", lambda E, hh=hh, src=src: E.bn_stats(out=st6b[:, hh, :], in_=src[:, hh * 512:(hh + 1) * 512]), reads=[r_src], writes=[r_st6b])
                    I("dve", lambda E: E.bn_aggr(out=mvb[:], in_=st6b[:].rearrange("p a b -> p (a b)")), reads=[r_st6b], writes=[r_mvb])
                    I("act", lambda E: E.activation(out=rstdb[:], in_=mvb[:, 1:2], func=AF.Sqrt, bias=epsc[:, 0:1], scale=1.0), reads=[r_mvb, r_epsc], writes=[r_rstdb])
                    I("dve", lambda E: E.reciprocal(out=rstdb[:], in_=rstdb[:]), reads=[r_rstdb], writes=[r_rstdb])
                    I("dve", lambda E: E.scalar_tensor_tensor(out=nbiasb[:], in0=mvb[:, 0:1], scalar=-1.0, in1=rstdb[:], op0=ALU.mult, op1=ALU.mult),
                      reads=[r_mvb, r_rstdb], writes=[r_nbiasb])

                fcount = [0]
                for tl in range(8):
                    l0 = tl * 512
                    for s in range(4):
                        ls = l0 + s * 128
                        ybanks = []
                        for hh in range(2):
                            pb, rpb = bank()
                            for k in range(8):
                                src_t = attT if k < 4 else s5T
                                rr = r_attT[tl] if k < 4 else r_s5T[tl]
                                I("pe", lambda E, pb=pb, k=k, hh=hh, ls=ls, src_t=src_t: E.matmul(pb[:, :], lhsT=src_t[:, k % 4, ls:ls + 128], rhs=woutb[:, k, hh * 512:(hh + 1) * 512], start=(k == 0), stop=(k == 7)),
                                  reads=[rr, r_woutb], writes=[rpb], signal=(k == 7))
                            ybanks.append((pb, rpb))
                        Dm("sp", g_xt2, lambda E, ls=ls: E.dma_start(out=xt2[:], in_=x_d[ls:ls + 128, :]), writes=[r_xt2])
                        for hh in range(2):
                            pb, rpb = ybanks[hh]
                            I("dve", lambda E, pb=pb, hh=hh: E.tensor_tensor(out=r1[:, hh * 512:(hh + 1) * 512], in0=pb[:, :], in1=g12b[:, 0, hh * 512:(hh + 1) * 512], op=ALU.mult),
                              reads=[rpb, r_g12b], writes=[r_r1])
                        I("dve", lambda E: E.scalar_tensor_tensor(out=r1[:], in0=xt2[:], scalar=ALPHA, in1=r1[:], op0=ALU.mult, op1=ALU.add), reads=[r_xt2, r_r1], writes=[r_r1])
                        ln_stats(r1, r_r1)
                        I("act", lambda E, s=s: E.activation(out=x1t[:, s, :], in_=r1[:], func=AF.Identity, bias=nbiasb[:, 0:1], scale=rstdb[:, 0:1]),
                          reads=[r_r1, r_rstdb, r_nbiasb], writes=[r_x1t[s]])
                        I("dve", lambda E, s=s: E.tensor_tensor(out=x1t[:, s, :], in0=x1t[:, s, :], in1=lnt[:, 0, :], op=ALU.mult), reads=[r_x1t[s], r_lnt], writes=[r_x1t[s]])
                        I("dve", lambda E, s=s: E.tensor_tensor(out=x1t[:, s, :], in0=x1t[:, s, :], in1=lnt[:, 1, :], op=ALU.add), reads=[r_x1t[s], r_lnt], writes=[r_x1t[s]])
                        ln_stats(x1t[:, s, :], r_x1t[s])
                        I("act", lambda E, s=s: E.activation(out=xh2[:], in_=x1t[:, s, :], func=AF.Identity, bias=nbiasb[:, 0:1], scale=rstdb[:, 0:1]),
                          reads=[r_x1t[s], r_rstdb, r_nbiasb], writes=[r_xh2])
                        for k in range(8):
                            I("pe", lambda E, k=k: E.transpose(out=ptr[:, k * 128:(k + 1) * 128], in_=xh2[:, k * 128:(k + 1) * 128], identity=ident[:, :]),
                              reads=[r_xh2, r_ident], writes=[r_ptr], signal=(k == 7))
                        for k in range(8):
                            if k % 2 == 0:
                                I("act", lambda E, k=k, s=s: E.activation(out=u2T[:, k, s * 128:(s + 1) * 128], in_=ptr[:, k * 128:(k + 1) * 128], func=AF.Identity,
                                                                       bias=modcol[:, k, 2:3], scale=modcol[:, k, 3:4]), reads=[r_ptr, r_modcol], writes=[r_u2T])
                            else:
                                I("dve", lambda E, k=k, s=s: E.tensor_scalar(out=u2T[:, k, s * 128:(s + 1) * 128], in0=ptr[:, k * 128:(k + 1) * 128],
                                                                          scalar1=modcol[:, k, 3:4], scalar2=modcol[:, k, 2:3], op0=ALU.mult, op1=ALU.add),
                                  reads=[r_ptr, r_modcol], writes=[r_u2T])
                    for f in range(22):
                        fcount[0] += 1
                        wb = fcount[0] % 2
                        for gu in range(2):
                            c = gu * 22 + f
                            Dm("sp", g_wgu[wb], lambda E, wb=wb, gu=gu, c=c: E.dma_start(out=wgub[wb][:, gu, :, :].rearrange("p k n -> p (k n)"), in_=wgu_bf[c]), reads=[r_wgubf], writes=[r_wgub[wb]])
                        pg, rpg = bank()
                        pu, rpu = bank()
                        for gu, (pp, rpp) in enumerate(((pg, rpg), (pu, rpu))):
                            for k in range(8):
                                I("pe", lambda E, pp=pp, k=k, gu=gu, wb=wb: E.matmul(pp[:, :], lhsT=wgub[wb][:, gu, k, :], rhs=u2T[:, k, :], start=(k == 0), stop=(k == 7)),
                                  reads=[r_wgub[wb], r_u2T], writes=[rpp], signal=(k == 7))
                        I("act", lambda E, pg=pg: E.activation(out=tmpa[:], in_=pg[:, :], func=AF.Silu), reads=[rpg], writes=[r_tmpa])
                        I("dve", lambda E, pu=pu, f=f: E.tensor_tensor(out=aT[:, f, :], in0=tmpa[:], in1=pu[:, :], op=ALU.mult), reads=[r_tmpa, rpu], writes=[r_aT])
                    for hh in range(2):
                        Dm("sp", g_wdn, lambda E, hh=hh: E.dma_start(out=wdnh[:].rearrange("p f n -> p (f n)"), in_=wdn_bf[hh]), reads=[r_wdnbf], writes=[r_wdnh])
                        for s in range(4):
                            pb, rpb = bank()
                            for f in range(22):
                                I("pe", lambda E, pb=pb, f=f, s=s: E.matmul(pb[:, :], lhsT=aT[:, f, s * 128:(s + 1) * 128], rhs=wdnh[:, f, :], start=(f == 0), stop=(f == 21)),
                                  reads=[r_aT, r_wdnh], writes=[rpb], signal=(f == 21))
                            I("dve", lambda E, pb=pb, hh=hh: E.tensor_tensor(out=tmpb[:], in0=pb[:, :], in1=g12b[:, 1, hh * 512:(hh + 1) * 512], op=ALU.mult), reads=[rpb, r_g12b], writes=[r_tmpb])
                            I("dve", lambda E, s=s, hh=hh: E.scalar_tensor_tensor(out=x1t[:, s, hh * 512:(hh + 1) * 512], in0=x1t[:, s, hh * 512:(hh + 1) * 512], scalar=ALPHA, in1=tmpb[:],
                                                                                op0=ALU.mult, op1=ALU.add), reads=[r_x1t[s], r_tmpb], writes=[r_x1t[s]])
                    for s in range(4):
                        ls = l0 + s * 128
                        ln_stats(x1t[:, s, :], r_x1t[s])
                        I("act", lambda E, s=s: E.activation(out=r1[:], in_=x1t[:, s, :], func=AF.Identity, bias=nbiasb[:, 0:1], scale=rstdb[:, 0:1]),
                          reads=[r_x1t[s], r_rstdb, r_nbiasb], writes=[r_r1])
                        I("dve", lambda E: E.tensor_tensor(out=r1[:], in0=r1[:], in1=lnt[:, 2, :], op=ALU.mult), reads=[r_r1, r_lnt], writes=[r_r1])
                        I("dve", lambda E: E.tensor_tensor(out=r1[:], in0=r1[:], in1=lnt[:, 3, :], op=ALU.add), reads=[r_r1, r_lnt], writes=[r_r1])
                        Dm("sp", g_out, lambda E, ls=ls: E.dma_start(out=out_d[ls:ls + 128, :], in_=r1[:]), reads=[r_r1])
                fw.barrier()
        except StopBuild:
            pass
        fw.barrier()
        fw.emit()
    return nc, used_inputs


def _rope_tables():
    rows = L // 64
    row = np.repeat(np.arange(rows), 64).astype(np.float32)
    col = np.tile(np.arange(64), rows).astype(np.float32)
    n_freq = 8
    freqs = (10000.0 ** (-np.arange(n_freq, dtype=np.float32) / n_freq)).astype(np.float32)
    ang = np.concatenate([row[:, None] * freqs, col[:, None] * freqs], axis=-1).astype(np.float32)
    cos = np.cos(ang).astype(np.float32)
    sin = np.sin(ang).astype(np.float32)
    t = np.zeros((32, 2, L), np.float32)
    for j in range(32):
        t[j, 0] = cos[:, j // 2]
        t[j, 1] = sin[:, j // 2] * (-1.0 if j % 2 == 0 else 1.0)
    return t


def host_shared(inp):
    sh = {}
    f = np.float32
    sh["w_ada"] = np.ascontiguousarray(inp["w_ada"][0], f)
    sh["b_ada"] = np.ascontiguousarray(inp["b_ada"][0][None, :], f)
    w_in = inp["w_in"][0]
    kr = w_in[:, 640:672]
    kr_sw = kr.reshape(D, 16, 2)[:, :, ::-1].reshape(D, 32)
    sh["w_in_l"] = np.ascontiguousarray(np.concatenate([w_in[:, :672], kr_sw, w_in[:, 672:], np.zeros((D, 64), f)], axis=1), f)
    wuq = inp["w_uq"][0]
    rope_sw = wuq[:, :, 64:96].reshape(QL, NH, 16, 2)[:, :, :, ::-1].reshape(QL, NH, 32)
    sh["w_uq_l"] = np.ascontiguousarray(np.concatenate([wuq, wuq[:, :, :64], rope_sw], axis=2).reshape(QL, NH * 192), f)
    wuk = inp["w_uk"][0]
    sh["w_uk_l"] = np.ascontiguousarray(np.concatenate([wuk, np.zeros((KVL, NH, 32), f)], axis=2).reshape(KVL, NH * 96), f)
    sh["w_uv_l"] = np.ascontiguousarray(inp["w_uv"][0].reshape(KVL, NH * 64), f)
    sh["qg_l"] = np.ascontiguousarray(inp["q_norm_g"][0].reshape(3, 128).T, f)
    sh["kvg_l"] = np.ascontiguousarray(inp["kv_norm_g"][0].reshape(2, 128).T, f)
    sh["rope_cs"] = _rope_tables()
    es_ = np.zeros((32, 96), f)
    es_[np.arange(32), 64 + np.arange(32)] = 1.0
    sh["esel"] = es_
    sh["ident"] = np.eye(128, dtype=f)
    lre, lim, ldt = inp["s5_lambda_re"][0], inp["s5_lambda_im"][0], inp["s5_log_dt"][0]
    bre, bim = inp["s5_b_re"][0], inp["s5_b_im"][0]
    cre, cim = inp["s5_c_re"][0], inp["s5_c_im"][0]
    col = np.zeros((128, 3, 32), f)
    rowl = np.zeros((128, 3, 1024), f)
    bl = np.zeros((128, 2, 1024), f)
    cl = np.zeros((128, 2, 32, 128), f)
    for d in range(2):
        for blk in range(4):
            for jj in range(4):
                up = (d * 4 + blk) * 4 + jj
                for gl in range(2):
                    g = blk * 8 + 2 * jj + gl
                    col[gl * 64:(gl + 1) * 64, 0, up] = lre[d, g]
                    col[gl * 64:(gl + 1) * 64, 1, up] = lim[d, g]
                    col[gl * 64:(gl + 1) * 64, 2, up] = ldt[d, g]
                    c0 = (d * 4 + blk) * 128 + gl * 64
                    rowl[32 * jj:32 * jj + 32, 0, c0:c0 + 64] = lre[d, g][None, :]
                    rowl[32 * jj:32 * jj + 32, 1, c0:c0 + 64] = lim[d, g][None, :]
                    rowl[32 * jj:32 * jj + 32, 2, c0:c0 + 64] = ldt[d, g]
                    p0 = 32 * jj + 16 * gl
                    bl[p0:p0 + 16, 0, c0:c0 + 64] = bre[d, g].T
                    bl[p0:p0 + 16, 1, c0:c0 + 64] = bim[d, g].T
                    m0 = 32 * jj + 16 * gl
                    cl[gl * 64:(gl + 1) * 64, 0, up, m0:m0 + 16] = cre[d, g].T
                    cl[gl * 64:(gl + 1) * 64, 1, up, m0:m0 + 16] = cim[d, g].T
    sh["s5_col"], sh["s5_row"], sh["s5_b_l"], sh["s5_c_l"] = col, rowl, bl, cl
    sh["s5_d_col"] = np.ascontiguousarray(inp["s5_d"][0].reshape(4, 128).T, f)
    sh["s5_w_glu"] = np.ascontiguousarray(inp["s5_w_glu"][0], f)
    sh["b_glu_col"] = np.ascontiguousarray(inp["s5_b_glu"][0].reshape(4, 128).T, f)
    sh["w_out"] = np.ascontiguousarray(inp["w_out"][0], f)
    sh["ln_params"] = np.ascontiguousarray(np.stack([inp["ln1_g"][0], inp["ln1_b"][0], inp["ln2_g"][0], inp["ln2_b"][0]]), f)
    sh["w_gate_up"] = np.ascontiguousarray(inp["w_gate_up"][0], f)
    sh["w_down"] = np.ascontiguousarray(inp["w_down"][0], f)
    return sh


def host_core(inp, b, sh):
    m = dict(sh)
    m["x"] = np.ascontiguousarray(inp["x"][b], np.float32)
    m["ctx"] = np.ascontiguousarray(inp["ctx"][b], np.float32)
    cc = np.stack([inp["c"][b], inp["c_ctx"]], axis=-1).astype(np.float32)
    m["cc"] = np.ascontiguousarray(cc.reshape(8, 128, 2).transpose(1, 0, 2).reshape(128, 16))
    return m


_NC = {}


def kernel(**inputs):
    inp = {k: np.asarray(v) for k, v in inputs.items()}
    if "nc" not in _NC:
        _NC["nc"], _NC["names"] = build()
    nc, names = _NC["nc"], set(_NC["names"])
    sh = host_shared(inp)
    in_maps = [{k: v for k, v in host_core(inp, b, sh).items() if k in names} for b in range(8)]
    res = run_bass_kernel_spmd(nc, in_maps, core_ids=list(range(8)))
    return np.stack([np.asarray(r["out"], np.float32) for r in res.results], axis=0)
```
